# Optimizing a Trainium2 kernel written in Bass

```python
import math
import jax, jax.numpy as jnp
from jax import lax
import numpy as np

D_MODEL = 1024
BATCH = 32
SEQ = 256
DEPTH = 2
DEC_BATCH = 2
DEC_SEQ = 4096
PAST_LEN = 256

GRID_W = 64
ROPE_BASE = 10000.0
EPS = 1e-6
BLK = 128
WINDOW = 128
NEG_INF = -1e30

N_AB = (DEPTH + 1) // 2
N_C = DEPTH // 2

A_HEADS = 8
A_KV_HEADS = 2
A_GROUP = A_HEADS // A_KV_HEADS
A_HEAD_DIM = 64
A_WIDTH = A_HEADS * A_HEAD_DIM
A_KV_WIDTH = A_KV_HEADS * A_HEAD_DIM
A_SCALE = A_HEAD_DIM ** -0.5

SSD_HEADS = 16
SSD_HEAD_DIM = 64
SSD_INNER = SSD_HEADS * SSD_HEAD_DIM
SSD_GROUPS = 2
SSD_STATE = 64
CONV_K = 5
CONV_CH = SSD_INNER + 2 * SSD_GROUPS * SSD_STATE
CHUNK = 128

AB_IN = 2 * A_WIDTH + 2 * A_KV_WIDTH + SSD_INNER + CONV_CH + 2 * SSD_HEADS
AB_OUT_IN = A_WIDTH + SSD_INNER

MLA_HEADS = 16
MLA_NOPE = 64
MLA_ROPE = 32
MLA_V = 64
Q_LORA = 256
KV_LORA = 128
MLA_WIDTH = MLA_HEADS * MLA_V
MLA_IN = Q_LORA + KV_LORA + MLA_ROPE + MLA_WIDTH
MLA_SCALE = (MLA_NOPE + MLA_ROPE) ** -0.5

kernel_name = "hybrid_swa_ssd_mla_prefix_dit_step"


def rms_norm(x, w):
    xf = x.astype(jnp.float32)
    y = xf * lax.rsqrt(jnp.mean(xf * xf, axis=-1, keepdims=True) + EPS)
    return (y * w.astype(jnp.float32)).astype(x.dtype)


def split_cols(t, sizes):
    idx = [int(s) for s in np.cumsum(sizes)[:-1]]
    return jnp.split(t, idx, axis=-1)


def modulation(cond, w_ada, b_ada):
    m = jax.nn.silu(cond) @ w_ada + b_ada
    shift, scale, gate = jnp.split(m[:, None, :], 3, axis=-1)
    return shift, scale, gate


def axial_rope_tables(length, dim, dtype):
    rows = length // GRID_W
    row = jnp.repeat(jnp.arange(rows), GRID_W).astype(jnp.float32)
    col = jnp.tile(jnp.arange(GRID_W), rows).astype(jnp.float32)
    nf = dim // 4
    inv = 1.0 / (ROPE_BASE ** (jnp.arange(nf, dtype=jnp.float32) / nf))
    ar = row[:, None] * inv[None, :]
    ac = col[:, None] * inv[None, :]
    ang = jnp.concatenate([ar, ar, ac, ac], axis=-1)
    return jnp.cos(ang).astype(dtype), jnp.sin(ang).astype(dtype)


def apply_rope(x, cos, sin):
    shape = (cos.shape[0],) + (1,) * (x.ndim - 3) + (cos.shape[1],)
    half = x.shape[-1] // 2
    nf = half // 2

    def rot(u):
        return jnp.concatenate([-u[..., nf:], u[..., :nf]], axis=-1)

    xr = jnp.concatenate([rot(x[..., :half]), rot(x[..., half:])], axis=-1)
    return x * cos.reshape(shape) + xr * sin.reshape(shape)


def dense_attention(q, k, v, scale, sink=None):
    b, lq, kvh, g, dk = q.shape
    nb = lq // BLK
    qb = jnp.moveaxis(q.reshape(b, nb, BLK, kvh, g, dk), 1, 0)

    def one_block(qblk):
        s = jnp.einsum("bqhgd,bkhd->bhgqk", qblk, k).astype(jnp.float32) * scale
        if sink is not None:
            s_sink = jnp.broadcast_to(sink.astype(jnp.float32)[None, :, :, None, None], s.shape[:-1] + (1,))
            p = jax.nn.softmax(jnp.concatenate([s, s_sink], axis=-1), axis=-1)[..., :-1]
        else:
            p = jax.nn.softmax(s, axis=-1)
        return jnp.einsum("bhgqk,bkhd->bqhgd", p.astype(v.dtype), v)

    out = lax.map(one_block, qb)
    return jnp.moveaxis(out, 0, 1).reshape(b, lq, kvh, g, v.shape[-1])


def banded_attention(q, k, v, k_ctx, v_ctx, sink, scale):
    b, L, kvh, g, d = q.shape
    nb = L // BLK
    qb = q.reshape(b, nb, BLK, kvh, g, d)

    def windows(t):
        tb = t.reshape(b, nb, BLK, kvh, d)
        tp = jnp.pad(tb, ((0, 0), (1, 1), (0, 0), (0, 0), (0, 0)))
        return jnp.concatenate([tp[:, :-2], tp[:, 1:-1], tp[:, 2:]], axis=2)

    kw = windows(k)
    vw = windows(v)
    qi = jnp.arange(BLK)
    kj = jnp.arange(3 * BLK) - BLK
    rel = kj[None, :] - qi[:, None]
    kpos = jnp.arange(nb)[:, None] * BLK + kj[None, :]
    valid = (jnp.abs(rel) <= WINDOW)[None] & ((kpos >= 0) & (kpos < L))[:, None, :]
    s_loc = jnp.einsum("bnqhgd,bnkhd->bnhgqk", qb, kw).astype(jnp.float32) * scale
    s_loc = jnp.where(valid[None, :, None, None], s_loc, NEG_INF)
    s_ctx = jnp.einsum("bnqhgd,bchd->bnhgqc", qb, k_ctx).astype(jnp.float32) * scale
    s_sink = jnp.broadcast_to(sink.astype(jnp.float32)[None, None, :, :, None, None], s_loc.shape[:-1] + (1,))
    p = jax.nn.softmax(jnp.concatenate([s_loc, s_ctx, s_sink], axis=-1), axis=-1).astype(v.dtype)
    nk = 3 * BLK
    nc = k_ctx.shape[1]
    out = (jnp.einsum("bnhgqk,bnkhd->bnqhgd", p[..., :nk], vw)
           + jnp.einsum("bnhgqc,bchd->bnqhgd", p[..., nk:nk + nc], v_ctx))
    return out.reshape(b, L, kvh, g, d)


def centred_conv(u, w, bias):
    ch = u.shape[-1]
    pad = CONV_K // 2
    out = lax.conv_general_dilated(u, w[:, None, :], window_strides=(1,), padding=[(pad, pad)],
                                   dimension_numbers=("NWC", "WIO", "NWC"), feature_group_count=ch)
    return out + bias


def ssd_scan(x, dt, a_head, bm, cm, h0):
    f32 = jnp.float32
    b, L, H, P = x.shape
    G, N = bm.shape[2], bm.shape[3]
    nc = L // CHUNK
    rep = H // G
    bh = jnp.repeat(bm, rep, axis=2).astype(f32).reshape(b, nc, CHUNK, H, N)
    ch = jnp.repeat(cm, rep, axis=2).astype(f32).reshape(b, nc, CHUNK, H, N)
    xd = (x.astype(f32) * dt[..., None]).reshape(b, nc, CHUNK, H, P)
    la = (dt * a_head.astype(f32)).reshape(b, nc, CHUNK, H)
    cum = jnp.cumsum(la, axis=2)
    causal = jnp.tril(jnp.ones((CHUNK, CHUNK), dtype=bool))[:, :, None]
    diff = cum[:, :, :, None, :] - cum[:, :, None, :, :]
    decay = jnp.exp(jnp.where(causal, diff, -jnp.inf))
    scores = jnp.einsum("bclhn,bcshn->bclsh", ch, bh) * decay
    y_diag = jnp.einsum("bclsh,bcshp->bclhp", scores, xd)
    w_state = jnp.exp(cum[:, :, -1:, :] - cum)
    states = jnp.einsum("bcshn,bcsh,bcshp->bchpn", bh, w_state, xd)
    chunk_decay = jnp.exp(cum[:, :, -1, :])

    def step(hc, inp):
        st, dec = inp
        return hc * dec[:, :, None, None] + st, hc

    h_final, h_in = lax.scan(step, h0.astype(f32), (jnp.moveaxis(states, 1, 0), jnp.moveaxis(chunk_decay, 1, 0)))
    h_in = jnp.moveaxis(h_in, 0, 1)
    y_off = jnp.einsum("bclhn,bchpn,bclh->bclhp", ch, h_in, jnp.exp(cum))
    y = (y_diag + y_off).reshape(b, L, H, P)
    return y.astype(x.dtype), h_final.astype(x.dtype)


def ssd_bidir(xs, bm, cm, dt, dt_bias, a_log, d_skip, h0_f, h0_b):
    dtp = jax.nn.softplus((dt + dt_bias).astype(jnp.float32))
    a = -jnp.exp(a_log.astype(jnp.float32))
    y_f, s_f = ssd_scan(xs, dtp[:, :, 0], a[0], bm, cm, h0_f)
    flip = lambda t: jnp.flip(t, axis=1)
    y_b, s_b = ssd_scan(flip(xs), flip(dtp[:, :, 1]), a[1], flip(bm), flip(cm), h0_b)
    y = y_f + flip(y_b) + d_skip[:, None] * xs
    return y, s_f, s_b


def ab_inputs(h, w_in, conv_w, conv_b):
    b, L, _ = h.shape
    q, k, v, g, z, xbc, dt = split_cols(h @ w_in, (A_WIDTH, A_KV_WIDTH, A_KV_WIDTH, A_WIDTH, SSD_INNER, CONV_CH, 2 * SSD_HEADS))
    q = q.reshape(b, L, A_KV_HEADS, A_GROUP, A_HEAD_DIM)
    k = k.reshape(b, L, A_KV_HEADS, A_HEAD_DIM)
    v = v.reshape(b, L, A_KV_HEADS, A_HEAD_DIM)
    xbc = jax.nn.silu(centred_conv(xbc, conv_w, conv_b))
    xs, bm, cm = split_cols(xbc, (SSD_INNER, SSD_GROUPS * SSD_STATE, SSD_GROUPS * SSD_STATE))
    xs = xs.reshape(b, L, SSD_HEADS, SSD_HEAD_DIM)
    bm = bm.reshape(b, L, SSD_GROUPS, SSD_STATE)
    cm = cm.reshape(b, L, SSD_GROUPS, SSD_STATE)
    dt = dt.reshape(b, L, 2, SSD_HEADS)
    return q, k, v, g, z, xs, bm, cm, dt


def ab_output(attn, g, y, z, gnorm_w, w_out):
    b, L = g.shape[0], g.shape[1]
    a_out = attn.reshape(b, L, A_WIDTH) * jax.nn.silu(g)
    s_out = rms_norm(y.reshape(b, L, SSD_INNER) * jax.nn.silu(z), gnorm_w)
    return jnp.concatenate([a_out, s_out], axis=-1) @ w_out


def ab_context(h, w_in, sink, conv_w, conv_b, dt_bias, a_log, d_skip, gnorm_w, w_out):
    q, k, v, g, z, xs, bm, cm, dt = ab_inputs(h, w_in, conv_w, conv_b)
    attn = dense_attention(q, k, v, A_SCALE, sink.reshape(A_KV_HEADS, A_GROUP))
    h0 = jnp.zeros((h.shape[0], SSD_HEADS, SSD_HEAD_DIM, SSD_STATE), h.dtype)
    y, s_f, s_b = ssd_bidir(xs, bm, cm, dt, dt_bias, a_log, d_skip, h0, h0)
    return ab_output(attn, g, y, z, gnorm_w, w_out), k, v, s_f, s_b


def ab_latent(h, k_ctx, v_ctx, s_f0, s_b0, w_in, sink, conv_w, conv_b, dt_bias, a_log, d_skip, gnorm_w, w_out):
    q, k, v, g, z, xs, bm, cm, dt = ab_inputs(h, w_in, conv_w, conv_b)
    cos, sin = axial_rope_tables(h.shape[1], A_HEAD_DIM, h.dtype)
    attn = banded_attention(apply_rope(q, cos, sin), apply_rope(k, cos, sin), v, k_ctx, v_ctx,
                            sink.reshape(A_KV_HEADS, A_GROUP), A_SCALE)
    y, _, _ = ssd_bidir(xs, bm, cm, dt, dt_bias, a_log, d_skip, s_f0, s_b0)
    return ab_output(attn, g, y, z, gnorm_w, w_out)


def mla_inputs(h, w_in, q_norm_w, kv_norm_w, w_uq):
    b, L, _ = h.shape
    cq, ckv, kpe, g = split_cols(h @ w_in, (Q_LORA, KV_LORA, MLA_ROPE, MLA_WIDTH))
    q = (rms_norm(cq, q_norm_w) @ w_uq).reshape(b, L, MLA_HEADS, MLA_NOPE + MLA_ROPE)
    return q, rms_norm(ckv, kv_norm_w), kpe, g


def mla_expand_kv(ckv, kpe, w_ukv):
    b, L, _ = ckv.shape
    kv = (ckv @ w_ukv).reshape(b, L, MLA_HEADS, MLA_NOPE + MLA_V)
    k_nope, v = kv[..., :MLA_NOPE], kv[..., MLA_NOPE:]
    k = jnp.concatenate([k_nope, jnp.broadcast_to(kpe[:, :, None, :], (b, L, MLA_HEADS, MLA_ROPE))], axis=-1)
    return k, v


def mla_attend(q, k, v, g, w_out):
    b, L = q.shape[0], q.shape[1]
    out = dense_attention(q[:, :, :, None, :], k, v, MLA_SCALE)
    return (out.reshape(b, L, MLA_WIDTH) * jax.nn.silu(g)) @ w_out


def mla_context(h, w_in, q_norm_w, kv_norm_w, w_uq, w_ukv, w_out):
    q, ckv, kpe, g = mla_inputs(h, w_in, q_norm_w, kv_norm_w, w_uq)
    k, v = mla_expand_kv(ckv, kpe, w_ukv)
    return mla_attend(q, k, v, g, w_out), ckv, kpe


def mla_latent(h, ckv_ctx, kpe_ctx, w_in, q_norm_w, kv_norm_w, w_uq, w_ukv, w_out):
    q, ckv, kpe, g = mla_inputs(h, w_in, q_norm_w, kv_norm_w, w_uq)
    cos, sin = axial_rope_tables(h.shape[1], MLA_ROPE, h.dtype)
    q = jnp.concatenate([q[..., :MLA_NOPE], apply_rope(q[..., MLA_NOPE:], cos, sin)], axis=-1)
    k_lat, v_lat = mla_expand_kv(ckv, apply_rope(kpe, cos, sin), w_ukv)
    k_ctx, v_ctx = mla_expand_kv(ckv_ctx, kpe_ctx, w_ukv)
    k = jnp.concatenate([k_lat, k_ctx], axis=1)
    v = jnp.concatenate([v_lat, v_ctx], axis=1)
    return mla_attend(q, k, v, g, w_out)


def setup_inputs(seed: int = 0) -> dict:
    key = jax.random.key(seed)
    ks = jax.random.split(key, 29)
    f32 = jnp.float32

    def nrm(k, shape, s=1.0):
        return s * jax.random.normal(k, shape, f32)

    dt0 = jnp.exp(jax.random.uniform(ks[17], (N_AB, 2, SSD_HEADS), f32, math.log(1e-3), math.log(1e-1)))
    return {
        "x_prompt": nrm(ks[0], (BATCH, SEQ, D_MODEL)),
        "x_sample": nrm(ks[1], (DEC_BATCH, DEC_SEQ, D_MODEL)),
        "cache_a_k": nrm(ks[2], (DEC_BATCH, N_AB, PAST_LEN, A_KV_HEADS, A_HEAD_DIM)),
        "cache_a_v": nrm(ks[3], (DEC_BATCH, N_AB, PAST_LEN, A_KV_HEADS, A_HEAD_DIM)),
        "state_ssd_fwd": nrm(ks[4], (DEC_BATCH, N_AB, SSD_HEADS, SSD_HEAD_DIM, SSD_STATE), 0.5),
        "state_ssd_bwd": nrm(ks[5], (DEC_BATCH, N_AB, SSD_HEADS, SSD_HEAD_DIM, SSD_STATE), 0.5),
        "cache_mla_ckv": nrm(ks[6], (DEC_BATCH, N_C, PAST_LEN, KV_LORA)),
        "cache_mla_kpe": nrm(ks[7], (DEC_BATCH, N_C, PAST_LEN, MLA_ROPE)),
        "c": nrm(ks[8], (DEC_BATCH, D_MODEL)),
        "c_ctx": nrm(ks[9], (D_MODEL,)),
        "ada_w": nrm(ks[10], (DEPTH, D_MODEL, 3 * D_MODEL), D_MODEL ** -0.5),
        "ada_b": nrm(ks[11], (DEPTH, 3 * D_MODEL), 0.02),
        "norm_w": 1.0 + nrm(ks[12], (DEPTH, D_MODEL), 0.02),
        "ab_w_in": nrm(ks[13], (N_AB, D_MODEL, AB_IN), D_MODEL ** -0.5),
        "ab_sink": nrm(ks[14], (N_AB, A_HEADS), 0.5),
        "ab_conv_w": nrm(ks[15], (N_AB, CONV_K, CONV_CH), CONV_K ** -0.5),
        "ab_conv_b": nrm(ks[16], (N_AB, CONV_CH), 0.02),
        "ab_dt_bias": dt0 + jnp.log(-jnp.expm1(-dt0)),
        "ab_a_log": jnp.log(jax.random.uniform(ks[18], (N_AB, 2, SSD_HEADS), f32, 1.0, 16.0)),
        "ab_d_skip": 1.0 + nrm(ks[19], (N_AB, SSD_HEADS), 0.1),
        "ab_gnorm_w": 1.0 + nrm(ks[20], (N_AB, SSD_INNER), 0.02),
        "ab_w_out": nrm(ks[21], (N_AB, AB_OUT_IN, D_MODEL), AB_OUT_IN ** -0.5),
        "mla_w_in": nrm(ks[22], (N_C, D_MODEL, MLA_IN), D_MODEL ** -0.5),
        "mla_q_norm_w": 1.0 + nrm(ks[23], (N_C, Q_LORA), 0.02),
        "mla_kv_norm_w": 1.0 + nrm(ks[24], (N_C, KV_LORA), 0.02),
        "mla_w_uq": nrm(ks[25], (N_C, Q_LORA, MLA_HEADS * (MLA_NOPE + MLA_ROPE)), Q_LORA ** -0.5),
        "mla_w_ukv": nrm(ks[26], (N_C, KV_LORA, MLA_HEADS * (MLA_NOPE + MLA_V)), KV_LORA ** -0.5),
        "mla_w_out": nrm(ks[27], (N_C, MLA_WIDTH, D_MODEL), MLA_WIDTH ** -0.5),
        "final_norm_w": 1.0 + nrm(ks[28], (D_MODEL,), 0.02),
    }


def reference(x_prompt, x_sample, cache_a_k, cache_a_v, state_ssd_fwd, state_ssd_bwd, cache_mla_ckv, cache_mla_kpe,
              c, c_ctx, ada_w, ada_b, norm_w, ab_w_in, ab_sink, ab_conv_w, ab_conv_b, ab_dt_bias, ab_a_log,
              ab_d_skip, ab_gnorm_w, ab_w_out, mla_w_in, mla_q_norm_w, mla_kv_norm_w, mla_w_uq, mla_w_ukv,
              mla_w_out, final_norm_w):
    xp = x_prompt
    ks_a, vs_a, sf_list, sb_list, ckv_list, kpe_list = [], [], [], [], [], []
    for layer in range(DEPTH):
        i = layer // 2
        shift, scale, gate = modulation(c_ctx[None, :], ada_w[layer], ada_b[layer])
        h = rms_norm(xp, norm_w[layer]) * (1.0 + scale) + shift
        if layer % 2 == 0:
            out, k, v, s_f, s_b = ab_context(h, ab_w_in[i], ab_sink[i], ab_conv_w[i], ab_conv_b[i], ab_dt_bias[i],
                                             ab_a_log[i], ab_d_skip[i], ab_gnorm_w[i], ab_w_out[i])
            ks_a.append(k)
            vs_a.append(v)
            sf_list.append(s_f)
            sb_list.append(s_b)
        else:
            out, ckv, kpe = mla_context(h, mla_w_in[i], mla_q_norm_w[i], mla_kv_norm_w[i], mla_w_uq[i],
                                        mla_w_ukv[i], mla_w_out[i])
            ckv_list.append(ckv)
            kpe_list.append(kpe)
        xp = xp + gate * out
    y_prompt = rms_norm(xp, final_norm_w)
    new_cache_a_k = jnp.stack(ks_a, axis=1)
    new_cache_a_v = jnp.stack(vs_a, axis=1)
    new_state_ssd_fwd = jnp.stack(sf_list, axis=1)
    new_state_ssd_bwd = jnp.stack(sb_list, axis=1)
    new_cache_mla_ckv = jnp.stack(ckv_list, axis=1)
    new_cache_mla_kpe = jnp.stack(kpe_list, axis=1)

    xl = x_sample
    for layer in range(DEPTH):
        i = layer // 2
        shift, scale, gate = modulation(c, ada_w[layer], ada_b[layer])
        h = rms_norm(xl, norm_w[layer]) * (1.0 + scale) + shift
        if layer % 2 == 0:
            out = ab_latent(h, cache_a_k[:, i], cache_a_v[:, i], state_ssd_fwd[:, i], state_ssd_bwd[:, i],
                            ab_w_in[i], ab_sink[i], ab_conv_w[i], ab_conv_b[i], ab_dt_bias[i], ab_a_log[i],
                            ab_d_skip[i], ab_gnorm_w[i], ab_w_out[i])
        else:
            out = mla_latent(h, cache_mla_ckv[:, i], cache_mla_kpe[:, i], mla_w_in[i], mla_q_norm_w[i],
                             mla_kv_norm_w[i], mla_w_uq[i], mla_w_ukv[i], mla_w_out[i])
        xl = xl + gate * out
    y_sample = rms_norm(xl, final_norm_w)

    return (y_prompt, y_sample, new_cache_a_k, new_cache_a_v, new_state_ssd_fwd, new_state_ssd_bwd,
            new_cache_mla_ckv, new_cache_mla_kpe)
```

```python
import math
import numpy as np
from contextlib import ExitStack
import concourse.bass as bass
import concourse.mybir as mybir
from concourse.bass_utils import run_bass_kernel_spmd

F32 = mybir.dt.float32
BF16 = mybir.dt.bfloat16
AF = mybir.ActivationFunctionType
ALU = mybir.AluOpType

N_DMA_SEMS = 24
import os
USE_CACHE = os.environ.get('KF_CACHE', '1') == '1'
EPS = 1e-6
Q0, K0, V0, G0, Z0, X0, DT0 = 0, 512, 640, 768, 1280, 2304, 3584
A_SCALE = 64 ** -0.5
MLA_SCALE = 96 ** -0.5


class Buf:
    __slots__ = ("name", "t", "lw", "rd", "rd_dma", "root", "excl")

    def __init__(self, name, t=None, root=None):
        self.excl = False
        self.name = name
        self.t = t
        self.lw = None
        self.rd = {}
        self.rd_dma = []
        self.root = root if root is not None else self

    def view(self, ap, name="view"):
        return Buf(name, ap, self.root)

    def __getitem__(self, k):
        return self.t[k]


class Op:
    __slots__ = ("eng", "fn", "deps", "signal", "is_dma", "dma_id", "cnt", "phase")

    def __init__(self, eng, fn, is_dma=False):
        self.eng = eng
        self.fn = fn
        self.deps = []
        self.signal = False
        self.is_dma = is_dma
        self.dma_id = None
        self.cnt = None
        self.phase = 0


class Prog:
    ENGS = ("tensor", "vector", "scalar", "gpsimd", "sync")

    def __init__(self, nc, stack):
        self.nc = nc
        self.stack = stack
        self.ops = {e: [] for e in self.ENGS}
        self.n_dma = 0
        self.nbuf = 0
        self.phase = 0
        self.cnt = {e: 0 for e in self.ENGS}
        self.seen = {e: {} for e in self.ENGS}
        self.seen_dma = {e: set() for e in self.ENGS}
        self.sems = {e: stack.enter_context(nc.semaphore(f"s_{e}")) for e in self.ENGS}
        self.dsems = [stack.enter_context(nc.semaphore(f"d_{i}")) for i in range(N_DMA_SEMS)]
        self.phase_dmas = []
        self.capture = None

    def sb(self, shape, dtype=F32, name="sb", stack=None):
        self.nbuf += 1
        t = (stack or self.stack).enter_context(self.nc.sbuf_tensor(f"{name}_{self.nbuf}", list(shape), dtype))
        return Buf(name, t)

    def ps(self, shape, dtype=F32, name="ps", stack=None):
        self.nbuf += 1
        t = (stack or self.stack).enter_context(self.nc.psum_tensor(f"{name}_{self.nbuf}", list(shape), dtype))
        b = Buf(name, t)
        b.excl = True
        return b

    def token(self, name="tok"):
        return Buf(name, None)

    def start_capture(self):
        self.capture = []

    def stop_capture(self):
        c = self.capture
        self.capture = None
        return c

    def merge(self, a, b, lead_b=1.0):
        na, nb = len(a), len(b)
        i = j = 0
        while i < na or j < nb:
            if j >= nb or (i < na and i * nb * lead_b <= j * na):
                o = a[i]; i += 1
            else:
                o = b[j]; j += 1
            self.ops[o.eng].append(o)

    def _add(self, op, reads, writes):
        op.phase = self.phase
        if self.capture is not None:
            self.capture.append(op)
        else:
            self.ops[op.eng].append(op)
        reads = list({id(r.root): r.root for r in reads}.values())
        writes = list({id(w.root): w.root for w in writes}.values())
        for r in reads:
            if r.excl and not any(r is w for w in writes):
                writes.append(r)
        deps = []
        for r in reads:
            if r.lw is not None:
                deps.append(r.lw)
        for w in writes:
            if w.lw is not None:
                deps.append(w.lw)
            deps.extend(w.rd.values())
            deps.extend(w.rd_dma)
        op.deps = [d for d in deps if d is not op and d.phase == self.phase
                   and not (op.eng == "tensor" and d.eng == "tensor" and not d.is_dma and not op.is_dma)]
        for r in reads:
            if any(r is w for w in writes):
                continue
            if op.is_dma:
                r.rd_dma.append(op)
            else:
                r.rd[op.eng] = op
        for w in writes:
            w.lw = op
            w.rd = {}
            w.rd_dma = []
        return op

    def op(self, eng, fn, reads=(), writes=()):
        return self._add(Op(eng, fn), list(reads), list(writes))

    def dma(self, out_ap, in_ap, reads=(), writes=(), eng="sync", **kw):
        def fn(e, out_ap=out_ap, in_ap=in_ap, kw=kw):
            return e.dma_start(out=out_ap, in_=in_ap, **kw)
        o = Op(eng, fn, is_dma=True)
        o.dma_id = self.n_dma
        self.n_dma += 1
        self.phase_dmas.append(o)
        return self._add(o, list(reads), list(writes))

    def _dma_wait(self, eng, did):
        eng.wait_ge(self.dsems[did % N_DMA_SEMS], 16 * (did // N_DMA_SEMS + 1))

    def end_phase(self):
        nc = self.nc
        ops = self.ops
        last = {}
        for e in self.ENGS:
            real = [o for o in ops[e] if not o.is_dma]
            if real:
                last[e] = real[-1]
                real[-1].signal = True
            for o in ops[e]:
                for d in o.deps:
                    if not d.is_dma:
                        d.signal = True
        for e in self.ENGS:
            for o in ops[e]:
                if o.signal and not o.is_dma:
                    self.cnt[e] += 1
                    o.cnt = self.cnt[e]
        n_dma = self.n_dma
        tail = list(range(max(0, n_dma - N_DMA_SEMS), n_dma))

        def run(ename, eng):
            seen = self.seen[ename]
            seen_dma = self.seen_dma[ename]
            for o in ops[ename]:
                if o.is_dma and o.dma_id >= N_DMA_SEMS:
                    prev = o.dma_id - N_DMA_SEMS
                    if prev not in seen_dma:
                        self._dma_wait(eng, prev)
                        seen_dma.add(prev)
                for d in o.deps:
                    if d.is_dma:
                        if d.dma_id not in seen_dma:
                            self._dma_wait(eng, d.dma_id)
                            seen_dma.add(d.dma_id)
                    elif seen.get(d.eng, 0) < d.cnt:
                        eng.wait_ge(self.sems[d.eng], d.cnt)
                        seen[d.eng] = d.cnt
                inst = o.fn(eng)
                if o.is_dma:
                    inst.then_inc(self.dsems[o.dma_id % N_DMA_SEMS], 16)
                elif o.signal:
                    inst.then_inc(self.sems[ename], 1)
            for e2, l in last.items():
                if e2 != ename and seen.get(e2, 0) < l.cnt:
                    eng.wait_ge(self.sems[e2], l.cnt)
                    seen[e2] = l.cnt
            for did in tail:
                if did not in seen_dma:
                    self._dma_wait(eng, did)
                    seen_dma.add(did)

        with nc.Block() as block:
            @block.sync
            def _(eng):
                run("sync", eng)

            @block.tensor
            def _(eng):
                run("tensor", eng)

            @block.vector
            def _(eng):
                run("vector", eng)

            @block.scalar
            def _(eng):
                run("scalar", eng)

            @block.gpsimd
            def _(eng):
                run("gpsimd", eng)
        self.ops = {e: [] for e in self.ENGS}
        self.phase_dmas = []
        self.phase += 1


def bc(ap, shape):
    return ap.to_broadcast(list(shape))


def build_program(stop_after=99):
    nc = bass.Bass("TRN2", target_bir_lowering=False)

    def din(name, shape, dt=F32):
        return nc.dram_tensor(name, list(shape), dt, kind="ExternalInput").ap()

    def dout(name, shape, dt=F32):
        return nc.dram_tensor(name, list(shape), dt, kind="ExternalOutput").ap()

    def dscr(name, shape, dt=F32):
        return nc.dram_tensor(name, list(shape), dt, kind="Internal").ap()

    xp = din("xp", [4, 256, 1024]); xs = din("xs", [1280, 1024])
    cak = din("cak", [256, 128]); cav = din("cav", [256, 128])
    h0f = din("h0f", [16, 64, 64]); h0b = din("h0b", [16, 64, 64])
    cckv = din("cckv", [256, 128]); ckpe = din("ckpe", [256, 32])
    condT = din("condT", [128, 8, 2]); sel = din("sel", [1, 4])
    fl = din("fl", [1, 2]); mlt = din("mlt", [1, 4]); mgt = din("mgt", [1, 4])
    ada_w = din("ada_w", [2, 1024, 768]); adabT = din("adabT", [2, 128, 6])
    cond3 = din("cond3", [128, 8, 3]); oh2 = din("oh2", [1, 2])
    mod_src = dscr("mod_src", [128, 64]); mod_dst = dscr("mod_dst", [512, 64]); mod_tok = Buf("mod_tok"); mod_tok2 = Buf("mod_tok2")
    normwT = din("normwT", [2, 128, 8])
    w_in0 = din("w_in0", [1024, 3616]); w_out0 = din("w_out0", [1536, 1024])
    sink = din("sink", [1, 8]); cwT = din("cwT", [128, 10, 5]); cbT = din("cbT", [128, 10])
    dtbias = din("dtbias", [1, 32]); alog = din("alog", [1, 32]); dskip = din("dskip", [1, 16])
    gnormw = din("gnormw", [1, 1024])
    w_in1 = din("w_in1", [1024, 1440]); qnormw = din("qnormw", [1, 256]); kvnormw = din("kvnormw", [1, 128])
    w_uq = din("w_uq", [256, 1536]); w_ukv = din("w_ukv", [128, 2048]); w_out1 = din("w_out1", [1024, 1024])
    fnormw = din("fnormw", [1, 1024])
    c_ident = din("c_ident", [128, 128]); c_le = din("c_le", [128, 128]); c_ge = din("c_ge", [128, 128])
    c_gt = din("c_gt", [128, 128]); c_lt = din("c_lt", [128, 128]); c_pma = din("c_pma", [128, 128])
    c_cosA = din("c_cosA", [128, 1280]); c_sinA = din("c_sinA", [128, 1280])
    c_cosK = din("c_cosK", [1024, 32]); c_sinK = din("c_sinK", [1024, 32])
    c_pmm = din("c_pmm", [96, 96]); c_cosQ = din("c_cosQ", [96, 1024]); c_sinQ = din("c_sinQ", [96, 1024])
    c_bsel = din("c_bsel", [32, 96])

    yp = dout("yp", [4, 256, 1024]); ys = dout("ys", [1024, 1024])
    nk = dout("nk", [4, 256, 128]); nv = dout("nv", [4, 256, 128])
    nsf = dout("nsf", [4, 16, 64, 64]); nsb = dout("nsb", [4, 16, 64, 64])
    nckv = dout("nckv", [4, 256, 128]); nkpe = dout("nkpe", [4, 256, 32])

    hT_d = dscr("hT_d", [128, 8, 4100], BF16)
    Hb_d = dscr("Hb_d", [32, 128, 512])
    xl1_d = dscr("xl1_d", [1024, 1024]); xc1_d = dscr("xc1_d", [4, 256, 1024])
    kv_d = dscr("kv_d", [160, 1024], BF16); ckvT_d = kv_d[0:128, :]; kpeT_d = kv_d[128:160, :]
    kv_g = dscr("kv_g", [640, 1024], BF16)
    ex_src = dscr("ex_src", [128, 1056]); ex_dst = dscr("ex_dst", [512, 1056])
    ex_tok = Buf("ex_tok"); ex_tok2 = Buf("ex_tok2"); kvg_tok = Buf("kvg_tok")
    hTd_tok = Buf("hTd_tok"); Hbd_tok = Buf("Hbd_tok"); xl1_tok = Buf("xl1_tok"); xc1_tok = Buf("xc1_tok")
    kvT_tok = Buf("kvT_tok")

    with ExitStack() as st:
        P = Prog(nc, st)
        rr = [0]

        def V(fn, r, w): return P.op("vector", fn, r, w)
        def A(fn, r, w): return P.op("scalar", fn, r, w)
        def G(fn, r, w): return P.op("gpsimd", fn, r, w)
        def T(fn, r, w): return P.op("tensor", fn, r, w)

        def VG(fn, r, w):
            rr[0] += 1
            return P.op("vector" if rr[0] % 2 else "gpsimd", fn, r, w)

        def mm(ps_buf, out_ap, lhsT, rhs, start, stop, reads):
            return T(lambda e: e.matmul(out_ap, lhsT=lhsT, rhs=rhs, start=start, stop=stop), reads, [ps_buf])

        wI = P.sb([128, 28928], BF16, "wI")
        wO = P.sb([128, 12288], BF16, "wO")
        wKV1 = P.sb([128, 8, 160], BF16, "wKV1")
        NSTG = 6
        stg = [None] * NSTG
        ident = P.sb([128, 128], F32, "ident"); idb = P.sb([128, 128], BF16, "idb")
        LE = P.sb([128, 128], F32, "LE"); GE = P.sb([128, 128], F32, "GE")
        GT = P.sb([128, 128], F32, "GT"); LT = P.sb([128, 128], F32, "LT")
        ones = P.sb([128, 128], F32, "ones"); onesb = P.sb([128, 128], BF16, "onesb")
        pma = P.sb([128, 128], F32, "pma")
        mvec = P.sb([128, 2, 24, 2], F32, "mvec")
        modA = P.sb([128, 2, 2, 8], F32, "modA")
        modB = P.sb([128, 2, 2, 8], F32, "modB")
        nwT = P.sb([128, 2, 8], F32, "nwT"); abT = P.sb([128, 2, 6], F32, "abT")
        gateb = [P.sb([128, 1024], F32, f"gateb{c}") for c in range(2)]
        gnormb = P.sb([128, 1024], F32, "gnormb")
        kvnormb = P.sb([128, 128], F32, "kvnormb")
        dskipb = P.sb([128, 16], F32, "dskipb"); dtbiasb = P.sb([128, 32], F32, "dtbiasb")
        arow = P.sb([128, 32], F32, "arow"); esink = P.sb([128, 8], F32, "esink")
        cw = P.sb([128, 10, 5], F32, "cw"); cb = P.sb([128, 10], F32, "cb")
        selb = P.sb([128, 4], F32, "selb")
        flb = P.sb([128, 2], F32, "flb"); mltb = P.sb([128, 4], F32, "mltb"); mgtb = P.sb([128, 4], F32, "mgtb")
        ccdummy = P.sb([128, 1], F32, "ccdummy")
        ccsem = st.enter_context(nc.semaphore("ccsem"))
        cc_count = [0]

        def all_gather4(src_ap, dst_ap, rtok, wtok, groups=((0, 1, 2, 3), (4, 5, 6, 7))):
            def fn(e):
                e.collective_compute("AllGather", op=ALU.bypass, replica_groups=[list(g) for g in groups],
                                     ins=[src_ap.opt()], outs=[dst_ap.opt()]).then_inc(ccsem)
                cc_count[0] += 1
                e.wait_ge(ccsem, cc_count[0])
                return e.memset(ccdummy[:], 0.0)
            P.op("gpsimd", fn, [rtok], [wtok, ccdummy])
        sc = P.sb([128, 8, 2], F32, "sc")
        PS = [P.ps([128, 512], F32, f"bank{i}") for i in range(8)]

        def load_const(buf, src, bcast=False):
            P.dma(buf[:], src.partition_broadcast(128) if bcast else src, writes=[buf])

        load_const(ident, c_ident); load_const(LE, c_le); load_const(GE, c_ge); load_const(GT, c_gt)
        load_const(LT, c_lt); load_const(pma, c_pma)
        V(lambda e: e.tensor_copy(out=idb[:], in_=ident[:]), [ident], [idb])
        V(lambda e: e.memset(ones[:], 1.0), [], [ones])
        V(lambda e: e.memset(onesb[:], 1.0), [], [onesb])
        load_const(nwT, normwT.rearrange("l p k -> p l k")); load_const(abT, adabT.rearrange("l p k -> p l k"))
        load_const(gnormb, gnormw, True)
        load_const(kvnormb, kvnormw, True)
        load_const(dskipb, dskip, True); load_const(dtbiasb, dtbias, True)
        load_const(arow, alog, True); load_const(esink, sink, True)
        load_const(cw, cwT); load_const(cb, cbT); load_const(selb, sel, True)
        load_const(flb, fl, True); load_const(mltb, mlt, True); load_const(mgtb, mgt, True)
        load_const(sc, condT)
        A(lambda e: e.activation(out=arow[:], in_=arow[:], func=AF.Exp), [arow], [arow])
        V(lambda e: e.tensor_scalar(out=arow[:], in0=arow[:], scalar1=-1.0, scalar2=None, op0=ALU.mult), [arow], [arow])
        A(lambda e: e.activation(out=esink[:], in_=esink[:], func=AF.Exp), [esink], [esink])
        A(lambda e: e.activation(out=sc[:], in_=sc[:], func=AF.Silu), [sc], [sc])

        def load_w(dst_buf, dst_view, src, kc, n):
            i = 0
            for k in range(kc):
                for c0 in range(0, n, 1024):
                    w = min(1024, n - c0)
                    s = stg[i % NSTG]
                    P.dma(s[:, 0:w], src[k * 128:(k + 1) * 128, c0:c0 + w], writes=[s])
                    eng = ("vector", "gpsimd", "scalar")[i % 3]
                    if eng == "scalar":
                        A(lambda e, s=s, k=k, c0=c0, w=w: e.copy(out=dst_view[:, k, c0:c0 + w], in_=s[:, 0:w]), [s], [dst_buf])
                    else:
                        P.op(eng, lambda e, s=s, k=k, c0=c0, w=w: e.tensor_copy(out=dst_view[:, k, c0:c0 + w], in_=s[:, 0:w]), [s], [dst_buf])
                    i += 1

        wI0 = wI[:, 0:28928].rearrange("p (k n) -> p k n", k=8)
        wO0 = wO[:, 0:12288].rearrange("p (k n) -> p k n", k=12)
        w1I = wI[:, 0:11520].rearrange("p (k n) -> p k n", k=8)
        wUQ = wI[:, 11520:14592].rearrange("p (k n) -> p k n", k=2)
        wUKV = wI[:, 14592:16640]
        wUKV3 = wI[:, 14592:16640].rearrange("p (k n) -> p k n", k=1)
        wK = wO[:, 8192:9728].rearrange("p (h c) -> p h c", h=16)
        wVp = wO[:, 9728:10752].rearrange("p (j c) -> p j c", j=8)
        wO1 = wO[:, 0:8192].rearrange("p (k n) -> p k n", k=8)

        with ExitStack() as ph:
            for i_ in range(NSTG):
                stg[i_] = P.sb([128, 1024], F32, f"stg{i_}", ph)
            aw = [P.sb([128, 8, 768], F32, f"aw{i}", ph) for i in range(2)]
            sc3 = P.sb([128, 8, 3], F32, "sc3", ph); mloc = P.sb([128, 2, 6, 3], F32, "mloc", ph)
            mall = P.sb([128, 4, 64], F32, "mall", ph); mpk = P.sb([128, 64], F32, "mpk", ph); ohb = P.sb([128, 2], F32, "ohb", ph); mt = P.sb([128, 4, 6], F32, "mt", ph)
            P.dma(sc3[:], cond3, writes=[sc3])
            P.dma(ohb[:], oh2.partition_broadcast(128), writes=[ohb])
            A(lambda e: e.activation(out=sc3[:], in_=sc3[:], func=AF.Silu), [sc3], [sc3])
            for l in range(2):
                a = aw[l]
                P.dma(a[:], ada_w[l].rearrange("(k p) n -> p k n", p=128), writes=[a])
                for j in range(6):
                    pb = PS[j % 2]
                    for k in range(8):
                        mm(pb, pb[:, 0:3], a[:, k, j * 128:(j + 1) * 128], sc3[:, k, :], k == 0, k == 7, [a, sc3])
                    A(lambda e, pb=pb, l=l, j=j: e.activation(out=mloc[:, l, j, :], in_=pb[:, 0:3], func=AF.Identity,
                                                              bias=abT[:, l, j:j + 1], scale=1.0), [pb, abT], [mloc])
            V(lambda e: e.memset(mpk[:], 0.0), [], [mpk])
            V(lambda e: e.tensor_copy(out=mpk[:, 0:36], in_=mloc[:].rearrange("p l j c -> p (l j c)")), [mloc], [mpk])
            P.dma(mod_src, mpk[:], reads=[mpk, mod_tok], writes=[mod_tok])
            all_gather4(mod_src, mod_dst, mod_tok, mod_tok2)
            P.dma(mall[:], mod_dst.rearrange("(r p) c -> p r c", p=128), reads=[mod_tok2], writes=[mall])
            m5 = mall[:, :, 0:36].rearrange("p r (l j c) -> p r l j c", l=2, j=6)
            for l in range(2):
                mv = mvec[:, l, :, :].rearrange("p (r j) c -> p r j c", j=6)
                V(lambda e, l=l, mv=mv: e.tensor_copy(out=mv[:, :, :, 0], in_=m5[:, :, l, :, 0]), [mall], [mvec])
                V(lambda e, l=l: e.tensor_scalar(out=mt[:], in0=m5[:, :, l, :, 1], scalar1=ohb[:, 0:1], scalar2=None, op0=ALU.mult), [mall, ohb], [mt])
                V(lambda e, l=l, mv=mv: e.scalar_tensor_tensor(out=mv[:, :, :, 1], in0=m5[:, :, l, :, 2], scalar=ohb[:, 1:2], in1=mt[:],
                                                               op0=ALU.mult, op1=ALU.add), [mall, ohb, mt], [mvec])
                for c in range(2):
                    V(lambda e, l=l, c=c: e.scalar_tensor_tensor(out=modA[:, l, c, :], in0=mvec[:, l, 8:16, c], scalar=1.0,
                                                                 in1=nwT[:, l, :], op0=ALU.add, op1=ALU.mult), [mvec, nwT], [modA])
                    V(lambda e, l=l, c=c: e.tensor_copy(out=modB[:, l, c, :], in_=mvec[:, l, 0:8, c]), [mvec], [modB])
            load_w(wI, wI0, w_in0, 8, 3616)
            load_w(wO, wO0, w_out0, 12, 1024)
            load_w(wKV1, wKV1[:], w_in1[:, 256:416], 8, 160)
            P.end_phase()

        def compute_gates(l, ph):
            dg = [P.sb([128, 128], F32, f"dg{i}", ph) for i in range(2)]
            for c in range(2):
                for hf in range(2):
                    pb = PS[2 * c + hf]
                    for k4 in range(4):
                        k = hf * 4 + k4
                        d = dg[k % 2]
                        V(lambda e, d=d, k=k, c=c: e.tensor_scalar(out=d[:], in0=ident[:], scalar1=mvec[:, l, 16 + k, c:c + 1], scalar2=None, op0=ALU.mult),
                          [ident, mvec], [d])
                        mm(pb, pb[:, k4 * 128:(k4 + 1) * 128], ones[:], d[:], True, True, [ones, d])
                    V(lambda e, pb=pb, c=c, hf=hf: e.tensor_copy(out=gateb[c][:, hf * 512:(hf + 1) * 512], in_=pb[:, 0:512]), [pb], [gateb[c]])

        def rms_stats(x_ap, n, xbuf, sq, ss, rstd):
            A(lambda e: e.activation(out=sq[:, 0:n], in_=x_ap, func=AF.Square, accum_out=ss[:]), [xbuf], [sq, ss])
            A(lambda e: e.activation(out=rstd[:], in_=ss[:], func=AF.Ln, scale=1.0 / n, bias=EPS), [ss], [rstd])
            A(lambda e: e.activation(out=rstd[:], in_=rstd[:], func=AF.Exp, scale=-0.5), [rstd], [rstd])

        def norm_to_hT(xr, l, c, hT, W, pt=None):
            rms_stats(xr[:], 1024, xr, W["sq"], W["ss"], W["rstd"])
            V(lambda e: e.tensor_scalar(out=W["xn"][:], in0=xr[:], scalar1=W["rstd"][:, 0:1], scalar2=None, op0=ALU.mult),
              [xr, W["rstd"]], [W["xn"]])
            pt = pt if pt is not None else PS[7]
            ptv = pt[:].bitcast(BF16)
            for k in range(8):
                T(lambda e, k=k: e.transpose(out=ptv[:, k * 128:(k + 1) * 128], in_=W["xn"][:, k * 128:(k + 1) * 128], identity=idb[:]),
                  [W["xn"], idb], [pt])
            V(lambda e: e.tensor_tensor(out=W["ht"][:], in0=ptv.rearrange("p (k t) -> p k t", k=8),
                                        in1=bc(modA[:, l, c, :].unsqueeze(2), [128, 8, 128]), op=ALU.mult), [pt, modA], [W["ht"]])
            V(lambda e: e.tensor_tensor(out=hT[:], in0=W["ht"][:], in1=bc(modB[:, l, c, :].unsqueeze(2), [128, 8, 128]), op=ALU.add),
              [W["ht"], modB], [hT])

        def transpose_state_out(Hst, dst, W):
            o = W["stout"]
            for g in range(2):
                for q4 in range(2):
                    pb = PS[(g * 2 + q4) % 4]
                    for i in range(4):
                        hl = q4 * 4 + i
                        T(lambda e, g=g, hl=hl, i=i, pb=pb: e.transpose(out=pb[0:64, i * 64:(i + 1) * 64], in_=Hst[64 * g:64 * g + 64, hl * 64:(hl + 1) * 64],
                                                                    identity=ident[64 * g:64 * g + 64, 64 * g:64 * g + 64]), [Hst, ident], [pb])
                    V(lambda e, g=g, q4=q4, pb=pb: e.tensor_copy(out=o[:, g * 8 + q4 * 4:g * 8 + q4 * 4 + 4, :],
                                                                in_=pb[0:64, 0:256].rearrange("p (h n) -> p h n", h=4)), [pb], [o])
            P.dma(dst.rearrange("h p n -> p h n"), o[:], reads=[o])

        def ab_sequence(kind, si, ph):
            lat = kind == "lat"
            cond = 1 if lat else 0
            L = 1280 if lat else 256
            nch = L // 128
            own = list(range(1, 9)) if lat else list(range(nch))

            def ro(c):
                return c - 1 if lat else c
            x_src = xs if lat else xp[si]
            Wk = {}
            G = V
            VG = V
            res_d = xl1_d if lat else xc1_d[si]
            fr_tok = xl1_tok if lat else xc1_tok

            def fr_x(c): return res_d[ro(c) * 128:(ro(c) + 1) * 128, :].bitcast(BF16)[:, 0:1024]
            def fr_b(c): return res_d[ro(c) * 128:(ro(c) + 1) * 128, :].bitcast(BF16)[:, 1024:1152]
            def fr_bc(c): return res_d[ro(c) * 128:(ro(c) + 1) * 128, :].bitcast(BF16)[:, 1152:1408].rearrange("p (a b) -> p a b", a=2)
            def fr_dt(c): return res_d[ro(c) * 128:(ro(c) + 1) * 128, 704:768].rearrange("p (a b) -> p a b", a=2)

            def sbw(name, shape, dt=F32):
                Wk[name] = P.sb(shape, dt, name, ph)
                return Wk[name]
            R = sbw("R", [128, 16, 128])
            Rf = R[:].rearrange("p h l -> p (h l)")
            Wk["sq"] = R.view(Rf[:, 0:1024]); Wk["ht"] = R.view(Rf[:, 1024:2048].rearrange("p (k t) -> p k t", k=8))
            sbw("ss", [128, 1]); sbw("rstd", [128, 1]); sbw("xn", [128, 1024], BF16)
            sbw("stout", [64, 16, 64])
            xr = sbw("xr", [128, 1024]); hT = sbw("hT", [128, 8, 128], BF16)
            nkb = nch + (2 if lat else 0)
            kT = sbw("kT", [128, nkb * 128], BF16); Vt = sbw("Vt", [128, nkb, 128], BF16)
            zt = sbw("zt", [128, 8, 4], BF16)
            uT = sbw("uT", [128, 128]); cs = sbw("cs", [128, 2, 128]); t1 = sbw("t1", [128, 128]); t2 = sbw("t2", [128, 128])
            kvo = sbw("kvo", [128, 2, 128])

            V(lambda e: e.memset(zt[:], 0.0), [], [zt])
            P.dma(hT_d[:, :, 0:2], zt[:, :, 0:2], reads=[zt, hTd_tok], writes=[hTd_tok])
            P.dma(hT_d[:, :, L + 2:L + 4], zt[:, :, 2:4], reads=[zt, hTd_tok], writes=[hTd_tok])
            if lat:
                cf = sbw("cf", [128, 2, 2, 128])
                P.dma(cf[:, 0, :, :], cak.rearrange("(b p) c -> p b c", p=128), writes=[cf])
                P.dma(cf[:, 1, :, :], cav.rearrange("(b p) c -> p b c", p=128), writes=[cf])
                cfb = sbw("cfb", [128, 2, 128], BF16)
                V(lambda e: e.tensor_copy(out=cfb[:], in_=cf[:, 0, :, :]), [cf], [cfb])
                V(lambda e: e.tensor_copy(out=Vt[:, nch:nch + 2, :], in_=cf[:, 1, :, :]), [cf], [Vt])
                pt = PS[6]; ptv = pt[:].bitcast(BF16)
                for b in range(2):
                    T(lambda e, b=b: e.transpose(out=ptv[:, b * 128:(b + 1) * 128], in_=cfb[:, b, :], identity=idb[:]), [cfb, idb], [pt])
                V(lambda e: e.tensor_copy(out=kT[:, L:L + 256], in_=ptv[:, 0:256]), [pt], [kT])
            for c in range(nch):
                P.dma(xr[:], x_src[c * 128:(c + 1) * 128, :], writes=[xr])
                norm_to_hT(xr, 0, cond, hT, Wk)
                if lat and c in (0, nch - 1):
                    fi = 0 if c == 0 else 1
                    V(lambda e, fi=fi: e.tensor_scalar(out=hT[:], in0=hT[:], scalar1=flb[:, fi:fi + 1], scalar2=None, op0=ALU.mult), [hT, flb], [hT])
                P.dma(hT_d[:, :, 2 + c * 128:2 + (c + 1) * 128], hT[:], reads=[hT, hTd_tok], writes=[hTd_tok])
                pk = PS[0]; pv = PS[1]
                for k in range(8):
                    mm(pk, pk[:, 0:128], wI0[:, k, K0:K0 + 128], hT[:, k, :], k == 0, k == 7, [wI, hT])
                for k in range(8):
                    mm(pv, pv[:, 0:128], hT[:, k, :], wI0[:, k, V0:V0 + 128], k == 0, k == 7, [wI, hT])
                if lat:
                    P.dma(cs[:, 0, :], c_cosA[:, c * 128:(c + 1) * 128], writes=[cs])
                    P.dma(cs[:, 1, :], c_sinA[:, c * 128:(c + 1) * 128], writes=[cs])
                    A(lambda e: e.copy(out=uT[:], in_=pk[:, 0:128]), [pk], [uT])
                    pr = PS[2]
                    mm(pr, pr[:, 0:128], pma[:], uT[:], True, True, [pma, uT])
                    V(lambda e: e.tensor_tensor(out=t1[:], in0=uT[:], in1=cs[:, 0, :], op=ALU.mult), [uT, cs], [t1])
                    V(lambda e: e.tensor_tensor(out=t2[:], in0=pr[:, 0:128], in1=cs[:, 1, :], op=ALU.mult), [pr, cs], [t2])
                    G(lambda e, c=c: e.tensor_tensor(out=kT[:, c * 128:(c + 1) * 128], in0=t1[:], in1=t2[:], op=ALU.add), [t1, t2], [kT])
                    V(lambda e, c=c: e.tensor_copy(out=Vt[:, c, :], in_=pv[:, 0:128]), [pv], [Vt])
                else:
                    A(lambda e, c=c: e.copy(out=kT[:, c * 128:(c + 1) * 128], in_=pk[:, 0:128]), [pk], [kT])
                    V(lambda e, c=c: e.tensor_copy(out=Vt[:, c, :], in_=pv[:, 0:128]), [pv], [Vt])
                    V(lambda e: e.tensor_copy(out=kvo[:, 1, :], in_=pv[:, 0:128]), [pv], [kvo])
                    pk2 = PS[2]
                    for k in range(8):
                        mm(pk2, pk2[:, 0:128], hT[:, k, :], wI0[:, k, K0:K0 + 128], k == 0, k == 7, [wI, hT])
                    A(lambda e: e.copy(out=kvo[:, 0, :], in_=pk2[:, 0:128]), [pk2], [kvo])
                    P.dma(nk[si, c * 128:(c + 1) * 128, :], kvo[:, 0, :], reads=[kvo])
                    P.dma(nv[si, c * 128:(c + 1) * 128, :], kvo[:, 1, :], reads=[kvo])

            Wn = sbw("Wn", [128, 8, 132], BF16)
            raw = sbw("rawb", [128, 10, 132], BF16)
            xbcT = sbw("xbcT", [128, 10, 128], BF16)
            xtok = sbw("xtok", [128, 1024], BF16); Btok = sbw("Btok", [128, 128], BF16)
            dtr = sbw("dtr", [128, 32]); dtl = sbw("dtl", [128, 2, 32])
            dtp = dtl.view(dtl[:, 0, :]); la = dtl.view(dtl[:, 1, :])
            tot = sbw("tot", [128, 16]); tmc = sbw("tmc", [128, 16]); wst = sbw("wst", [128, 16])
            dec = sbw("dec", [128, 16]); ecum = sbw("ecum", [128, 16]); dw = sbw("dw", [128, 16])
            xd = sbw("xd", [128, 1024], BF16); xdw = sbw("xdw", [128, 1024], BF16)
            Hst = [sbw(f"Hst{d}", [128, 512]) for d in range(2)]

            def front(c, nxc):
                P.dma(Wn[:], hT_d[:, :, c * 128:c * 128 + 132], reads=[hTd_tok], writes=[Wn])
                for j in range(nxc):
                    pb = PS[j % 4]
                    col = X0 + (j if nxc == 10 else j) * 128
                    for k in range(8):
                        mm(pb, pb[:, 0:132], wI0[:, k, col:col + 128], Wn[:, k, :], k == 0, k == 7, [wI, Wn])
                    if j % 2:
                        A(lambda e, j=j, pb=pb: e.copy(out=raw[:, j, :], in_=pb[:, 0:132]), [pb], [raw])
                    else:
                        V(lambda e, j=j, pb=pb: e.tensor_copy(out=raw[:, j, :], in_=pb[:, 0:132]), [pb], [raw])
                for j in range(nxc):
                    pb = PS[j % 4]
                    for kk in range(5):
                        mm(pb, pb[:, 0:128], wdiag[:, j, kk, :], raw[:, j, kk:kk + 128], kk == 0, kk == 4, [wdiag, raw])
                    A(lambda e, j=j, pb=pb: e.activation(out=xbcT[:, j, :], in_=pb[:, 0:128], func=AF.Silu, bias=cb[:, j:j + 1], scale=1.0), [pb, cb], [xbcT])
                pd = PS[4]
                for k in range(8):
                    mm(pd, pd[:, 0:32], Wn[:, k, 2:130], wI0[:, k, DT0:DT0 + 32], k == 0, k == 7, [wI, Wn])
                V(lambda e: e.tensor_tensor(out=dtr[:], in0=pd[:, 0:32], in1=dtbiasb[:], op=ALU.add), [pd, dtbiasb], [dtr])
                A(lambda e: e.activation(out=dtr[:], in_=dtr[:], func=AF.Exp), [dtr], [dtr])
                A(lambda e: e.activation(out=dtp[:], in_=dtr[:], func=AF.Ln, bias=1.0, scale=1.0), [dtr], [dtp])
                V(lambda e: e.tensor_tensor(out=la[:], in0=dtp[:], in1=arow[:], op=ALU.mult), [dtp, arow], [la])
                pt = PS[5]; ptv = pt[:].bitcast(BF16)
                for j in range(8):
                    T(lambda e, j=j: e.transpose(out=ptv[:, j * 128:(j + 1) * 128], in_=xbcT[:, j, :], identity=idb[:]), [xbcT, idb], [pt])
                V(lambda e: e.tensor_copy(out=xtok[:], in_=ptv), [pt], [xtok])
                pt2 = PS[6]; ptv2 = pt2[:].bitcast(BF16)
                T(lambda e: e.transpose(out=ptv2[:, 0:128], in_=xbcT[:, 8, :], identity=idb[:]), [xbcT, idb], [pt2])
                A(lambda e: e.copy(out=Btok[:], in_=ptv2[:, 0:128]), [pt2], [Btok])

            def dir_stats(d):
                pb = PS[6]
                ld = la[:, 16 * d:16 * d + 16]
                mm(pb, pb[:, 0:16], (LE if d == 0 else GE)[:], ld, True, True, [LE, GE, la])
                mm(pb, pb[:, 16:32], ones[:], ld, True, True, [ones, la])
                A(lambda e: e.activation(out=ecum[:], in_=pb[:, 0:16], func=AF.Exp), [pb], [ecum])
                V(lambda e: e.tensor_copy(out=tot[:], in_=pb[:, 16:32]), [pb], [tot])
                V(lambda e: e.tensor_tensor(out=tmc[:], in0=tot[:], in1=pb[:, 0:16], op=ALU.subtract), [tot, pb], [tmc])
                A(lambda e: e.activation(out=wst[:], in_=tmc[:], func=AF.Exp), [tmc], [wst])
                A(lambda e: e.activation(out=dec[:], in_=tot[:], func=AF.Exp), [tot], [dec])
                V(lambda e: e.tensor_tensor(out=dw[:], in0=dtp[:, 16 * d:16 * d + 16], in1=wst[:], op=ALU.mult), [dtp, wst], [dw])
                xv = xtok[:].rearrange("p (h q) -> p h q", h=16)
                G(lambda e: e.tensor_tensor(out=xdw[:].rearrange("p (h q) -> p h q", h=16), in0=xv, in1=bc(dw[:].unsqueeze(2), [128, 16, 64]), op=ALU.mult),
                  [xtok, dw], [xdw])

            def state_update(d, psS):
                H = Hst[d]
                for g in range(2):
                    pb = psS[g]
                    mm(pb, pb[:, 0:512], Btok[:], xdw[:, g * 512:(g + 1) * 512], True, True, [Btok, xdw])
                for g in range(2):
                    pb = psS[g]
                    r = slice(64 * g, 64 * g + 64)
                    G(lambda e, r=r, g=g: e.tensor_tensor(out=H[r, :].rearrange("p (h q) -> p h q", h=8), in0=H[r, :].rearrange("p (h q) -> p h q", h=8),
                                                          in1=bc(dec[r, 8 * g:8 * g + 8].unsqueeze(2), [64, 8, 64]), op=ALU.mult), [H, dec], [H])
                    V(lambda e, r=r, pb=pb: e.tensor_tensor(out=H[r, :], in0=H[r, :], in1=pb[r, 0:512], op=ALU.add), [H, pb], [H])

            def load_h0(src, H):
                o = Wk["stout"]
                P.dma(o[:], src.rearrange("h p n -> p h n"), writes=[o])
                V(lambda e: e.memset(R[:], 0.0), [], [R])
                for g in range(2):
                    V(lambda e, g=g: e.tensor_copy(out=R[0:64, 8 * g:8 * g + 8, 64 * g:64 * g + 64], in_=o[:, 8 * g:8 * g + 8, :]), [o], [R])
                for g in range(2):
                    for q4 in range(2):
                        pb = PS[(g * 2 + q4) % 4]
                        for i in range(4):
                            h = g * 8 + q4 * 4 + i
                            mm(pb, pb[:, i * 64:(i + 1) * 64], R[0:64, h, :], ident[0:64, 0:64], True, True, [R, ident])
                        V(lambda e, g=g, q4=q4, pb=pb: e.tensor_copy(out=H[64 * g:64 * g + 64, q4 * 256:(q4 + 1) * 256],
                                                                    in_=pb[64 * g:64 * g + 64, 0:256]), [pb], [H])

            V(lambda e: e.memset(Hst[1][:], 0.0), [], [Hst[1]])
            if lat:
                cdbT = sbw("cdbT", [128, 8, 16]); runb = sbw("runb", [128, 16]); runf = sbw("runf", [128, 16])
                dq = sbw("dq", [128, 16]); coef = sbw("coef", [128, 16]); HinB = sbw("HinB", [128, 512])
                V(lambda e: e.memset(runb[:], 1.0), [], [runb])
                V(lambda e: e.memset(runf[:], 1.0), [], [runf])
            for c in reversed(own):
                P.dma(Hb_d[ro(c)], Hst[1][:], reads=[Hst[1], Hbd_tok], writes=[Hbd_tok])
                if lat:
                    V(lambda e, c=c: e.tensor_copy(out=cdbT[:, ro(c), :], in_=runb[:]), [runb], [cdbT])
                front(c, 10 if USE_CACHE else 9)
                if USE_CACHE:
                  P.dma(fr_x(c), xtok[:], reads=[xtok, fr_tok], writes=[fr_tok])
                  P.dma(fr_b(c), Btok[:], reads=[Btok, fr_tok], writes=[fr_tok])
                  P.dma(fr_bc(c), xbcT[:, 8:10, :], reads=[xbcT, fr_tok], writes=[fr_tok])
                  P.dma(fr_dt(c), dtl[:], reads=[dtl, fr_tok], writes=[fr_tok])
                dir_stats(1)
                state_update(1, [PS[0], PS[1]])
                if lat:
                    V(lambda e: e.tensor_tensor(out=runb[:], in0=runb[:], in1=dec[:], op=ALU.mult), [runb, dec], [runb])
            if not lat:
                transpose_state_out(Hst[1], nsb[si], Wk)
            else:
                V(lambda e: e.memset(Hst[0][:], 0.0), [], [Hst[0]])
                for c in own:
                    P.dma(xtok[:], fr_x(c), reads=[fr_tok], writes=[xtok])
                    P.dma(Btok[:], fr_b(c), reads=[fr_tok], writes=[Btok])
                    P.dma(dtl[:], fr_dt(c), reads=[fr_tok], writes=[dtl])
                    dir_stats(0)
                    state_update(0, [PS[0], PS[1]])
                    V(lambda e: e.tensor_tensor(out=runf[:], in0=runf[:], in1=dec[:], op=ALU.mult), [runf, dec], [runf])
                P.dma(ex_src[:, 0:512], Hst[0][:], reads=[Hst[0], ex_tok], writes=[ex_tok])
                P.dma(ex_src[:, 512:1024], Hst[1][:], reads=[Hst[1], ex_tok], writes=[ex_tok])
                P.dma(ex_src[:, 1024:1040], runf[:], reads=[runf, ex_tok], writes=[ex_tok])
                P.dma(ex_src[:, 1040:1056], runb[:], reads=[runb, ex_tok], writes=[ex_tok])
                all_gather4(ex_src, ex_dst, ex_tok, ex_tok2)

            qT = sbw("qT", [128, 4, 128], BF16); sg = sbw("sg", [128, 4, 128], BF16)
            zs = sbw("zs", [128, 1024], BF16)
            STt = sbw("ST", [128, 16, 128], BF16)
            CBm = sbw("CBm", [128, 2, 2, 128], BF16)
            Hin = sbw("Hin", [128, 512], BF16); Hbin = sbw("Hbin", [128, 512])
            yacc = sbw("yacc", [128, 1024]); tmp = sbw("tmp", [128, 512])
            sn = sbw("sn", [128, 1024], BF16); sT = sbw("sT", [128, 8, 128], BF16)
            ET = [sbw(f"ET{i}", [128, 512], BF16) for i in range(2)]
            rd = [sbw(f"rd{i}", [128, 512]) for i in range(2)]
            Es = [sbw(f"Es{i}", [128, 512]) for i in range(2)]
            aT = sbw("aT", [128, 4, 128], BF16)
            xnew = sbw("xnew", [128, 1024])
            if lat:
                h1T = sbw("h1T", [128, 8, 128], BF16)
                ckn = sbw("ckn", [128, 128], BF16); kp = sbw("kp", [128, 32]); kp1 = sbw("kp1", [128, 32]); kp2 = sbw("kp2", [128, 32])
                kpb = sbw("kpb", [128, 32], BF16); csk = sbw("csk", [128, 2, 32])
                ckT = sbw("ckT", [128, 128], BF16); kpT = sbw("kpT", [32, 128], BF16)
            if lat:
                def compose(H, h0src, col0, dcol0, mk, order):
                    load_h0(h0src, H)
                    for q in order:
                        P.dma(Hbin[:], ex_dst[q * 128:(q + 1) * 128, col0:col0 + 512], reads=[ex_tok2], writes=[Hbin])
                        P.dma(dq[:], ex_dst[q * 128:(q + 1) * 128, dcol0:dcol0 + 16], reads=[ex_tok2], writes=[dq])
                        V(lambda e, q=q: e.tensor_scalar(out=coef[:], in0=dq[:], scalar1=-1.0, scalar2=mk[:, q:q + 1], op0=ALU.add, op1=ALU.mult), [dq, mk], [coef])
                        V(lambda e: e.tensor_scalar(out=coef[:], in0=coef[:], scalar1=1.0, scalar2=None, op0=ALU.add), [coef], [coef])
                        for g in range(2):
                            r = slice(64 * g, 64 * g + 64)
                            V(lambda e, r=r, g=g: e.tensor_tensor(out=H[r, :].rearrange("p (h q) -> p h q", h=8), in0=H[r, :].rearrange("p (h q) -> p h q", h=8),
                                                                  in1=bc(coef[r, 8 * g:8 * g + 8].unsqueeze(2), [64, 8, 64]), op=ALU.mult), [H, coef], [H])
                        V(lambda e, q=q: e.scalar_tensor_tensor(out=H[:], in0=Hbin[:], scalar=mk[:, q:q + 1], in1=H[:], op0=ALU.mult, op1=ALU.add), [Hbin, mk, H], [H])
                compose(Hst[0], h0f, 0, 1024, mltb, [0, 1, 2, 3])
                compose(HinB, h0b, 512, 1040, mgtb, [3, 2, 1, 0])
            else:
                V(lambda e: e.memset(Hst[0][:], 0.0), [], [Hst[0]])
            for c in own:
                if USE_CACHE:
                    P.dma(Wn[:], hT_d[:, :, c * 128:c * 128 + 132], reads=[hTd_tok], writes=[Wn])
                    P.dma(xtok[:], fr_x(c), reads=[fr_tok], writes=[xtok])
                    P.dma(Btok[:], fr_b(c), reads=[fr_tok], writes=[Btok])
                    P.dma(xbcT[:, 8:10, :], fr_bc(c), reads=[fr_tok], writes=[xbcT])
                    P.dma(dtl[:], fr_dt(c), reads=[fr_tok], writes=[dtl])
                else:
                    front(c, 10)
                P.dma(xr[:], x_src[c * 128:(c + 1) * 128, :], writes=[xr])
                P.dma(Hbin[:], Hb_d[ro(c)], reads=[Hbd_tok], writes=[Hbin])
                if lat:
                    for g in range(2):
                        r = slice(64 * g, 64 * g + 64)
                        V(lambda e, r=r, g=g, c=c: e.tensor_tensor(out=tmp[r, :].rearrange("p (h q) -> p h q", h=8), in0=HinB[r, :].rearrange("p (h q) -> p h q", h=8),
                                                                   in1=bc(cdbT[r, ro(c), 8 * g:8 * g + 8].unsqueeze(2), [64, 8, 64]), op=ALU.mult), [HinB, cdbT], [tmp])
                    V(lambda e: e.tensor_tensor(out=Hbin[:], in0=Hbin[:], in1=tmp[:], op=ALU.add), [Hbin, tmp], [Hbin])
                Wm = Wn
                P.start_capture()
                if lat:
                    P.dma(cs[:, 0, :], c_cosA[:, c * 128:(c + 1) * 128], writes=[cs])
                    P.dma(cs[:, 1, :], c_sinA[:, c * 128:(c + 1) * 128], writes=[cs])
                for j in range(4):
                    pg = PS[2 + j % 2]
                    for k in range(8):
                        mm(pg, pg[:, 0:128], wI0[:, k, G0 + j * 128:G0 + (j + 1) * 128], Wm[:, k, 2:130], k == 0, k == 7, [wI, Wn])
                    A(lambda e, j=j, pg=pg: e.activation(out=sg[:, j, :], in_=pg[:, 0:128], func=AF.Silu), [pg], [sg])
                for j in range(4):
                    pb = PS[j % 2]
                    for k in range(8):
                        mm(pb, pb[:, 0:128], wI0[:, k, Q0 + j * 128:Q0 + (j + 1) * 128], Wm[:, k, 2:130], k == 0, k == 7, [wI, Wn])
                    if lat:
                        A(lambda e, pb=pb: e.copy(out=uT[:], in_=pb[:, 0:128]), [pb], [uT])
                        pr = PS[2]
                        mm(pr, pr[:, 0:128], pma[:], uT[:], True, True, [pma, uT])
                        V(lambda e: e.tensor_tensor(out=t1[:], in0=uT[:], in1=cs[:, 0, :], op=ALU.mult), [uT, cs], [t1])
                        V(lambda e, pr=pr: e.tensor_tensor(out=t2[:], in0=pr[:, 0:128], in1=cs[:, 1, :], op=ALU.mult), [pr, cs], [t2])
                        G(lambda e, j=j: e.tensor_tensor(out=qT[:, j, :], in0=t1[:], in1=t2[:], op=ALU.add), [t1, t2], [qT])
                    else:
                        V(lambda e, j=j, pb=pb: e.tensor_copy(out=qT[:, j, :], in_=pb[:, 0:128]), [pb], [qT])
                if lat:
                    kbs = [(c - 1, GE, 0 if c == 1 else None), (c, None, None), (c + 1, LE, 1 if c == 8 else None), (nch, None, None), (nch + 1, None, None)]
                else:
                    kbs = [(0, None, None), (1, None, None)]
                for ki, (kb, msk, fidx) in enumerate(kbs):
                    for hh in range(2):
                        pS = PS[hh]
                        r = slice(64 * hh, 64 * hh + 64)
                        for j in range(4):
                            mm(pS, pS[:, j * 128:(j + 1) * 128], kT[r, kb * 128:(kb + 1) * 128], qT[r, j, :], True, True, [kT, qT])
                        A(lambda e, hh=hh, pS=pS: e.activation(out=ET[hh][:], in_=pS[:, 0:512], func=AF.Exp, scale=A_SCALE), [pS], [ET[hh]])
                        if msk is not None:
                            VG(lambda e, hh=hh, msk=msk: e.tensor_tensor(out=ET[hh][:].rearrange("p (j q) -> p j q", j=4), in0=ET[hh][:].rearrange("p (j q) -> p j q", j=4),
                                                                         in1=bc(msk[:].unsqueeze(1), [128, 4, 128]), op=ALU.mult), [ET[hh], msk], [ET[hh]])
                        if fidx is not None:
                            V(lambda e, hh=hh, fidx=fidx: e.tensor_scalar(out=ET[hh][:], in0=ET[hh][:], scalar1=flb[:, fidx:fidx + 1], scalar2=None, op0=ALU.mult),
                              [ET[hh], flb], [ET[hh]])
                        mm(PS[2 + hh], PS[2 + hh][:, 0:512], Vt[:, kb, :], ET[hh][:], ki == 0, ki == len(kbs) - 1, [Vt, ET[hh]])
                        if ki == 0:
                            V(lambda e, hh=hh: e.tensor_copy(out=Es[hh][:], in_=ET[hh][:]), [ET[hh]], [Es[hh]])
                        else:
                            V(lambda e, hh=hh: e.tensor_tensor(out=Es[hh][:], in0=Es[hh][:], in1=ET[hh][:], op=ALU.add), [ET[hh], Es[hh]], [Es[hh]])
                for hh in range(2):
                    mm(PS[hh], PS[hh][:, 0:512], ones[:], Es[hh][:], True, True, [ones, Es[hh]])
                for hh in range(2):
                    r = slice(64 * hh, 64 * hh + 64)
                    V(lambda e, hh=hh, r=r: e.tensor_tensor(out=rd[hh][r, :].rearrange("p (j q) -> p j q", j=4), in0=PS[hh][r, 0:512].rearrange("p (j q) -> p j q", j=4),
                                                            in1=bc(esink[r, 4 * hh:4 * hh + 4].unsqueeze(2), [64, 4, 128]), op=ALU.add), [PS[hh], esink], [rd[hh]])
                    A(lambda e, hh=hh, r=r: e.activation(out=rd[hh][r, :], in_=rd[hh][r, :], func=AF.Ln), [rd[hh]], [rd[hh]])
                    A(lambda e, hh=hh, r=r: e.activation(out=rd[hh][r, :], in_=rd[hh][r, :], func=AF.Exp, scale=-1.0), [rd[hh]], [rd[hh]])
                    V(lambda e, hh=hh, r=r: e.tensor_tensor(out=rd[hh][r, :], in0=PS[2 + hh][r, 0:512], in1=rd[hh][r, :], op=ALU.mult), [PS[2 + hh], rd[hh]], [rd[hh]])
                    G(lambda e, hh=hh, r=r: e.tensor_tensor(out=aT[r, :, :].rearrange("p j q -> p (j q)"), in0=rd[hh][r, :],
                                                            in1=sg[r, :, :].rearrange("p j q -> p (j q)"), op=ALU.mult), [rd[hh], sg], [aT])
                att_ops = P.stop_capture()
                P.start_capture()
                for hf in range(2):
                    pz = PS[6 + hf]
                    for k in range(8):
                        mm(pz, pz[:, 0:512], Wm[:, k, 2:130], wI0[:, k, Z0 + hf * 512:Z0 + (hf + 1) * 512], k == 0, k == 7, [wI, Wn])
                    A(lambda e, hf=hf, pz=pz: e.activation(out=zs[:, hf * 512:(hf + 1) * 512], in_=pz[:, 0:512], func=AF.Silu), [pz], [zs])
                pcbs = [PS[4], PS[5]]
                for g in range(2):
                    mm(pcbs[g], pcbs[g][:, 0:128], xbcT[64 * g:64 * g + 64, 8, :], xbcT[64 * g:64 * g + 64, 9, :], True, True, [xbcT])
                for g in range(2):
                    for d in range(2):
                        V(lambda e, g=g, d=d: e.tensor_tensor(out=CBm[:, g, d, :], in0=pcbs[g][:, 0:128],
                                                              in1=(LE if d == 0 else GE)[:], op=ALU.mult), [pcbs[g], LE, GE], [CBm])
                xv = xtok[:].rearrange("p (h q) -> p h q", h=16)
                G(lambda e: e.tensor_tensor(out=yacc[:].rearrange("p (h q) -> p h q", h=16), in0=xv, in1=bc(dskipb[:].unsqueeze(2), [128, 16, 64]), op=ALU.mult),
                  [xtok, dskipb], [yacc])
                for d in range(2):
                    dir_stats(d)
                    H = Hst[0] if d == 0 else Hbin
                    A(lambda e, H=H: e.copy(out=Hin[:], in_=H[:]), [H], [Hin])
                    G(lambda e, d=d: e.tensor_tensor(out=xd[:].rearrange("p (h q) -> p h q", h=16), in0=xv,
                                                     in1=bc(dtp[:, 16 * d:16 * d + 16].unsqueeze(2), [128, 16, 64]), op=ALU.mult), [xtok, dtp], [xd])
                    tri = LE if d == 0 else GE
                    G(lambda e, d=d, tri=tri: e.tensor_tensor(out=R[:], in0=bc(tri[:].unsqueeze(1), [128, 16, 128]),
                                                              in1=bc(la[:, 16 * d:16 * d + 16].unsqueeze(2), [128, 16, 128]), op=ALU.mult), [tri, la], [R])
                    st = GT if d == 0 else LT
                    for i in range(4):
                        pb = PS[4 + i % 2]
                        mm(pb, pb[:, 0:512], st[:], R[:, 4 * i:4 * i + 4, :].rearrange("p h l -> p (h l)"), True, True, [st, R])
                        A(lambda e, i=i, pb=pb: e.activation(out=STt[:, 4 * i:4 * i + 4, :].rearrange("p h l -> p (h l)"), in_=pb[:, 0:512], func=AF.Exp), [pb], [STt])
                    for i in range(4):
                        g = i // 2
                        VG(lambda e, i=i, g=g, d=d: e.tensor_tensor(out=STt[:, 4 * i:4 * i + 4, :], in0=STt[:, 4 * i:4 * i + 4, :],
                                                                    in1=bc(CBm[:, g, d, :].unsqueeze(1), [128, 4, 128]), op=ALU.mult), [STt, CBm], [STt])
                    for g in range(2):
                        po = PS[6 + g]
                        mm(po, po[:, 0:512], xbcT[64 * g:64 * g + 64, 9, :], Hin[64 * g:64 * g + 64, :], True, True, [xbcT, Hin])
                        pdg = PS[4 + g]
                        for hl in range(8):
                            h = g * 8 + hl
                            mm(pdg, pdg[:, hl * 64:(hl + 1) * 64], STt[:, h, :], xd[:, h * 64:(h + 1) * 64], True, True, [STt, xd])
                        ya = yacc[:, g * 512:(g + 1) * 512]
                        V(lambda e, g=g, po=po: e.tensor_tensor(out=tmp[:].rearrange("p (h q) -> p h q", h=8), in0=po[:, 0:512].rearrange("p (h q) -> p h q", h=8),
                                                                in1=bc(ecum[:, 8 * g:8 * g + 8].unsqueeze(2), [128, 8, 64]), op=ALU.mult), [po, ecum], [tmp])
                        G(lambda e, ya=ya: e.tensor_tensor(out=ya, in0=ya, in1=tmp[:], op=ALU.add), [yacc, tmp], [yacc])
                        V(lambda e, ya=ya, pdg=pdg: e.tensor_tensor(out=ya, in0=ya, in1=pdg[:, 0:512], op=ALU.add), [yacc, pdg], [yacc])
                    if d == 0:
                        state_update(0, [PS[6], PS[7]])
                V(lambda e: e.tensor_tensor(out=yacc[:], in0=yacc[:], in1=zs[:], op=ALU.mult), [yacc, zs], [yacc])
                rms_stats(yacc[:], 1024, yacc, Wk["sq"], Wk["ss"], Wk["rstd"])
                V(lambda e: e.scalar_tensor_tensor(out=sn[:], in0=yacc[:], scalar=Wk["rstd"][:, 0:1], in1=gnormb[:], op0=ALU.mult, op1=ALU.mult),
                  [yacc, Wk["rstd"], gnormb], [sn])
                pt = PS[7]; ptv = pt[:].bitcast(BF16)
                for k in range(8):
                    T(lambda e, k=k: e.transpose(out=ptv[:, k * 128:(k + 1) * 128], in_=sn[:, k * 128:(k + 1) * 128], identity=idb[:]), [sn, idb], [pt])
                V(lambda e: e.tensor_copy(out=sT[:], in_=ptv.rearrange("p (k t) -> p k t", k=8)), [pt], [sT])
                ssd_ops = P.stop_capture()
                P.merge(att_ops, ssd_ops, lead_b=1.6)
                for hf in range(2):
                    po = PS[6 + hf]
                    for k in range(12):
                        lhs = aT[:, k, :] if k < 4 else sT[:, k - 4, :]
                        mm(po, po[:, 0:512], lhs, wO0[:, k, hf * 512:(hf + 1) * 512], k == 0, k == 11, [aT, sT, wO])
                    V(lambda e, hf=hf, po=po: e.tensor_tensor(out=tmp[:], in0=po[:, 0:512], in1=gateb[cond][:, hf * 512:(hf + 1) * 512], op=ALU.mult),
                      [po, gateb[cond]], [tmp])
                    G(lambda e, hf=hf: e.tensor_tensor(out=xnew[:, hf * 512:(hf + 1) * 512], in0=tmp[:], in1=xr[:, hf * 512:(hf + 1) * 512], op=ALU.add),
                      [tmp, xr], [xnew])
                if not lat:
                    P.dma(xc1_d[si, c * 128:(c + 1) * 128, :], xnew[:], reads=[xnew, xc1_tok], writes=[xc1_tok])
                else:
                    P.dma(xl1_d[ro(c) * 128:(ro(c) + 1) * 128, :], xnew[:], reads=[xnew, xl1_tok], writes=[xl1_tok])
                    norm_to_hT(xnew, 1, 1, h1T, Wk)
                    pkv = PS[0]
                    for k in range(8):
                        mm(pkv, pkv[:, 0:160], h1T[:, k, :], wKV1[:, k, :], k == 0, k == 7, [h1T, wKV1])
                    A(lambda e: e.activation(out=Wk["sq"][:, 0:128], in_=pkv[:, 0:128], func=AF.Square, accum_out=Wk["ss"][:]), [pkv], [Wk["sq"], Wk["ss"]])
                    A(lambda e: e.activation(out=Wk["rstd"][:], in_=Wk["ss"][:], func=AF.Ln, scale=1.0 / 128, bias=EPS), [Wk["ss"]], [Wk["rstd"]])
                    A(lambda e: e.activation(out=Wk["rstd"][:], in_=Wk["rstd"][:], func=AF.Exp, scale=-0.5), [Wk["rstd"]], [Wk["rstd"]])
                    V(lambda e: e.scalar_tensor_tensor(out=ckn[:], in0=pkv[:, 0:128], scalar=Wk["rstd"][:, 0:1], in1=kvnormb[:], op0=ALU.mult, op1=ALU.mult),
                      [pkv, Wk["rstd"], kvnormb], [ckn])
                    P.dma(csk[:, 0, :], c_cosK[ro(c) * 128:(ro(c) + 1) * 128, :], writes=[csk])
                    P.dma(csk[:, 1, :], c_sinK[ro(c) * 128:(ro(c) + 1) * 128, :], writes=[csk])
                    A(lambda e: e.copy(out=kp[:], in_=pkv[:, 128:160]), [pkv], [kp])
                    V(lambda e: e.tensor_tensor(out=kp1[:], in0=kp[:], in1=csk[:, 0, :], op=ALU.mult), [kp, csk], [kp1])
                    kv4 = kp[:].rearrange("p (a b i) -> p a b i", a=2, b=2)
                    k24 = kp2[:].rearrange("p (a b i) -> p a b i", a=2, b=2)
                    s4 = csk[:, 1, :].rearrange("p (a b i) -> p a b i", a=2, b=2)
                    V(lambda e: e.tensor_tensor(out=k24[:, :, 0, :], in0=kv4[:, :, 1, :], in1=s4[:, :, 0, :], op=ALU.mult), [kp, csk], [kp2])
                    V(lambda e: e.tensor_tensor(out=k24[:, :, 1, :], in0=kv4[:, :, 0, :], in1=s4[:, :, 1, :], op=ALU.mult), [kp, csk, kp2], [kp2])
                    V(lambda e: e.tensor_tensor(out=kpb[:], in0=kp1[:], in1=kp2[:], op=ALU.add), [kp1, kp2], [kpb])
                    pt3 = PS[1]; ptv3 = pt3[:].bitcast(BF16)
                    T(lambda e: e.transpose(out=ptv3[:, 0:128], in_=ckn[:], identity=idb[:]), [ckn, idb], [pt3])
                    T(lambda e: e.transpose(out=ptv3[0:32, 128:256], in_=kpb[:], identity=idb[:]), [kpb, idb], [pt3])
                    V(lambda e: e.tensor_copy(out=ckT[:], in_=ptv3[:, 0:128]), [pt3], [ckT])
                    V(lambda e: e.tensor_copy(out=kpT[:], in_=ptv3[0:32, 128:256]), [pt3], [kpT])
                    P.dma(ckvT_d[:, ro(c) * 128:(ro(c) + 1) * 128], ckT[:], reads=[ckT, kvT_tok], writes=[kvT_tok])
                    P.dma(kpeT_d[:, ro(c) * 128:(ro(c) + 1) * 128], kpT[:], reads=[kpT, kvT_tok], writes=[kvT_tok])
            if not lat:
                transpose_state_out(Hst[0], nsf[si], Wk)
            else:
                all_gather4(kv_d, kv_g, kvT_tok, kvg_tok)


        def l1_setup(ph):
            for i_ in range(NSTG):
                stg[i_] = P.sb([128, 1024], F32, f"stg{i_}", ph)
            load_w(wI, w1I, w_in1, 8, 1440)
            load_w(wI, wUQ, w_uq, 2, 1536)
            load_w(wI, wUKV3, w_ukv, 1, 2048)
            load_w(wO, wO1, w_out1, 8, 1024)
            V(lambda e: e.memset(wK[:], 0.0), [], [wO])
            ukv4 = wUKV.rearrange("p (h c) -> p h c", h=16)
            V(lambda e: e.tensor_copy(out=wK[:, :, 0:64], in_=ukv4[:, :, 0:64]), [wI], [wO])
            V(lambda e: e.tensor_copy(out=wVp[:, :, 0:64], in_=ukv4[:, 0:8, 64:128]), [wI], [wO])
            V(lambda e: e.tensor_copy(out=wVp[:, :, 64:128], in_=ukv4[:, 8:16, 64:128]), [wI], [wO])
            compute_gates(1, ph)
            P.dma(gnormb[:], fnormw.partition_broadcast(128), writes=[gnormb])

        def mla_sequence(kind, si, ph, boff=None):
            lat = kind == "lat"

            def bk(i):
                return PS[i] if boff is None else PS[boff + i % 4]
            cond = 1 if lat else 0
            ntok = 1024 if lat else 256
            nblk = ntok // 128
            nkb = 34 if lat else 2
            nkeys = nkb * 128
            TG = 512 if lat else 256
            ntg = ntok // TG
            Wk = {}

            def sbw(name, shape, dt=F32):
                Wk[name] = P.sb(shape, dt, name, ph)
                return Wk[name]
            R = sbw("R", [128, 2048])
            Wk["sq"] = R.view(R[:, 0:1024]); Wk["ht"] = R.view(R[:, 1024:2048].rearrange("p (k t) -> p k t", k=8))
            sbw("ss", [128, 1]); sbw("rstd", [128, 1]); sbw("xn", [128, 1024], BF16)
            xr = sbw("xr", [128, 1024]); xq = xr; hT1 = sbw("hT1", [128, 8, 128], BF16)
            cqn = sbw("cqn", [128, 256], BF16); cqT = sbw("cqT", [128, 2, ntok], BF16)
            sg1 = sbw("sg1", [128, 8, ntok], BF16); a1T = sg1
            ckvT = sbw("ckvT", [128, nkeys], BF16); kpeT = sbw("kpeT", [32, nkeys], BF16)
            if lat:
                KT = Vp = None
            else:
                KT = sbw("KT", [96, nkeys], BF16); Vp = sbw("Vp", [128, nkb, 128], BF16)
            qp = sbw("qp", [96, ntok], BF16) if not lat else None
            ET = [sbw(f"ET{i}", [128, TG], BF16) for i in range(6 if lat else 4)]
            rd = sbw("rd", [128, TG])
            fnormb = gnormb; qnormb = sbw("qnormb", [128, 256])
            xfin = sbw("xfin", [128, 1024]); tmp = R.view(R[:, 1024:1536]); yo = xq
            accs = xfin.view(xfin[:, 0:512])
            swp = sbw("swp", [128, 128])
            V(lambda e: e.tensor_copy(out=swp[:, 0:64], in_=ident[:, 64:128]), [ident], [swp])
            V(lambda e: e.tensor_copy(out=swp[:, 64:128], in_=ident[:, 0:64]), [ident], [swp])
            bselb = sbw("bselb", [32, 96], BF16); bself = sbw("bself", [32, 96])
            P.dma(qnormb[:], qnormw.partition_broadcast(128), writes=[qnormb])
            P.dma(bself[:], c_bsel, writes=[bself])
            V(lambda e: e.tensor_copy(out=bselb[:], in_=bself[:]), [bself], [bselb])
            if lat:
                pmm = sbw("pmm", [96, 96]); cq_ = sbw("cosq", [96, 1024]); sq_ = sbw("sinq", [96, 1024])
                uq = sbw("uq", [96, 512]); tq1 = sbw("tq1", [96, 512]); tq2 = sbw("tq2", [96, 512])
                cf = sbw("cf", [128, 2, 160]); cfb = sbw("cfb", [128, 2, 160], BF16)
                P.dma(pmm[:], c_pmm, writes=[pmm]); P.dma(cq_[:], c_cosQ, writes=[cq_]); P.dma(sq_[:], c_sinQ, writes=[sq_])
                for q in range(4):
                    P.dma(ckvT[:, q * 1024:(q + 1) * 1024], kv_g[q * 160:q * 160 + 128, :], reads=[kvg_tok], writes=[ckvT])
                    P.dma(kpeT[:, q * 1024:(q + 1) * 1024], kv_g[q * 160 + 128:(q + 1) * 160, :], reads=[kvg_tok], writes=[kpeT])
                P.dma(cf[:, :, 0:128], cckv.rearrange("(b p) c -> p b c", p=128), writes=[cf])
                P.dma(cf[:, :, 128:160], ckpe.rearrange("(b p) c -> p b c", p=128), writes=[cf])
                V(lambda e: e.tensor_copy(out=cfb[:], in_=cf[:]), [cf], [cfb])
                pt = bk(6); ptv = pt[:].bitcast(BF16)
                for b in range(2):
                    T(lambda e, b=b: e.transpose(out=ptv[:, b * 128:(b + 1) * 128], in_=cfb[:, b, 0:128], identity=idb[:]), [cfb, idb], [pt])
                    T(lambda e, b=b: e.transpose(out=ptv[0:32, 256 + b * 128:256 + (b + 1) * 128], in_=cfb[:, b, 128:160], identity=idb[:]), [cfb, idb], [pt])
                V(lambda e: e.tensor_copy(out=ckvT[:, 4096:4352], in_=ptv[:, 0:256]), [pt], [ckvT])
                V(lambda e: e.tensor_copy(out=kpeT[:, 4096:4352], in_=ptv[0:32, 256:512]), [pt], [kpeT])
            else:
                ckf = sbw("ckf", [128, 160]); ckb = sbw("ckb", [128, 160], BF16)

            def load_x(i):
                if lat:
                    P.dma(xr[:], xl1_d[i * 128:(i + 1) * 128, :], reads=[xl1_tok], writes=[xr])
                else:
                    P.dma(xr[:], xc1_d[si, i * 128:(i + 1) * 128, :], reads=[xc1_tok], writes=[xr])

            for i in range(nblk):
                tsl = slice(i * 128, (i + 1) * 128)
                load_x(i)
                norm_to_hT(xr, 1, cond, hT1, Wk, bk(7))
                ncols = 256 if lat else 416
                pp = bk(0)
                for k in range(8):
                    mm(pp, pp[:, 0:ncols], hT1[:, k, :], w1I[:, k, 0:ncols], k == 0, k == 7, [hT1, wI])
                A(lambda e: e.activation(out=Wk["sq"][:, 0:256], in_=pp[:, 0:256], func=AF.Square, accum_out=Wk["ss"][:]), [pp], [Wk["sq"], Wk["ss"]])
                A(lambda e: e.activation(out=Wk["rstd"][:], in_=Wk["ss"][:], func=AF.Ln, scale=1.0 / 256, bias=EPS), [Wk["ss"]], [Wk["rstd"]])
                A(lambda e: e.activation(out=Wk["rstd"][:], in_=Wk["rstd"][:], func=AF.Exp, scale=-0.5), [Wk["rstd"]], [Wk["rstd"]])
                V(lambda e: e.scalar_tensor_tensor(out=cqn[:], in0=pp[:, 0:256], scalar=Wk["rstd"][:, 0:1], in1=qnormb[:], op0=ALU.mult, op1=ALU.mult),
                  [pp, Wk["rstd"], qnormb], [cqn])
                pt = bk(1); ptv = pt[:].bitcast(BF16)
                for k2 in range(2):
                    T(lambda e, k2=k2: e.transpose(out=ptv[:, k2 * 128:(k2 + 1) * 128], in_=cqn[:, k2 * 128:(k2 + 1) * 128], identity=idb[:]), [cqn, idb], [pt])
                V(lambda e, tsl=tsl: e.tensor_copy(out=cqT[:, :, tsl], in_=ptv[:, 0:256].rearrange("p (k t) -> p k t", k=2)), [pt], [cqT])
                if not lat:
                    A(lambda e: e.activation(out=Wk["sq"][:, 0:128], in_=pp[:, 256:384], func=AF.Square, accum_out=Wk["ss"][:]), [pp], [Wk["sq"], Wk["ss"]])
                    A(lambda e: e.activation(out=Wk["rstd"][:], in_=Wk["ss"][:], func=AF.Ln, scale=1.0 / 128, bias=EPS), [Wk["ss"]], [Wk["rstd"]])
                    A(lambda e: e.activation(out=Wk["rstd"][:], in_=Wk["rstd"][:], func=AF.Exp, scale=-0.5), [Wk["rstd"]], [Wk["rstd"]])
                    V(lambda e: e.scalar_tensor_tensor(out=ckf[:, 0:128], in0=pp[:, 256:384], scalar=Wk["rstd"][:, 0:1], in1=kvnormb[:], op0=ALU.mult, op1=ALU.mult),
                      [pp, Wk["rstd"], kvnormb], [ckf])
                    A(lambda e: e.copy(out=ckf[:, 128:160], in_=pp[:, 384:416]), [pp], [ckf])
                    P.dma(nckv[si, tsl, :], ckf[:, 0:128], reads=[ckf])
                    P.dma(nkpe[si, tsl, :], ckf[:, 128:160], reads=[ckf])
                    V(lambda e: e.tensor_copy(out=ckb[:], in_=ckf[:]), [ckf], [ckb])
                    pt2 = bk(2); ptv2 = pt2[:].bitcast(BF16)
                    T(lambda e: e.transpose(out=ptv2[:, 0:128], in_=ckb[:, 0:128], identity=idb[:]), [ckb, idb], [pt2])
                    T(lambda e: e.transpose(out=ptv2[0:32, 128:256], in_=ckb[:, 128:160], identity=idb[:]), [ckb, idb], [pt2])
                    V(lambda e, tsl=tsl: e.tensor_copy(out=ckvT[:, tsl], in_=ptv2[:, 0:128]), [pt2], [ckvT])
                    V(lambda e, tsl=tsl: e.tensor_copy(out=kpeT[:, tsl], in_=ptv2[0:32, 128:256]), [pt2], [kpeT])
                for j in range(8):
                    pg = bk(3 + j % 2)
                    for k in range(8):
                        mm(pg, pg[:, 0:128], w1I[:, k, 416 + j * 128:416 + (j + 1) * 128], hT1[:, k, :], k == 0, k == 7, [wI, hT1])
                    A(lambda e, j=j, pg=pg, tsl=tsl: e.activation(out=sg1[:, j, tsl], in_=pg[:, 0:128], func=AF.Silu), [pg], [sg1])

            segs = [(s0, min(512, nkeys - s0)) for s0 in range(0, nkeys, 512)]
            ev = [0]

            def evac(out_ap, in_ap, rbufs, wbufs):
                ev[0] += 1
                V(lambda e: e.tensor_copy(out=out_ap, in_=in_ap), rbufs, wbufs)

            if lat:
                NHALF = 2
                hkb = nkb // NHALF
                KTh = [sbw("KTa", [128, hkb * 128], BF16), sbw("KTb", [128, hkb * 128], BF16)]
                Vph = [sbw("Vpa", [128, hkb, 128], BF16), sbw("Vpb", [128, hkb, 128], BF16)]
                qps = [sbw("qp1", [128, ntok], BF16), sbw("qp2", [128, ntok], BF16)]
                for hf_ in range(NHALF):
                    V(lambda e, hf_=hf_: e.memset(KTh[hf_][:], 0.0), [], [KTh[hf_]])
                    P.dma(KTh[hf_][64:96, :], kpeT[:, hf_ * hkb * 128:(hf_ + 1) * hkb * 128], reads=[kpeT], writes=[KTh[hf_]])
                    V(lambda e, hf_=hf_: e.memset(qps[hf_][:], 0.0), [], [qps[hf_]])
                units = [(j, hs, hf) for j in range(8) for hs in range(2) for hf in range(NHALF)]

                def expand(ui):
                    j, hs, hf = units[ui]
                    h = j + 8 * hs
                    KTu, Vpu, qpu = KTh[ui % 2], Vph[ui % 2], qps[(ui // NHALF) % 2]
                    k0 = hf * hkb * 128
                    G(lambda e, hs=hs, Vpu=Vpu: e.memset(Vpu[:, :, 64 * (1 - hs):64 * (1 - hs) + 64], 1.0), [], [Vpu])
                    for kb0 in range(0, hkb, 8):
                        nb = min(8, hkb - kb0)
                        pb = bk(6 + (kb0 // 8) % 2)
                        for b_ in range(nb):
                            kk = k0 + (kb0 + b_) * 128
                            mm(pb, pb[:, b_ * 64:(b_ + 1) * 64], ckvT[:, kk:kk + 128], wVp[:, j, 64 * hs:64 * hs + 64], True, True, [ckvT, wO])
                        evac(Vpu[:, kb0:kb0 + nb, 64 * hs:64 * hs + 64], pb[:, 0:nb * 64].rearrange("p (b c) -> p b c", c=64), [pb], [Vpu])
                    for si_, s0 in enumerate(range(0, hkb * 128, 512)):
                        sn_ = min(512, hkb * 128 - s0)
                        pb = bk(6 + si_ % 2)
                        mm(pb, pb[0:64, 0:sn_], wK[:, h, 0:64], ckvT[:, k0 + s0:k0 + s0 + sn_], True, True, [wO, ckvT])
                        evac(KTu[0:64, s0:s0 + sn_], pb[0:64, 0:sn_], [pb], [KTu])
                    if hf == 0:
                        for tg in range(ntg):
                            tt = slice(tg * TG, (tg + 1) * TG)
                            pb = bk(6 + tg % 2)
                            for k2 in range(2):
                                mm(pb, pb[0:96, 0:TG], wUQ[:, k2, 96 * h:96 * h + 96], cqT[:, k2, tt], k2 == 0, k2 == 1, [wI, cqT])
                            V(lambda e, pb=pb: e.tensor_copy(out=uq[:], in_=pb[0:96, 0:TG]), [pb], [uq])
                            pr = bk(7 - tg % 2)
                            mm(pr, pr[0:96, 0:TG], pmm[:], uq[:], True, True, [pmm, uq])
                            V(lambda e, tt=tt: e.tensor_tensor(out=tq1[:], in0=uq[:], in1=cq_[:, tt], op=ALU.mult), [uq, cq_], [tq1])
                            V(lambda e, tt=tt, pr=pr: e.tensor_tensor(out=tq2[:], in0=pr[0:96, 0:TG], in1=sq_[:, tt], op=ALU.mult), [pr, sq_], [tq2])
                            G(lambda e, tt=tt, qpu=qpu: e.tensor_tensor(out=qpu[0:96, tt], in0=tq1[:], in1=tq2[:], op=ALU.add), [tq1, tq2], [qpu])

                def attend(ui):
                    j, hs, hf = units[ui]
                    r = slice(64 * hs, 64 * hs + 64)
                    KTu, Vpu, qpu = KTh[ui % 2], Vph[ui % 2], qps[(ui // NHALF) % 2]
                    SK = 3
                    for tg in range(ntg):
                        tt = slice(tg * TG, (tg + 1) * TG)
                        pacc = bk(4 + tg)

                        def s_stage(kb):
                            bi = kb % len(ET)
                            pS = bk(kb % 4)
                            mm(pS, pS[:, 0:TG], KTu[:, kb * 128:(kb + 1) * 128], qpu[:, tt], True, True, [KTu, qpu])
                            A(lambda e, bi=bi, pS=pS: e.activation(out=ET[bi][:], in_=pS[:, 0:TG], func=AF.Exp, scale=MLA_SCALE), [pS], [ET[bi]])

                        def pv_stage(kb):
                            bi = kb % len(ET)
                            mm(pacc, pacc[:, 0:TG], Vpu[:, kb, :], ET[bi][:], hf == 0 and kb == 0, hf == NHALF - 1 and kb == hkb - 1, [Vpu, ET[bi]])
                        for kb in range(hkb + SK):
                            if kb < hkb:
                                s_stage(kb)
                            if kb >= SK:
                                pv_stage(kb - SK)
                        if hf == NHALF - 1:
                            A(lambda e, pacc=pacc: e.copy(out=accs[:, 0:TG], in_=pacc[:, 0:TG]), [pacc], [accs])
                            mm(bk(0), bk(0)[:, 0:TG], swp[:], accs[:, 0:TG], True, True, [swp, accs])
                            A(lambda e, r=r: e.activation(out=rd[r, :], in_=bk(0)[r, 0:TG], func=AF.Ln), [bk(0)], [rd])
                            A(lambda e, r=r: e.activation(out=rd[r, :], in_=rd[r, :], func=AF.Exp, scale=-1.0), [rd], [rd])
                            V(lambda e, r=r: e.tensor_tensor(out=rd[r, :], in0=accs[r, 0:TG], in1=rd[r, :], op=ALU.mult), [accs, rd], [rd])
                            G(lambda e, r=r, j=j, tt=tt: e.tensor_tensor(out=a1T[r, j, tt], in0=rd[r, :], in1=sg1[r, j, tt], op=ALU.mult), [rd, sg1], [a1T])

                expand(0)
                for ui in range(len(units)):
                    P.start_capture()
                    attend(ui)
                    sa_ = P.stop_capture()
                    sb2 = []
                    if ui + 1 < len(units):
                        P.start_capture()
                        expand(ui + 1)
                        sb2 = P.stop_capture()
                    P.merge(sa_, sb2)
            else:
                for j in range(8):
                    for hs in range(2):
                        h = j + 8 * hs
                        r = slice(64 * hs, 64 * hs + 64)
                        G(lambda e, hs=hs: e.memset(Vp[:, :, 64 * (1 - hs):64 * (1 - hs) + 64], 1.0), [], [Vp])
                        for kb0 in range(0, nkb, 8):
                            nb = min(8, nkb - kb0)
                            pb = bk(6 + (kb0 // 8) % 2)
                            for b in range(nb):
                                mm(pb, pb[:, b * 64:(b + 1) * 64], ckvT[:, (kb0 + b) * 128:(kb0 + b + 1) * 128], wVp[:, j, 64 * hs:64 * hs + 64], True, True, [ckvT, wO])
                            evac(Vp[:, kb0:kb0 + nb, 64 * hs:64 * hs + 64], pb[:, 0:nb * 64].rearrange("p (b c) -> p b c", c=64), [pb], [Vp])
                        for (s0, sn_) in segs:
                            pb = bk(6 + (s0 // 512) % 2)
                            mm(pb, pb[0:96, 0:sn_], wK[:, h, :], ckvT[:, s0:s0 + sn_], True, False, [wO, ckvT])
                            mm(pb, pb[0:96, 0:sn_], bselb[:], kpeT[:, s0:s0 + sn_], False, True, [bselb, kpeT])
                            evac(KT[:, s0:s0 + sn_], pb[0:96, 0:sn_], [pb], [KT])
                        for tg in range(ntg):
                            tt = slice(tg * TG, (tg + 1) * TG)
                            pb = bk(6 + tg % 2)
                            for k2 in range(2):
                                mm(pb, pb[0:96, 0:TG], wUQ[:, k2, 96 * h:96 * h + 96], cqT[:, k2, tt], k2 == 0, k2 == 1, [wI, cqT])
                            if lat:
                                V(lambda e, pb=pb: e.tensor_copy(out=uq[:], in_=pb[0:96, 0:TG]), [pb], [uq])
                                pr = bk(5)
                                mm(pr, pr[0:96, 0:TG], pmm[:], uq[:], True, True, [pmm, uq])
                                V(lambda e, tt=tt: e.tensor_tensor(out=tq1[:], in0=uq[:], in1=cq_[:, tt], op=ALU.mult), [uq, cq_], [tq1])
                                V(lambda e, tt=tt, pr=pr: e.tensor_tensor(out=tq2[:], in0=pr[0:96, 0:TG], in1=sq_[:, tt], op=ALU.mult), [pr, sq_], [tq2])
                                G(lambda e, tt=tt: e.tensor_tensor(out=qp[:, tt], in0=tq1[:], in1=tq2[:], op=ALU.add), [tq1, tq2], [qp])
                            else:
                                evac(qp[:, tt], pb[0:96, 0:TG], [pb], [qp])
                        for tg in range(ntg):
                            tt = slice(tg * TG, (tg + 1) * TG)
                            SK = 2

                            def s_stage(kb):
                                bi = kb % 4
                                pS = bk(bi)
                                mm(pS, pS[:, 0:TG], KT[:, kb * 128:(kb + 1) * 128], qp[:, tt], True, True, [KT, qp])
                                A(lambda e, bi=bi, pS=pS: e.activation(out=ET[bi][:], in_=pS[:, 0:TG], func=AF.Exp, scale=MLA_SCALE), [pS], [ET[bi]])

                            def pv_stage(kb):
                                bi = kb % 4
                                mm(bk(4), bk(4)[:, 0:TG], Vp[:, kb, :], ET[bi][:], kb == 0, kb == nkb - 1, [Vp, ET[bi]])
                            for kb in range(nkb + SK):
                                if kb < nkb:
                                    s_stage(kb)
                                if kb >= SK:
                                    pv_stage(kb - SK)
                            A(lambda e: e.copy(out=accs[:, 0:TG], in_=bk(4)[:, 0:TG]), [bk(4)], [accs])
                            mm(bk(5), bk(5)[:, 0:TG], swp[:], accs[:, 0:TG], True, True, [swp, accs])
                            V(lambda e, r=r: e.reciprocal(out=rd[r, :], in_=bk(5)[r, 0:TG]), [bk(5)], [rd])
                            V(lambda e, r=r: e.tensor_tensor(out=rd[r, :], in0=accs[r, 0:TG], in1=rd[r, :], op=ALU.mult), [accs, rd], [rd])
                            G(lambda e, r=r, j=j, tt=tt: e.tensor_tensor(out=a1T[r, j, tt], in0=rd[r, :], in1=sg1[r, j, tt], op=ALU.mult), [rd, sg1], [a1T])

            for i in range(nblk):
                tsl = slice(i * 128, (i + 1) * 128)
                load_x(i)
                for hf in range(2):
                    po = bk(hf)
                    for jj in range(8):
                        mm(po, po[:, 0:512], a1T[:, jj, tsl], wO1[:, jj, hf * 512:(hf + 1) * 512], jj == 0, jj == 7, [a1T, wO])
                    V(lambda e, hf=hf, po=po: e.tensor_tensor(out=tmp[:], in0=po[:, 0:512], in1=gateb[cond][:, hf * 512:(hf + 1) * 512], op=ALU.mult),
                      [po, gateb[cond]], [tmp])
                    G(lambda e, hf=hf: e.tensor_tensor(out=xfin[:, hf * 512:(hf + 1) * 512], in0=tmp[:], in1=xr[:, hf * 512:(hf + 1) * 512], op=ALU.add),
                      [tmp, xr], [xfin])
                rms_stats(xfin[:], 1024, xfin, Wk["sq"], Wk["ss"], Wk["rstd"])
                V(lambda e: e.scalar_tensor_tensor(out=yo[:], in0=xfin[:], scalar=Wk["rstd"][:, 0:1], in1=fnormb[:], op0=ALU.mult, op1=ALU.mult),
                  [xfin, Wk["rstd"], fnormb], [yo])
                dst = ys[tsl, :] if lat else yp[si, tsl, :]
                P.dma(dst, yo[:], reads=[yo])

        l0 = ExitStack()
        wdiag = P.sb([128, 10, 5, 128], BF16, "wdiag", l0)
        with ExitStack() as ph:
            compute_gates(0, ph)
            for j in range(10):
                for kk in range(5):
                    V(lambda e, j=j, kk=kk: e.tensor_scalar(out=wdiag[:, j, kk, :], in0=ident[:], scalar1=cw[:, j, kk:kk + 1], scalar2=None, op0=ALU.mult),
                      [ident, cw], [wdiag])
            P.end_phase()
        for si in range(4):
            with ExitStack() as ph:
                ab_sequence("ctx", si, ph)
                P.end_phase()
        if stop_after >= 1:
            with ExitStack() as ph:
                ab_sequence("lat", 0, ph)
                P.end_phase()
        l0.close()
        with ExitStack() as ph:
            l1_setup(ph)
            P.end_phase()
        for si in (0, 2):
            with ExitStack() as ph:
                P.start_capture()
                mla_sequence("ctx", si, ph, 0)
                sa = P.stop_capture()
                P.start_capture()
                mla_sequence("ctx", si + 1, ph, 4)
                sb_ = P.stop_capture()
                P.merge(sa, sb_)
                P.end_phase()
        if stop_after >= 1:
            with ExitStack() as ph:
                mla_sequence("lat", 0, ph)
                P.end_phase()
    return nc


def _consts():
    a = np.arange(128)
    c = {}
    c["c_ident"] = np.eye(128, dtype=np.float32)
    c["c_le"] = (a[:, None] <= a[None, :]).astype(np.float32)
    c["c_ge"] = (a[:, None] >= a[None, :]).astype(np.float32)
    c["c_gt"] = (a[:, None] > a[None, :]).astype(np.float32)
    c["c_lt"] = (a[:, None] < a[None, :]).astype(np.float32)

    def rope_tabs(length, dim):
        rows = length // 64
        row = np.repeat(np.arange(rows), 64).astype(np.float32)
        col = np.tile(np.arange(64), rows).astype(np.float32)
        nf = dim // 4
        inv = (1.0 / (10000.0 ** (np.arange(nf, dtype=np.float32) / nf))).astype(np.float32)
        ar = row[:, None] * inv[None, :]
        ac = col[:, None] * inv[None, :]
        ang = np.concatenate([ar, ar, ac, ac], axis=-1).astype(np.float32)
        sign = np.concatenate([-np.ones(nf), np.ones(nf), -np.ones(nf), np.ones(nf)]).astype(np.float32)
        perm = np.concatenate([np.arange(nf, 2 * nf), np.arange(0, nf), np.arange(3 * nf, 4 * nf), np.arange(2 * nf, 3 * nf)])
        return np.cos(ang).astype(np.float32), (np.sin(ang) * sign[None, :]).astype(np.float32), perm
    cosA, sinA, permA = rope_tabs(4096, 64)
    c["_cosA"] = np.ascontiguousarray(np.concatenate([cosA.T, cosA.T], axis=0))
    c["_sinA"] = np.ascontiguousarray(np.concatenate([sinA.T, sinA.T], axis=0))
    pm = np.zeros((128, 128), np.float32)
    for m in range(128):
        pm[(m // 64) * 64 + permA[m % 64], m] = 1.0
    c["c_pma"] = pm
    cosK, sinK, permK = rope_tabs(4096, 32)
    c["_cosK"] = cosK
    c["_sinK"] = sinK
    pmm = np.zeros((96, 96), np.float32)
    cosQ = np.ones((96, 4096), np.float32)
    sinQ = np.zeros((96, 4096), np.float32)
    cosQ[64:96] = cosK.T
    sinQ[64:96] = sinK.T
    for m in range(96):
        pmm[m if m < 64 else 64 + permK[m - 64], m] = 1.0
    c["c_pmm"] = pmm
    c["_cosQ"] = cosQ
    c["_sinQ"] = sinQ
    bs = np.zeros((32, 96), np.float32)
    bs[np.arange(32), 64 + np.arange(32)] = 1.0
    c["c_bsel"] = bs
    return c


_NC_CACHE = {}


def kernel(**inp):
    f = lambda a: np.ascontiguousarray(np.asarray(a, dtype=np.float32))
    inp = {k: f(v) for k, v in inp.items()}
    pq = np.concatenate([np.concatenate([np.arange(j * 64, (j + 1) * 64), np.arange(256 + j * 64, 256 + (j + 1) * 64)]) for j in range(4)])
    w_in0 = inp["ab_w_in"][0]
    perm0 = np.concatenate([pq, np.arange(512, 768), 768 + pq, np.arange(1280, 3616)])
    w_in0p = np.ascontiguousarray(w_in0[:, perm0])
    w_out0 = inp["ab_w_out"][0]
    w_out0p = np.ascontiguousarray(np.concatenate([w_out0[pq], w_out0[512:]], axis=0))
    pm = np.concatenate([np.concatenate([np.arange(h * 64, (h + 1) * 64), np.arange((h + 8) * 64, (h + 9) * 64)]) for h in range(8)])
    w_in1 = inp["mla_w_in"][0]
    w_in1p = np.ascontiguousarray(np.concatenate([w_in1[:, :416], w_in1[:, 416:][:, pm]], axis=1))
    w_out1p = np.ascontiguousarray(inp["mla_w_out"][0][pm])
    cst = _consts()
    shared = {
        "normwT": np.ascontiguousarray(inp["norm_w"].reshape(2, 8, 128).transpose(0, 2, 1)),
        "w_in0": w_in0p, "w_out0": w_out0p,
        "sink": inp["ab_sink"].reshape(1, 8),
        "cwT": np.ascontiguousarray(inp["ab_conv_w"][0].T.reshape(10, 128, 5).transpose(1, 0, 2)),
        "cbT": np.ascontiguousarray(inp["ab_conv_b"][0].reshape(10, 128).T),
        "dtbias": inp["ab_dt_bias"].reshape(1, 32), "alog": inp["ab_a_log"].reshape(1, 32),
        "dskip": inp["ab_d_skip"].reshape(1, 16), "gnormw": inp["ab_gnorm_w"].reshape(1, 1024),
        "w_in1": w_in1p, "qnormw": inp["mla_q_norm_w"].reshape(1, 256), "kvnormw": inp["mla_kv_norm_w"].reshape(1, 128),
        "w_uq": inp["mla_w_uq"][0], "w_ukv": inp["mla_w_ukv"][0], "w_out1": w_out1p,
        "fnormw": inp["final_norm_w"].reshape(1, 1024),
    }
    for k, v in cst.items():
        if not k.startswith("_"):
            shared[k] = v
    in_maps = []
    for i in range(8):
        b, r = i // 4, i % 4
        m = dict(shared)
        m["xp"] = inp["x_prompt"][4 * i:4 * i + 4]
        lo = 1024 * r - 128
        xs_l = np.zeros((1280, 1024), np.float32)
        a0, a1 = max(lo, 0), min(lo + 1280, 4096)
        xs_l[a0 - lo:a1 - lo] = inp["x_sample"][b, a0:a1]
        m["xs"] = xs_l
        ca = np.zeros((128, 1280), np.float32); sa = np.zeros((128, 1280), np.float32)
        ca[:, a0 - lo:a1 - lo] = cst["_cosA"][:, a0:a1]; sa[:, a0 - lo:a1 - lo] = cst["_sinA"][:, a0:a1]
        m["c_cosA"] = ca; m["c_sinA"] = sa
        m["c_cosK"] = np.ascontiguousarray(cst["_cosK"][r * 1024:(r + 1) * 1024])
        m["c_sinK"] = np.ascontiguousarray(cst["_sinK"][r * 1024:(r + 1) * 1024])
        m["fl"] = np.array([[1.0 if r > 0 else 0.0, 1.0 if r < 3 else 0.0]], np.float32)
        m["mlt"] = np.array([[1.0 if q < r else 0.0 for q in range(4)]], np.float32)
        m["mgt"] = np.array([[1.0 if q > r else 0.0 for q in range(4)]], np.float32)
        m["cak"] = inp["cache_a_k"][b, 0].reshape(256, 128)
        m["cav"] = inp["cache_a_v"][b, 0].reshape(256, 128)
        m["h0f"] = inp["state_ssd_fwd"][b, 0]
        m["h0b"] = inp["state_ssd_bwd"][b, 0]
        m["cckv"] = inp["cache_mla_ckv"][b, 0]
        m["ckpe"] = inp["cache_mla_kpe"][b, 0]
        cond = np.stack([inp["c_ctx"], inp["c"][b]], axis=1)
        c3 = np.stack([inp["c_ctx"], inp["c"][0], inp["c"][1]], axis=1)
        m["cond3"] = np.ascontiguousarray(c3.reshape(8, 128, 3).transpose(1, 0, 2))
        m["oh2"] = np.array([[1.0 if b == 0 else 0.0, 1.0 if b == 1 else 0.0]], np.float32)
        m["ada_w"] = np.ascontiguousarray(inp["ada_w"][:, :, r * 768:(r + 1) * 768])
        m["adabT"] = np.ascontiguousarray(inp["ada_b"].reshape(2, 24, 128).transpose(0, 2, 1)[:, :, 6 * r:6 * r + 6])
        m["condT"] = np.ascontiguousarray(cond.reshape(8, 128, 2).transpose(1, 0, 2))
        s = np.zeros((1, 4), np.float32)
        s[0, r] = 1.0
        m["sel"] = s
        m["c_cosQ"] = np.ascontiguousarray(cst["_cosQ"][:, r * 1024:(r + 1) * 1024])
        m["c_sinQ"] = np.ascontiguousarray(cst["_sinQ"][:, r * 1024:(r + 1) * 1024])
        in_maps.append(m)
    if "nc" not in _NC_CACHE:
        _NC_CACHE["nc"] = build_program()
    res = run_bass_kernel_spmd(_NC_CACHE["nc"], in_maps, core_ids=list(range(8)))
    R = res.results
    cat = lambda k: np.concatenate([np.asarray(R[i][k], dtype=np.float32) for i in range(8)], axis=0)
    y_prompt = cat("yp")
    y_sample = cat("ys").reshape(2, 4096, 1024)
    nk_ = cat("nk").reshape(32, 1, 256, 2, 64)
    nv_ = cat("nv").reshape(32, 1, 256, 2, 64)
    nsf_ = cat("nsf").reshape(32, 1, 16, 64, 64)
    nsb_ = cat("nsb").reshape(32, 1, 16, 64, 64)
    nckv_ = cat("nckv").reshape(32, 1, 256, 128)
    nkpe_ = cat("nkpe").reshape(32, 1, 256, 32)
    return (y_prompt, y_sample, nk_, nv_, nsf_, nsb_, nckv_, nkpe_)
```

```python
import math
import numpy as np
from contextlib import ExitStack
import concourse.bass as bass
import concourse.mybir as mybir
from concourse.bass_utils import run_bass_kernel_spmd

F32 = mybir.dt.float32
BF16 = mybir.dt.bfloat16
AF = mybir.ActivationFunctionType
ALU = mybir.AluOpType

N_DMA_SEMS = 24
import os
USE_CACHE = os.environ.get('KF_CACHE', '1') == '1'
EPS = 1e-6
Q0, K0, V0, G0, Z0, X0, DT0 = 0, 512, 640, 768, 1280, 2304, 3584
A_SCALE = 64 ** -0.5
MLA_SCALE = 96 ** -0.5


class Buf:
    __slots__ = ("name", "t", "lw", "rd", "rd_dma", "root", "excl")

    def __init__(self, name, t=None, root=None):
        self.excl = False
        self.name = name
        self.t = t
        self.lw = None
        self.rd = {}
        self.rd_dma = []
        self.root = root if root is not None else self

    def view(self, ap, name="view"):
        return Buf(name, ap, self.root)

    def __getitem__(self, k):
        return self.t[k]


class Op:
    __slots__ = ("eng", "fn", "deps", "signal", "is_dma", "dma_id", "cnt", "phase")

    def __init__(self, eng, fn, is_dma=False):
        self.eng = eng
        self.fn = fn
        self.deps = []
        self.signal = False
        self.is_dma = is_dma
        self.dma_id = None
        self.cnt = None
        self.phase = 0


class Prog:
    ENGS = ("tensor", "vector", "scalar", "gpsimd", "sync")

    def __init__(self, nc, stack):
        self.nc = nc
        self.stack = stack
        self.ops = {e: [] for e in self.ENGS}
        self.n_dma = 0
        self.nbuf = 0
        self.phase = 0
        self.cnt = {e: 0 for e in self.ENGS}
        self.seen = {e: {} for e in self.ENGS}
        self.seen_dma = {e: set() for e in self.ENGS}
        self.sems = {e: stack.enter_context(nc.semaphore(f"s_{e}")) for e in self.ENGS}
        self.dsems = [stack.enter_context(nc.semaphore(f"d_{i}")) for i in range(N_DMA_SEMS)]
        self.phase_dmas = []
        self.capture = None

    def sb(self, shape, dtype=F32, name="sb", stack=None):
        self.nbuf += 1
        t = (stack or self.stack).enter_context(self.nc.sbuf_tensor(f"{name}_{self.nbuf}", list(shape), dtype))
        return Buf(name, t)

    def ps(self, shape, dtype=F32, name="ps", stack=None):
        self.nbuf += 1
        t = (stack or self.stack).enter_context(self.nc.psum_tensor(f"{name}_{self.nbuf}", list(shape), dtype))
        b = Buf(name, t)
        b.excl = True
        return b

    def token(self, name="tok"):
        return Buf(name, None)

    def start_capture(self):
        self.capture = []

    def stop_capture(self):
        c = self.capture
        self.capture = None
        return c

    def merge(self, a, b, lead_b=1.0):
        na, nb = len(a), len(b)
        i = j = 0
        while i < na or j < nb:
            if j >= nb or (i < na and i * nb * lead_b <= j * na):
                o = a[i]; i += 1
            else:
                o = b[j]; j += 1
            self.ops[o.eng].append(o)

    def _add(self, op, reads, writes):
        op.phase = self.phase
        if self.capture is not None:
            self.capture.append(op)
        else:
            self.ops[op.eng].append(op)
        reads = list({id(r.root): r.root for r in reads}.values())
        writes = list({id(w.root): w.root for w in writes}.values())
        for r in reads:
            if r.excl and not any(r is w for w in writes):
                writes.append(r)
        deps = []
        for r in reads:
            if r.lw is not None:
                deps.append(r.lw)
        for w in writes:
            if w.lw is not None:
                deps.append(w.lw)
            deps.extend(w.rd.values())
            deps.extend(w.rd_dma)
        op.deps = [d for d in deps if d is not op and d.phase == self.phase
                   and not (op.eng == "tensor" and d.eng == "tensor" and not d.is_dma and not op.is_dma)]
        for r in reads:
            if any(r is w for w in writes):
                continue
            if op.is_dma:
                r.rd_dma.append(op)
            else:
                r.rd[op.eng] = op
        for w in writes:
            w.lw = op
            w.rd = {}
            w.rd_dma = []
        return op

    def op(self, eng, fn, reads=(), writes=()):
        return self._add(Op(eng, fn), list(reads), list(writes))

    def dma(self, out_ap, in_ap, reads=(), writes=(), eng="sync", **kw):
        def fn(e, out_ap=out_ap, in_ap=in_ap, kw=kw):
            return e.dma_start(out=out_ap, in_=in_ap, **kw)
        o = Op(eng, fn, is_dma=True)
        o.dma_id = self.n_dma
        self.n_dma += 1
        self.phase_dmas.append(o)
        return self._add(o, list(reads), list(writes))

    def _dma_wait(self, eng, did):
        eng.wait_ge(self.dsems[did % N_DMA_SEMS], 16 * (did // N_DMA_SEMS + 1))

    def end_phase(self):
        nc = self.nc
        ops = self.ops
        last = {}
        for e in self.ENGS:
            real = [o for o in ops[e] if not o.is_dma]
            if real:
                last[e] = real[-1]
                real[-1].signal = True
            for o in ops[e]:
                for d in o.deps:
                    if not d.is_dma:
                        d.signal = True
        for e in self.ENGS:
            for o in ops[e]:
                if o.signal and not o.is_dma:
                    self.cnt[e] += 1
                    o.cnt = self.cnt[e]
        n_dma = self.n_dma
        tail = list(range(max(0, n_dma - N_DMA_SEMS), n_dma))

        def run(ename, eng):
            seen = self.seen[ename]
            seen_dma = self.seen_dma[ename]
            for o in ops[ename]:
                if o.is_dma and o.dma_id >= N_DMA_SEMS:
                    prev = o.dma_id - N_DMA_SEMS
                    if prev not in seen_dma:
                        self._dma_wait(eng, prev)
                        seen_dma.add(prev)
                for d in o.deps:
                    if d.is_dma:
                        if d.dma_id not in seen_dma:
                            self._dma_wait(eng, d.dma_id)
                            seen_dma.add(d.dma_id)
                    elif seen.get(d.eng, 0) < d.cnt:
                        eng.wait_ge(self.sems[d.eng], d.cnt)
                        seen[d.eng] = d.cnt
                inst = o.fn(eng)
                if o.is_dma:
                    inst.then_inc(self.dsems[o.dma_id % N_DMA_SEMS], 16)
                elif o.signal:
                    inst.then_inc(self.sems[ename], 1)
            for e2, l in last.items():
                if e2 != ename and seen.get(e2, 0) < l.cnt:
                    eng.wait_ge(self.sems[e2], l.cnt)
                    seen[e2] = l.cnt
            for did in tail:
                if did not in seen_dma:
                    self._dma_wait(eng, did)
                    seen_dma.add(did)

        with nc.Block() as block:
            @block.sync
            def _(eng):
                run("sync", eng)

            @block.tensor
            def _(eng):
                run("tensor", eng)

            @block.vector
            def _(eng):
                run("vector", eng)

            @block.scalar
            def _(eng):
                run("scalar", eng)

            @block.gpsimd
            def _(eng):
                run("gpsimd", eng)
        self.ops = {e: [] for e in self.ENGS}
        self.phase_dmas = []
        self.phase += 1


def bc(ap, shape):
    return ap.to_broadcast(list(shape))


def build_program(stop_after=99):
    nc = bass.Bass("TRN2", target_bir_lowering=False)

    def din(name, shape, dt=F32):
        return nc.dram_tensor(name, list(shape), dt, kind="ExternalInput").ap()

    def dout(name, shape, dt=F32):
        return nc.dram_tensor(name, list(shape), dt, kind="ExternalOutput").ap()

    def dscr(name, shape, dt=F32):
        return nc.dram_tensor(name, list(shape), dt, kind="Internal").ap()

    xp = din("xp", [4, 256, 1024]); xs = din("xs", [1280, 1024])
    cak = din("cak", [256, 128]); cav = din("cav", [256, 128])
    h0f = din("h0f", [16, 64, 64]); h0b = din("h0b", [16, 64, 64])
    cckv = din("cckv", [256, 128]); ckpe = din("ckpe", [256, 32])
    condT = din("condT", [128, 8, 2]); sel = din("sel", [1, 4])
    fl = din("fl", [1, 2]); mlt = din("mlt", [1, 4]); mgt = din("mgt", [1, 4])
    ada_w = din("ada_w", [2, 1024, 768]); adabT = din("adabT", [2, 128, 6])
    cond3 = din("cond3", [128, 8, 3]); oh2 = din("oh2", [1, 2])
    mod_src = dscr("mod_src", [128, 64]); mod_dst = dscr("mod_dst", [512, 64]); mod_tok = Buf("mod_tok"); mod_tok2 = Buf("mod_tok2")
    normwT = din("normwT", [2, 128, 8])
    w_in0 = din("w_in0", [1024, 3616]); w_out0 = din("w_out0", [1536, 1024])
    sink = din("sink", [1, 8]); cwT = din("cwT", [128, 10, 5]); cbT = din("cbT", [128, 10])
    dtbias = din("dtbias", [1, 32]); alog = din("alog", [1, 32]); dskip = din("dskip", [1, 16])
    gnormw = din("gnormw", [1, 1024])
    w_in1 = din("w_in1", [1024, 1440]); qnormw = din("qnormw", [1, 256]); kvnormw = din("kvnormw", [1, 128])
    w_uq = din("w_uq", [256, 1536]); w_ukv = din("w_ukv", [128, 2048]); w_out1 = din("w_out1", [1024, 1024])
    fnormw = din("fnormw", [1, 1024])
    c_ident = din("c_ident", [128, 128]); c_le = din("c_le", [128, 128]); c_ge = din("c_ge", [128, 128])
    c_gt = din("c_gt", [128, 128]); c_lt = din("c_lt", [128, 128]); c_pma = din("c_pma", [128, 128])
    c_cosA = din("c_cosA", [128, 1280]); c_sinA = din("c_sinA", [128, 1280])
    c_cosK = din("c_cosK", [1024, 32]); c_sinK = din("c_sinK", [1024, 32])
    c_pmm = din("c_pmm", [96, 96]); c_cosQ = din("c_cosQ", [96, 1024]); c_sinQ = din("c_sinQ", [96, 1024])
    c_bsel = din("c_bsel", [32, 96])

    yp = dout("yp", [4, 256, 1024]); ys = dout("ys", [1024, 1024])
    nk = dout("nk", [4, 256, 128]); nv = dout("nv", [4, 256, 128])
    nsf = dout("nsf", [4, 16, 64, 64]); nsb = dout("nsb", [4, 16, 64, 64])
    nckv = dout("nckv", [4, 256, 128]); nkpe = dout("nkpe", [4, 256, 32])

    hT_d = dscr("hT_d", [128, 8, 4100], BF16)
    Hb_d = dscr("Hb_d", [32, 128, 512])
    xl1_d = dscr("xl1_d", [1024, 1024]); xc1_d = dscr("xc1_d", [4, 256, 1024])
    kv_d = dscr("kv_d", [160, 1024], BF16); ckvT_d = kv_d[0:128, :]; kpeT_d = kv_d[128:160, :]
    kv_g = dscr("kv_g", [640, 1024], BF16)
    ex_src = dscr("ex_src", [128, 1056]); ex_dst = dscr("ex_dst", [512, 1056])
    ex_tok = Buf("ex_tok"); ex_tok2 = Buf("ex_tok2"); kvg_tok = Buf("kvg_tok")
    hTd_tok = Buf("hTd_tok"); Hbd_tok = Buf("Hbd_tok"); xl1_tok = Buf("xl1_tok"); xc1_tok = Buf("xc1_tok")
    kvT_tok = Buf("kvT_tok")

    with ExitStack() as st:
        P = Prog(nc, st)
        rr = [0]

        def V(fn, r, w): return P.op("vector", fn, r, w)
        def A(fn, r, w): return P.op("scalar", fn, r, w)
        def G(fn, r, w): return P.op("gpsimd", fn, r, w)
        def T(fn, r, w): return P.op("tensor", fn, r, w)

        def VG(fn, r, w):
            rr[0] += 1
            return P.op("vector" if rr[0] % 2 else "gpsimd", fn, r, w)

        def mm(ps_buf, out_ap, lhsT, rhs, start, stop, reads):
            return T(lambda e: e.matmul(out_ap, lhsT=lhsT, rhs=rhs, start=start, stop=stop), reads, [ps_buf])

        wI = P.sb([128, 28928], BF16, "wI")
        wO = P.sb([128, 12288], BF16, "wO")
        wKV1 = P.sb([128, 8, 160], BF16, "wKV1")
        NSTG = 6
        stg = [None] * NSTG
        ident = P.sb([128, 128], F32, "ident"); idb = P.sb([128, 128], BF16, "idb")
        LE = P.sb([128, 128], F32, "LE"); GE = P.sb([128, 128], F32, "GE")
        GT = P.sb([128, 128], F32, "GT"); LT = P.sb([128, 128], F32, "LT")
        ones = P.sb([128, 128], F32, "ones"); onesb = P.sb([128, 128], BF16, "onesb")
        pma = P.sb([128, 128], F32, "pma")
        mvec = P.sb([128, 2, 24, 2], F32, "mvec")
        modA = P.sb([128, 2, 2, 8], F32, "modA")
        modB = P.sb([128, 2, 2, 8], F32, "modB")
        nwT = P.sb([128, 2, 8], F32, "nwT"); abT = P.sb([128, 2, 6], F32, "abT")
        gateb = [P.sb([128, 1024], F32, f"gateb{c}") for c in range(2)]
        gnormb = P.sb([128, 1024], F32, "gnormb")
        kvnormb = P.sb([128, 128], F32, "kvnormb")
        dskipb = P.sb([128, 16], F32, "dskipb"); dtbiasb = P.sb([128, 32], F32, "dtbiasb")
        arow = P.sb([128, 32], F32, "arow"); esink = P.sb([128, 8], F32, "esink")
        cw = P.sb([128, 10, 5], F32, "cw"); cb = P.sb([128, 10], F32, "cb")
        selb = P.sb([128, 4], F32, "selb")
        flb = P.sb([128, 2], F32, "flb"); mltb = P.sb([128, 4], F32, "mltb"); mgtb = P.sb([128, 4], F32, "mgtb")
        ccdummy = P.sb([128, 1], F32, "ccdummy")
        ccsem = st.enter_context(nc.semaphore("ccsem"))
        cc_count = [0]

        def all_gather4(src_ap, dst_ap, rtok, wtok, groups=((0, 1, 2, 3), (4, 5, 6, 7))):
            def fn(e):
                e.collective_compute("AllGather", op=ALU.bypass, replica_groups=[list(g) for g in groups],
                                     ins=[src_ap.opt()], outs=[dst_ap.opt()]).then_inc(ccsem)
                cc_count[0] += 1
                e.wait_ge(ccsem, cc_count[0])
                return e.memset(ccdummy[:], 0.0)
            P.op("gpsimd", fn, [rtok], [wtok, ccdummy])
        sc = P.sb([128, 8, 2], F32, "sc")
        PS = [P.ps([128, 512], F32, f"bank{i}") for i in range(8)]

        def load_const(buf, src, bcast=False):
            P.dma(buf[:], src.partition_broadcast(128) if bcast else src, writes=[buf])

        load_const(ident, c_ident); load_const(LE, c_le); load_const(GE, c_ge); load_const(GT, c_gt)
        load_const(LT, c_lt); load_const(pma, c_pma)
        V(lambda e: e.tensor_copy(out=idb[:], in_=ident[:]), [ident], [idb])
        V(lambda e: e.memset(ones[:], 1.0), [], [ones])
        V(lambda e: e.memset(onesb[:], 1.0), [], [onesb])
        load_const(nwT, normwT.rearrange("l p k -> p l k")); load_const(abT, adabT.rearrange("l p k -> p l k"))
        load_const(gnormb, gnormw, True)
        load_const(kvnormb, kvnormw, True)
        load_const(dskipb, dskip, True); load_const(dtbiasb, dtbias, True)
        load_const(arow, alog, True); load_const(esink, sink, True)
        load_const(cw, cwT); load_const(cb, cbT); load_const(selb, sel, True)
        load_const(flb, fl, True); load_const(mltb, mlt, True); load_const(mgtb, mgt, True)
        load_const(sc, condT)
        A(lambda e: e.activation(out=arow[:], in_=arow[:], func=AF.Exp), [arow], [arow])
        V(lambda e: e.tensor_scalar(out=arow[:], in0=arow[:], scalar1=-1.0, scalar2=None, op0=ALU.mult), [arow], [arow])
        A(lambda e: e.activation(out=esink[:], in_=esink[:], func=AF.Exp), [esink], [esink])
        A(lambda e: e.activation(out=sc[:], in_=sc[:], func=AF.Silu), [sc], [sc])

        def load_w(dst_buf, dst_view, src, kc, n):
            i = 0
            for k in range(kc):
                for c0 in range(0, n, 1024):
                    w = min(1024, n - c0)
                    s = stg[i % NSTG]
                    P.dma(s[:, 0:w], src[k * 128:(k + 1) * 128, c0:c0 + w], writes=[s])
                    eng = ("vector", "gpsimd", "scalar")[i % 3]
                    if eng == "scalar":
                        A(lambda e, s=s, k=k, c0=c0, w=w: e.copy(out=dst_view[:, k, c0:c0 + w], in_=s[:, 0:w]), [s], [dst_buf])
                    else:
                        P.op(eng, lambda e, s=s, k=k, c0=c0, w=w: e.tensor_copy(out=dst_view[:, k, c0:c0 + w], in_=s[:, 0:w]), [s], [dst_buf])
                    i += 1

        wI0 = wI[:, 0:28928].rearrange("p (k n) -> p k n", k=8)
        wO0 = wO[:, 0:12288].rearrange("p (k n) -> p k n", k=12)
        w1I = wI[:, 0:11520].rearrange("p (k n) -> p k n", k=8)
        wUQ = wI[:, 11520:14592].rearrange("p (k n) -> p k n", k=2)
        wUKV = wI[:, 14592:16640]
        wUKV3 = wI[:, 14592:16640].rearrange("p (k n) -> p k n", k=1)
        wK = wO[:, 8192:9728].rearrange("p (h c) -> p h c", h=16)
        wVp = wO[:, 9728:10752].rearrange("p (j c) -> p j c", j=8)
        wO1 = wO[:, 0:8192].rearrange("p (k n) -> p k n", k=8)

        with ExitStack() as ph:
            for i_ in range(NSTG):
                stg[i_] = P.sb([128, 1024], F32, f"stg{i_}", ph)
            aw = [P.sb([128, 8, 768], F32, f"aw{i}", ph) for i in range(2)]
            sc3 = P.sb([128, 8, 3], F32, "sc3", ph); mloc = P.sb([128, 2, 6, 3], F32, "mloc", ph)
            mall = P.sb([128, 4, 64], F32, "mall", ph); mpk = P.sb([128, 64], F32, "mpk", ph); ohb = P.sb([128, 2], F32, "ohb", ph); mt = P.sb([128, 4, 6], F32, "mt", ph)
            P.dma(sc3[:], cond3, writes=[sc3])
            P.dma(ohb[:], oh2.partition_broadcast(128), writes=[ohb])
            A(lambda e: e.activation(out=sc3[:], in_=sc3[:], func=AF.Silu), [sc3], [sc3])
            for l in range(2):
                a = aw[l]
                P.dma(a[:], ada_w[l].rearrange("(k p) n -> p k n", p=128), writes=[a])
                for j in range(6):
                    pb = PS[j % 2]
                    for k in range(8):
                        mm(pb, pb[:, 0:3], a[:, k, j * 128:(j + 1) * 128], sc3[:, k, :], k == 0, k == 7, [a, sc3])
                    A(lambda e, pb=pb, l=l, j=j: e.activation(out=mloc[:, l, j, :], in_=pb[:, 0:3], func=AF.Identity,
                                                              bias=abT[:, l, j:j + 1], scale=1.0), [pb, abT], [mloc])
            V(lambda e: e.memset(mpk[:], 0.0), [], [mpk])
            V(lambda e: e.tensor_copy(out=mpk[:, 0:36], in_=mloc[:].rearrange("p l j c -> p (l j c)")), [mloc], [mpk])
            P.dma(mod_src, mpk[:], reads=[mpk, mod_tok], writes=[mod_tok])
            all_gather4(mod_src, mod_dst, mod_tok, mod_tok2)
            P.dma(mall[:], mod_dst.rearrange("(r p) c -> p r c", p=128), reads=[mod_tok2], writes=[mall])
            m5 = mall[:, :, 0:36].rearrange("p r (l j c) -> p r l j c", l=2, j=6)
            for l in range(2):
                mv = mvec[:, l, :, :].rearrange("p (r j) c -> p r j c", j=6)
                V(lambda e, l=l, mv=mv: e.tensor_copy(out=mv[:, :, :, 0], in_=m5[:, :, l, :, 0]), [mall], [mvec])
                V(lambda e, l=l: e.tensor_scalar(out=mt[:], in0=m5[:, :, l, :, 1], scalar1=ohb[:, 0:1], scalar2=None, op0=ALU.mult), [mall, ohb], [mt])
                V(lambda e, l=l, mv=mv: e.scalar_tensor_tensor(out=mv[:, :, :, 1], in0=m5[:, :, l, :, 2], scalar=ohb[:, 1:2], in1=mt[:],
                                                               op0=ALU.mult, op1=ALU.add), [mall, ohb, mt], [mvec])
                for c in range(2):
                    V(lambda e, l=l, c=c: e.scalar_tensor_tensor(out=modA[:, l, c, :], in0=mvec[:, l, 8:16, c], scalar=1.0,
                                                                 in1=nwT[:, l, :], op0=ALU.add, op1=ALU.mult), [mvec, nwT], [modA])
                    V(lambda e, l=l, c=c: e.tensor_copy(out=modB[:, l, c, :], in_=mvec[:, l, 0:8, c]), [mvec], [modB])
            load_w(wI, wI0, w_in0, 8, 3616)
            load_w(wO, wO0, w_out0, 12, 1024)
            load_w(wKV1, wKV1[:], w_in1[:, 256:416], 8, 160)
            P.end_phase()

        def compute_gates(l, ph):
            dg = [P.sb([128, 128], F32, f"dg{i}", ph) for i in range(2)]
            for c in range(2):
                for hf in range(2):
                    pb = PS[2 * c + hf]
                    for k4 in range(4):
                        k = hf * 4 + k4
                        d = dg[k % 2]
                        V(lambda e, d=d, k=k, c=c: e.tensor_scalar(out=d[:], in0=ident[:], scalar1=mvec[:, l, 16 + k, c:c + 1], scalar2=None, op0=ALU.mult),
                          [ident, mvec], [d])
                        mm(pb, pb[:, k4 * 128:(k4 + 1) * 128], ones[:], d[:], True, True, [ones, d])
                    V(lambda e, pb=pb, c=c, hf=hf: e.tensor_copy(out=gateb[c][:, hf * 512:(hf + 1) * 512], in_=pb[:, 0:512]), [pb], [gateb[c]])

        def rms_stats(x_ap, n, xbuf, sq, ss, rstd):
            A(lambda e: e.activation(out=sq[:, 0:n], in_=x_ap, func=AF.Square, accum_out=ss[:]), [xbuf], [sq, ss])
            A(lambda e: e.activation(out=rstd[:], in_=ss[:], func=AF.Ln, scale=1.0 / n, bias=EPS), [ss], [rstd])
            A(lambda e: e.activation(out=rstd[:], in_=rstd[:], func=AF.Exp, scale=-0.5), [rstd], [rstd])

        def norm_to_hT(xr, l, c, hT, W, pt=None):
            rms_stats(xr[:], 1024, xr, W["sq"], W["ss"], W["rstd"])
            V(lambda e: e.tensor_scalar(out=W["xn"][:], in0=xr[:], scalar1=W["rstd"][:, 0:1], scalar2=None, op0=ALU.mult),
              [xr, W["rstd"]], [W["xn"]])
            pt = pt if pt is not None else PS[7]
            ptv = pt[:].bitcast(BF16)
            for k in range(8):
                T(lambda e, k=k: e.transpose(out=ptv[:, k * 128:(k + 1) * 128], in_=W["xn"][:, k * 128:(k + 1) * 128], identity=idb[:]),
                  [W["xn"], idb], [pt])
            V(lambda e: e.tensor_tensor(out=W["ht"][:], in0=ptv.rearrange("p (k t) -> p k t", k=8),
                                        in1=bc(modA[:, l, c, :].unsqueeze(2), [128, 8, 128]), op=ALU.mult), [pt, modA], [W["ht"]])
            V(lambda e: e.tensor_tensor(out=hT[:], in0=W["ht"][:], in1=bc(modB[:, l, c, :].unsqueeze(2), [128, 8, 128]), op=ALU.add),
              [W["ht"], modB], [hT])

        def transpose_state_out(Hst, dst, W):
            o = W["stout"]
            for g in range(2):
                for q4 in range(2):
                    pb = PS[(g * 2 + q4) % 4]
                    for i in range(4):
                        hl = q4 * 4 + i
                        T(lambda e, g=g, hl=hl, i=i, pb=pb: e.transpose(out=pb[0:64, i * 64:(i + 1) * 64], in_=Hst[64 * g:64 * g + 64, hl * 64:(hl + 1) * 64],
                                                                    identity=ident[64 * g:64 * g + 64, 64 * g:64 * g + 64]), [Hst, ident], [pb])
                    V(lambda e, g=g, q4=q4, pb=pb: e.tensor_copy(out=o[:, g * 8 + q4 * 4:g * 8 + q4 * 4 + 4, :],
                                                                in_=pb[0:64, 0:256].rearrange("p (h n) -> p h n", h=4)), [pb], [o])
            P.dma(dst.rearrange("h p n -> p h n"), o[:], reads=[o])

        def ab_sequence(kind, si, ph):
            lat = kind == "lat"
            cond = 1 if lat else 0
            L = 1280 if lat else 256
            nch = L // 128
            own = list(range(1, 9)) if lat else list(range(nch))

            def ro(c):
                return c - 1 if lat else c
            x_src = xs if lat else xp[si]
            Wk = {}
            G = V
            VG = V
            res_d = xl1_d if lat else xc1_d[si]
            fr_tok = xl1_tok if lat else xc1_tok

            def fr_x(c): return res_d[ro(c) * 128:(ro(c) + 1) * 128, :].bitcast(BF16)[:, 0:1024]
            def fr_b(c): return res_d[ro(c) * 128:(ro(c) + 1) * 128, :].bitcast(BF16)[:, 1024:1152]
            def fr_bc(c): return res_d[ro(c) * 128:(ro(c) + 1) * 128, :].bitcast(BF16)[:, 1152:1408].rearrange("p (a b) -> p a b", a=2)
            def fr_dt(c): return res_d[ro(c) * 128:(ro(c) + 1) * 128, 704:768].rearrange("p (a b) -> p a b", a=2)

            def sbw(name, shape, dt=F32):
                Wk[name] = P.sb(shape, dt, name, ph)
                return Wk[name]
            R = sbw("R", [128, 16, 128])
            Rf = R[:].rearrange("p h l -> p (h l)")
            Wk["sq"] = R.view(Rf[:, 0:1024]); Wk["ht"] = R.view(Rf[:, 1024:2048].rearrange("p (k t) -> p k t", k=8))
            sbw("ss", [128, 1]); sbw("rstd", [128, 1]); sbw("xn", [128, 1024], BF16)
            sbw("stout", [64, 16, 64])
            xr = sbw("xr", [128, 1024]); hT = sbw("hT", [128, 8, 128], BF16)
            nkb = nch + (2 if lat else 0)
            kT = sbw("kT", [128, nkb * 128], BF16); Vt = sbw("Vt", [128, nkb, 128], BF16)
            zt = sbw("zt", [128, 8, 4], BF16)
            uT = sbw("uT", [128, 128]); cs = sbw("cs", [128, 2, 128]); t1 = sbw("t1", [128, 128]); t2 = sbw("t2", [128, 128])
            kvo = sbw("kvo", [128, 2, 128])

            V(lambda e: e.memset(zt[:], 0.0), [], [zt])
            P.dma(hT_d[:, :, 0:2], zt[:, :, 0:2], reads=[zt, hTd_tok], writes=[hTd_tok])
            P.dma(hT_d[:, :, L + 2:L + 4], zt[:, :, 2:4], reads=[zt, hTd_tok], writes=[hTd_tok])
            if lat:
                cf = sbw("cf", [128, 2, 2, 128])
                P.dma(cf[:, 0, :, :], cak.rearrange("(b p) c -> p b c", p=128), writes=[cf])
                P.dma(cf[:, 1, :, :], cav.rearrange("(b p) c -> p b c", p=128), writes=[cf])
                cfb = sbw("cfb", [128, 2, 128], BF16)
                V(lambda e: e.tensor_copy(out=cfb[:], in_=cf[:, 0, :, :]), [cf], [cfb])
                V(lambda e: e.tensor_copy(out=Vt[:, nch:nch + 2, :], in_=cf[:, 1, :, :]), [cf], [Vt])
                pt = PS[6]; ptv = pt[:].bitcast(BF16)
                for b in range(2):
                    T(lambda e, b=b: e.transpose(out=ptv[:, b * 128:(b + 1) * 128], in_=cfb[:, b, :], identity=idb[:]), [cfb, idb], [pt])
                V(lambda e: e.tensor_copy(out=kT[:, L:L + 256], in_=ptv[:, 0:256]), [pt], [kT])
            for c in range(nch):
                P.dma(xr[:], x_src[c * 128:(c + 1) * 128, :], writes=[xr])
                norm_to_hT(xr, 0, cond, hT, Wk)
                if lat and c in (0, nch - 1):
                    fi = 0 if c == 0 else 1
                    V(lambda e, fi=fi: e.tensor_scalar(out=hT[:], in0=hT[:], scalar1=flb[:, fi:fi + 1], scalar2=None, op0=ALU.mult), [hT, flb], [hT])
                P.dma(hT_d[:, :, 2 + c * 128:2 + (c + 1) * 128], hT[:], reads=[hT, hTd_tok], writes=[hTd_tok])
                pk = PS[0]; pv = PS[1]
                for k in range(8):
                    mm(pk, pk[:, 0:128], wI0[:, k, K0:K0 + 128], hT[:, k, :], k == 0, k == 7, [wI, hT])
                for k in range(8):
                    mm(pv, pv[:, 0:128], hT[:, k, :], wI0[:, k, V0:V0 + 128], k == 0, k == 7, [wI, hT])
                if lat:
                    P.dma(cs[:, 0, :], c_cosA[:, c * 128:(c + 1) * 128], writes=[cs])
                    P.dma(cs[:, 1, :], c_sinA[:, c * 128:(c + 1) * 128], writes=[cs])
                    A(lambda e: e.copy(out=uT[:], in_=pk[:, 0:128]), [pk], [uT])
                    pr = PS[2]
                    mm(pr, pr[:, 0:128], pma[:], uT[:], True, True, [pma, uT])
                    V(lambda e: e.tensor_tensor(out=t1[:], in0=uT[:], in1=cs[:, 0, :], op=ALU.mult), [uT, cs], [t1])
                    V(lambda e: e.tensor_tensor(out=t2[:], in0=pr[:, 0:128], in1=cs[:, 1, :], op=ALU.mult), [pr, cs], [t2])
                    G(lambda e, c=c: e.tensor_tensor(out=kT[:, c * 128:(c + 1) * 128], in0=t1[:], in1=t2[:], op=ALU.add), [t1, t2], [kT])
                    V(lambda e, c=c: e.tensor_copy(out=Vt[:, c, :], in_=pv[:, 0:128]), [pv], [Vt])
                else:
                    A(lambda e, c=c: e.copy(out=kT[:, c * 128:(c + 1) * 128], in_=pk[:, 0:128]), [pk], [kT])
                    V(lambda e, c=c: e.tensor_copy(out=Vt[:, c, :], in_=pv[:, 0:128]), [pv], [Vt])
                    V(lambda e: e.tensor_copy(out=kvo[:, 1, :], in_=pv[:, 0:128]), [pv], [kvo])
                    pk2 = PS[2]
                    for k in range(8):
                        mm(pk2, pk2[:, 0:128], hT[:, k, :], wI0[:, k, K0:K0 + 128], k == 0, k == 7, [wI, hT])
                    A(lambda e: e.copy(out=kvo[:, 0, :], in_=pk2[:, 0:128]), [pk2], [kvo])
                    P.dma(nk[si, c * 128:(c + 1) * 128, :], kvo[:, 0, :], reads=[kvo])
                    P.dma(nv[si, c * 128:(c + 1) * 128, :], kvo[:, 1, :], reads=[kvo])

            Wn = sbw("Wn", [128, 8, 132], BF16)
            raw = sbw("rawb", [128, 10, 132], BF16)
            xbcT = sbw("xbcT", [128, 10, 128], BF16)
            xtok = sbw("xtok", [128, 1024], BF16); Btok = sbw("Btok", [128, 128], BF16)
            dtr = sbw("dtr", [128, 32]); dtl = sbw("dtl", [128, 2, 32])
            dtp = dtl.view(dtl[:, 0, :]); la = dtl.view(dtl[:, 1, :])
            tot = sbw("tot", [128, 16]); tmc = sbw("tmc", [128, 16]); wst = sbw("wst", [128, 16])
            dec = sbw("dec", [128, 16]); ecum = sbw("ecum", [128, 16]); dw = sbw("dw", [128, 16])
            xd = sbw("xd", [128, 1024], BF16); xdw = sbw("xdw", [128, 1024], BF16)
            Hst = [sbw(f"Hst{d}", [128, 512]) for d in range(2)]

            def front(c, nxc):
                P.dma(Wn[:], hT_d[:, :, c * 128:c * 128 + 132], reads=[hTd_tok], writes=[Wn])
                for j in range(nxc):
                    pb = PS[j % 4]
                    col = X0 + (j if nxc == 10 else j) * 128
                    for k in range(8):
                        mm(pb, pb[:, 0:132], wI0[:, k, col:col + 128], Wn[:, k, :], k == 0, k == 7, [wI, Wn])
                    if j % 2:
                        A(lambda e, j=j, pb=pb: e.copy(out=raw[:, j, :], in_=pb[:, 0:132]), [pb], [raw])
                    else:
                        V(lambda e, j=j, pb=pb: e.tensor_copy(out=raw[:, j, :], in_=pb[:, 0:132]), [pb], [raw])
                for j in range(nxc):
                    pb = PS[j % 4]
                    for kk in range(5):
                        mm(pb, pb[:, 0:128], wdiag[:, j, kk, :], raw[:, j, kk:kk + 128], kk == 0, kk == 4, [wdiag, raw])
                    A(lambda e, j=j, pb=pb: e.activation(out=xbcT[:, j, :], in_=pb[:, 0:128], func=AF.Silu, bias=cb[:, j:j + 1], scale=1.0), [pb, cb], [xbcT])
                pd = PS[4]
                for k in range(8):
                    mm(pd, pd[:, 0:32], Wn[:, k, 2:130], wI0[:, k, DT0:DT0 + 32], k == 0, k == 7, [wI, Wn])
                V(lambda e: e.tensor_tensor(out=dtr[:], in0=pd[:, 0:32], in1=dtbiasb[:], op=ALU.add), [pd, dtbiasb], [dtr])
                A(lambda e: e.activation(out=dtr[:], in_=dtr[:], func=AF.Exp), [dtr], [dtr])
                A(lambda e: e.activation(out=dtp[:], in_=dtr[:], func=AF.Ln, bias=1.0, scale=1.0), [dtr], [dtp])
                V(lambda e: e.tensor_tensor(out=la[:], in0=dtp[:], in1=arow[:], op=ALU.mult), [dtp, arow], [la])
                pt = PS[5]; ptv = pt[:].bitcast(BF16)
                for j in range(8):
                    T(lambda e, j=j: e.transpose(out=ptv[:, j * 128:(j + 1) * 128], in_=xbcT[:, j, :], identity=idb[:]), [xbcT, idb], [pt])
                V(lambda e: e.tensor_copy(out=xtok[:], in_=ptv), [pt], [xtok])
                pt2 = PS[6]; ptv2 = pt2[:].bitcast(BF16)
                T(lambda e: e.transpose(out=ptv2[:, 0:128], in_=xbcT[:, 8, :], identity=idb[:]), [xbcT, idb], [pt2])
                A(lambda e: e.copy(out=Btok[:], in_=ptv2[:, 0:128]), [pt2], [Btok])

            def dir_stats(d):
                pb = PS[6]
                ld = la[:, 16 * d:16 * d + 16]
                mm(pb, pb[:, 0:16], (LE if d == 0 else GE)[:], ld, True, True, [LE, GE, la])
                mm(pb, pb[:, 16:32], ones[:], ld, True, True, [ones, la])
                A(lambda e: e.activation(out=ecum[:], in_=pb[:, 0:16], func=AF.Exp), [pb], [ecum])
                V(lambda e: e.tensor_copy(out=tot[:], in_=pb[:, 16:32]), [pb], [tot])
                V(lambda e: e.tensor_tensor(out=tmc[:], in0=tot[:], in1=pb[:, 0:16], op=ALU.subtract), [tot, pb], [tmc])
                A(lambda e: e.activation(out=wst[:], in_=tmc[:], func=AF.Exp), [tmc], [wst])
                A(lambda e: e.activation(out=dec[:], in_=tot[:], func=AF.Exp), [tot], [dec])
                V(lambda e: e.tensor_tensor(out=dw[:], in0=dtp[:, 16 * d:16 * d + 16], in1=wst[:], op=ALU.mult), [dtp, wst], [dw])
                xv = xtok[:].rearrange("p (h q) -> p h q", h=16)
                G(lambda e: e.tensor_tensor(out=xdw[:].rearrange("p (h q) -> p h q", h=16), in0=xv, in1=bc(dw[:].unsqueeze(2), [128, 16, 64]), op=ALU.mult),
                  [xtok, dw], [xdw])

            def state_update(d, psS):
                H = Hst[d]
                for g in range(2):
                    pb = psS[g]
                    mm(pb, pb[:, 0:512], Btok[:], xdw[:, g * 512:(g + 1) * 512], True, True, [Btok, xdw])
                for g in range(2):
                    pb = psS[g]
                    r = slice(64 * g, 64 * g + 64)
                    G(lambda e, r=r, g=g: e.tensor_tensor(out=H[r, :].rearrange("p (h q) -> p h q", h=8), in0=H[r, :].rearrange("p (h q) -> p h q", h=8),
                                                          in1=bc(dec[r, 8 * g:8 * g + 8].unsqueeze(2), [64, 8, 64]), op=ALU.mult), [H, dec], [H])
                    V(lambda e, r=r, pb=pb: e.tensor_tensor(out=H[r, :], in0=H[r, :], in1=pb[r, 0:512], op=ALU.add), [H, pb], [H])

            def load_h0(src, H):
                o = Wk["stout"]
                P.dma(o[:], src.rearrange("h p n -> p h n"), writes=[o])
                V(lambda e: e.memset(R[:], 0.0), [], [R])
                for g in range(2):
                    V(lambda e, g=g: e.tensor_copy(out=R[0:64, 8 * g:8 * g + 8, 64 * g:64 * g + 64], in_=o[:, 8 * g:8 * g + 8, :]), [o], [R])
                for g in range(2):
                    for q4 in range(2):
                        pb = PS[(g * 2 + q4) % 4]
                        for i in range(4):
                            h = g * 8 + q4 * 4 + i
                            mm(pb, pb[:, i * 64:(i + 1) * 64], R[0:64, h, :], ident[0:64, 0:64], True, True, [R, ident])
                        V(lambda e, g=g, q4=q4, pb=pb: e.tensor_copy(out=H[64 * g:64 * g + 64, q4 * 256:(q4 + 1) * 256],
                                                                    in_=pb[64 * g:64 * g + 64, 0:256]), [pb], [H])

            V(lambda e: e.memset(Hst[1][:], 0.0), [], [Hst[1]])
            if lat:
                cdbT = sbw("cdbT", [128, 8, 16]); runb = sbw("runb", [128, 16]); runf = sbw("runf", [128, 16])
                dq = sbw("dq", [128, 16]); coef = sbw("coef", [128, 16]); HinB = sbw("HinB", [128, 512])
                V(lambda e: e.memset(runb[:], 1.0), [], [runb])
                V(lambda e: e.memset(runf[:], 1.0), [], [runf])
            for c in reversed(own):
                P.dma(Hb_d[ro(c)], Hst[1][:], reads=[Hst[1], Hbd_tok], writes=[Hbd_tok])
                if lat:
                    V(lambda e, c=c: e.tensor_copy(out=cdbT[:, ro(c), :], in_=runb[:]), [runb], [cdbT])
                front(c, 10 if USE_CACHE else 9)
                if USE_CACHE:
                  P.dma(fr_x(c), xtok[:], reads=[xtok, fr_tok], writes=[fr_tok])
                  P.dma(fr_b(c), Btok[:], reads=[Btok, fr_tok], writes=[fr_tok])
                  P.dma(fr_bc(c), xbcT[:, 8:10, :], reads=[xbcT, fr_tok], writes=[fr_tok])
                  P.dma(fr_dt(c), dtl[:], reads=[dtl, fr_tok], writes=[fr_tok])
                dir_stats(1)
                state_update(1, [PS[0], PS[1]])
                if lat:
                    V(lambda e: e.tensor_tensor(out=runb[:], in0=runb[:], in1=dec[:], op=ALU.mult), [runb, dec], [runb])
            if not lat:
                transpose_state_out(Hst[1], nsb[si], Wk)
            else:
                V(lambda e: e.memset(Hst[0][:], 0.0), [], [Hst[0]])
                for c in own:
                    P.dma(xtok[:], fr_x(c), reads=[fr_tok], writes=[xtok])
                    P.dma(Btok[:], fr_b(c), reads=[fr_tok], writes=[Btok])
                    P.dma(dtl[:], fr_dt(c), reads=[fr_tok], writes=[dtl])
                    dir_stats(0)
                    state_update(0, [PS[0], PS[1]])
                    V(lambda e: e.tensor_tensor(out=runf[:], in0=runf[:], in1=dec[:], op=ALU.mult), [runf, dec], [runf])
                P.dma(ex_src[:, 0:512], Hst[0][:], reads=[Hst[0], ex_tok], writes=[ex_tok])
                P.dma(ex_src[:, 512:1024], Hst[1][:], reads=[Hst[1], ex_tok], writes=[ex_tok])
                P.dma(ex_src[:, 1024:1040], runf[:], reads=[runf, ex_tok], writes=[ex_tok])
                P.dma(ex_src[:, 1040:1056], runb[:], reads=[runb, ex_tok], writes=[ex_tok])
                all_gather4(ex_src, ex_dst, ex_tok, ex_tok2)

            qT = sbw("qT", [128, 4, 128], BF16); sg = sbw("sg", [128, 4, 128], BF16)
            zs = sbw("zs", [128, 1024], BF16)
            STt = sbw("ST", [128, 16, 128], BF16)
            CBm = sbw("CBm", [128, 2, 2, 128], BF16)
            Hin = sbw("Hin", [128, 512], BF16); Hbin = sbw("Hbin", [128, 512])
            yacc = sbw("yacc", [128, 1024]); tmp = sbw("tmp", [128, 512])
            sn = sbw("sn", [128, 1024], BF16); sT = sbw("sT", [128, 8, 128], BF16)
            ET = [sbw(f"ET{i}", [128, 512], BF16) for i in range(2)]
            rd = [sbw(f"rd{i}", [128, 512]) for i in range(2)]
            Es = [sbw(f"Es{i}", [128, 512]) for i in range(2)]
            aT = sbw("aT", [128, 4, 128], BF16)
            xnew = sbw("xnew", [128, 1024])
            if lat:
                h1T = sbw("h1T", [128, 8, 128], BF16)
                ckn = sbw("ckn", [128, 128], BF16); kp = sbw("kp", [128, 32]); kp1 = sbw("kp1", [128, 32]); kp2 = sbw("kp2", [128, 32])
                kpb = sbw("kpb", [128, 32], BF16); csk = sbw("csk", [128, 2, 32])
                ckT = sbw("ckT", [128, 128], BF16); kpT = sbw("kpT", [32, 128], BF16)
            if lat:
                def compose(H, h0src, col0, dcol0, mk, order):
                    load_h0(h0src, H)
                    for q in order:
                        P.dma(Hbin[:], ex_dst[q * 128:(q + 1) * 128, col0:col0 + 512], reads=[ex_tok2], writes=[Hbin])
                        P.dma(dq[:], ex_dst[q * 128:(q + 1) * 128, dcol0:dcol0 + 16], reads=[ex_tok2], writes=[dq])
                        V(lambda e, q=q: e.tensor_scalar(out=coef[:], in0=dq[:], scalar1=-1.0, scalar2=mk[:, q:q + 1], op0=ALU.add, op1=ALU.mult), [dq, mk], [coef])
                        V(lambda e: e.tensor_scalar(out=coef[:], in0=coef[:], scalar1=1.0, scalar2=None, op0=ALU.add), [coef], [coef])
                        for g in range(2):
                            r = slice(64 * g, 64 * g + 64)
                            V(lambda e, r=r, g=g: e.tensor_tensor(out=H[r, :].rearrange("p (h q) -> p h q", h=8), in0=H[r, :].rearrange("p (h q) -> p h q", h=8),
                                                                  in1=bc(coef[r, 8 * g:8 * g + 8].unsqueeze(2), [64, 8, 64]), op=ALU.mult), [H, coef], [H])
                        V(lambda e, q=q: e.scalar_tensor_tensor(out=H[:], in0=Hbin[:], scalar=mk[:, q:q + 1], in1=H[:], op0=ALU.mult, op1=ALU.add), [Hbin, mk, H], [H])
                compose(Hst[0], h0f, 0, 1024, mltb, [0, 1, 2, 3])
                compose(HinB, h0b, 512, 1040, mgtb, [3, 2, 1, 0])
            else:
                V(lambda e: e.memset(Hst[0][:], 0.0), [], [Hst[0]])
            for c in own:
                if USE_CACHE:
                    P.dma(Wn[:], hT_d[:, :, c * 128:c * 128 + 132], reads=[hTd_tok], writes=[Wn])
                    P.dma(xtok[:], fr_x(c), reads=[fr_tok], writes=[xtok])
                    P.dma(Btok[:], fr_b(c), reads=[fr_tok], writes=[Btok])
                    P.dma(xbcT[:, 8:10, :], fr_bc(c), reads=[fr_tok], writes=[xbcT])
                    P.dma(dtl[:], fr_dt(c), reads=[fr_tok], writes=[dtl])
                else:
                    front(c, 10)
                P.dma(xr[:], x_src[c * 128:(c + 1) * 128, :], writes=[xr])
                P.dma(Hbin[:], Hb_d[ro(c)], reads=[Hbd_tok], writes=[Hbin])
                if lat:
                    for g in range(2):
                        r = slice(64 * g, 64 * g + 64)
                        V(lambda e, r=r, g=g, c=c: e.tensor_tensor(out=tmp[r, :].rearrange("p (h q) -> p h q", h=8), in0=HinB[r, :].rearrange("p (h q) -> p h q", h=8),
                                                                   in1=bc(cdbT[r, ro(c), 8 * g:8 * g + 8].unsqueeze(2), [64, 8, 64]), op=ALU.mult), [HinB, cdbT], [tmp])
                    V(lambda e: e.tensor_tensor(out=Hbin[:], in0=Hbin[:], in1=tmp[:], op=ALU.add), [Hbin, tmp], [Hbin])
                Wm = Wn
                P.start_capture()
                if lat:
                    P.dma(cs[:, 0, :], c_cosA[:, c * 128:(c + 1) * 128], writes=[cs])
                    P.dma(cs[:, 1, :], c_sinA[:, c * 128:(c + 1) * 128], writes=[cs])
                for j in range(4):
                    pg = PS[2 + j % 2]
                    for k in range(8):
                        mm(pg, pg[:, 0:128], wI0[:, k, G0 + j * 128:G0 + (j + 1) * 128], Wm[:, k, 2:130], k == 0, k == 7, [wI, Wn])
                    A(lambda e, j=j, pg=pg: e.activation(out=sg[:, j, :], in_=pg[:, 0:128], func=AF.Silu), [pg], [sg])
                for j in range(4):
                    pb = PS[j % 2]
                    for k in range(8):
                        mm(pb, pb[:, 0:128], wI0[:, k, Q0 + j * 128:Q0 + (j + 1) * 128], Wm[:, k, 2:130], k == 0, k == 7, [wI, Wn])
                    if lat:
                        A(lambda e, pb=pb: e.copy(out=uT[:], in_=pb[:, 0:128]), [pb], [uT])
                        pr = PS[2]
                        mm(pr, pr[:, 0:128], pma[:], uT[:], True, True, [pma, uT])
                        V(lambda e: e.tensor_tensor(out=t1[:], in0=uT[:], in1=cs[:, 0, :], op=ALU.mult), [uT, cs], [t1])
                        V(lambda e, pr=pr: e.tensor_tensor(out=t2[:], in0=pr[:, 0:128], in1=cs[:, 1, :], op=ALU.mult), [pr, cs], [t2])
                        G(lambda e, j=j: e.tensor_tensor(out=qT[:, j, :], in0=t1[:], in1=t2[:], op=ALU.add), [t1, t2], [qT])
                    else:
                        V(lambda e, j=j, pb=pb: e.tensor_copy(out=qT[:, j, :], in_=pb[:, 0:128]), [pb], [qT])
                if lat:
                    kbs = [(c - 1, GE, 0 if c == 1 else None), (c, None, None), (c + 1, LE, 1 if c == 8 else None), (nch, None, None), (nch + 1, None, None)]
                else:
                    kbs = [(0, None, None), (1, None, None)]
                for ki, (kb, msk, fidx) in enumerate(kbs):
                    for hh in range(2):
                        pS = PS[hh]
                        r = slice(64 * hh, 64 * hh + 64)
                        for j in range(4):
                            mm(pS, pS[:, j * 128:(j + 1) * 128], kT[r, kb * 128:(kb + 1) * 128], qT[r, j, :], True, True, [kT, qT])
                        A(lambda e, hh=hh, pS=pS: e.activation(out=ET[hh][:], in_=pS[:, 0:512], func=AF.Exp, scale=A_SCALE), [pS], [ET[hh]])
                        if msk is not None:
                            VG(lambda e, hh=hh, msk=msk: e.tensor_tensor(out=ET[hh][:].rearrange("p (j q) -> p j q", j=4), in0=ET[hh][:].rearrange("p (j q) -> p j q", j=4),
                                                                         in1=bc(msk[:].unsqueeze(1), [128, 4, 128]), op=ALU.mult), [ET[hh], msk], [ET[hh]])
                        if fidx is not None:
                            V(lambda e, hh=hh, fidx=fidx: e.tensor_scalar(out=ET[hh][:], in0=ET[hh][:], scalar1=flb[:, fidx:fidx + 1], scalar2=None, op0=ALU.mult),
                              [ET[hh], flb], [ET[hh]])
                        mm(PS[2 + hh], PS[2 + hh][:, 0:512], Vt[:, kb, :], ET[hh][:], ki == 0, ki == len(kbs) - 1, [Vt, ET[hh]])
                        if ki == 0:
                            V(lambda e, hh=hh: e.tensor_copy(out=Es[hh][:], in_=ET[hh][:]), [ET[hh]], [Es[hh]])
                        else:
                            V(lambda e, hh=hh: e.tensor_tensor(out=Es[hh][:], in0=Es[hh][:], in1=ET[hh][:], op=ALU.add), [ET[hh], Es[hh]], [Es[hh]])
                for hh in range(2):
                    mm(PS[hh], PS[hh][:, 0:512], ones[:], Es[hh][:], True, True, [ones, Es[hh]])
                for hh in range(2):
                    r = slice(64 * hh, 64 * hh + 64)
                    V(lambda e, hh=hh, r=r: e.tensor_tensor(out=rd[hh][r, :].rearrange("p (j q) -> p j q", j=4), in0=PS[hh][r, 0:512].rearrange("p (j q) -> p j q", j=4),
                                                            in1=bc(esink[r, 4 * hh:4 * hh + 4].unsqueeze(2), [64, 4, 128]), op=ALU.add), [PS[hh], esink], [rd[hh]])
                    A(lambda e, hh=hh, r=r: e.activation(out=rd[hh][r, :], in_=rd[hh][r, :], func=AF.Ln), [rd[hh]], [rd[hh]])
                    A(lambda e, hh=hh, r=r: e.activation(out=rd[hh][r, :], in_=rd[hh][r, :], func=AF.Exp, scale=-1.0), [rd[hh]], [rd[hh]])
                    V(lambda e, hh=hh, r=r: e.tensor_tensor(out=rd[hh][r, :], in0=PS[2 + hh][r, 0:512], in1=rd[hh][r, :], op=ALU.mult), [PS[2 + hh], rd[hh]], [rd[hh]])
                    G(lambda e, hh=hh, r=r: e.tensor_tensor(out=aT[r, :, :].rearrange("p j q -> p (j q)"), in0=rd[hh][r, :],
                                                            in1=sg[r, :, :].rearrange("p j q -> p (j q)"), op=ALU.mult), [rd[hh], sg], [aT])
                att_ops = P.stop_capture()
                P.start_capture()
                for hf in range(2):
                    pz = PS[6 + hf]
                    for k in range(8):
                        mm(pz, pz[:, 0:512], Wm[:, k, 2:130], wI0[:, k, Z0 + hf * 512:Z0 + (hf + 1) * 512], k == 0, k == 7, [wI, Wn])
                    A(lambda e, hf=hf, pz=pz: e.activation(out=zs[:, hf * 512:(hf + 1) * 512], in_=pz[:, 0:512], func=AF.Silu), [pz], [zs])
                pcbs = [PS[4], PS[5]]
                for g in range(2):
                    mm(pcbs[g], pcbs[g][:, 0:128], xbcT[64 * g:64 * g + 64, 8, :], xbcT[64 * g:64 * g + 64, 9, :], True, True, [xbcT])
                for g in range(2):
                    for d in range(2):
                        V(lambda e, g=g, d=d: e.tensor_tensor(out=CBm[:, g, d, :], in0=pcbs[g][:, 0:128],
                                                              in1=(LE if d == 0 else GE)[:], op=ALU.mult), [pcbs[g], LE, GE], [CBm])
                xv = xtok[:].rearrange("p (h q) -> p h q", h=16)
                G(lambda e: e.tensor_tensor(out=yacc[:].rearrange("p (h q) -> p h q", h=16), in0=xv, in1=bc(dskipb[:].unsqueeze(2), [128, 16, 64]), op=ALU.mult),
                  [xtok, dskipb], [yacc])
                for d in range(2):
                    dir_stats(d)
                    H = Hst[0] if d == 0 else Hbin
                    A(lambda e, H=H: e.copy(out=Hin[:], in_=H[:]), [H], [Hin])
                    G(lambda e, d=d: e.tensor_tensor(out=xd[:].rearrange("p (h q) -> p h q", h=16), in0=xv,
                                                     in1=bc(dtp[:, 16 * d:16 * d + 16].unsqueeze(2), [128, 16, 64]), op=ALU.mult), [xtok, dtp], [xd])
                    tri = LE if d == 0 else GE
                    G(lambda e, d=d, tri=tri: e.tensor_tensor(out=R[:], in0=bc(tri[:].unsqueeze(1), [128, 16, 128]),
                                                              in1=bc(la[:, 16 * d:16 * d + 16].unsqueeze(2), [128, 16, 128]), op=ALU.mult), [tri, la], [R])
                    st = GT if d == 0 else LT
                    for i in range(4):
                        pb = PS[4 + i % 2]
                        mm(pb, pb[:, 0:512], st[:], R[:, 4 * i:4 * i + 4, :].rearrange("p h l -> p (h l)"), True, True, [st, R])
                        A(lambda e, i=i, pb=pb: e.activation(out=STt[:, 4 * i:4 * i + 4, :].rearrange("p h l -> p (h l)"), in_=pb[:, 0:512], func=AF.Exp), [pb], [STt])
                    for i in range(4):
                        g = i // 2
                        VG(lambda e, i=i, g=g, d=d: e.tensor_tensor(out=STt[:, 4 * i:4 * i + 4, :], in0=STt[:, 4 * i:4 * i + 4, :],
                                                                    in1=bc(CBm[:, g, d, :].unsqueeze(1), [128, 4, 128]), op=ALU.mult), [STt, CBm], [STt])
                    for g in range(2):
                        po = PS[6 + g]
                        mm(po, po[:, 0:512], xbcT[64 * g:64 * g + 64, 9, :], Hin[64 * g:64 * g + 64, :], True, True, [xbcT, Hin])
                        pdg = PS[4 + g]
                        for hl in range(8):
                            h = g * 8 + hl
                            mm(pdg, pdg[:, hl * 64:(hl + 1) * 64], STt[:, h, :], xd[:, h * 64:(h + 1) * 64], True, True, [STt, xd])
                        ya = yacc[:, g * 512:(g + 1) * 512]
                        V(lambda e, g=g, po=po: e.tensor_tensor(out=tmp[:].rearrange("p (h q) -> p h q", h=8), in0=po[:, 0:512].rearrange("p (h q) -> p h q", h=8),
                                                                in1=bc(ecum[:, 8 * g:8 * g + 8].unsqueeze(2), [128, 8, 64]), op=ALU.mult), [po, ecum], [tmp])
                        G(lambda e, ya=ya: e.tensor_tensor(out=ya, in0=ya, in1=tmp[:], op=ALU.add), [yacc, tmp], [yacc])
                        V(lambda e, ya=ya, pdg=pdg: e.tensor_tensor(out=ya, in0=ya, in1=pdg[:, 0:512], op=ALU.add), [yacc, pdg], [yacc])
                    if d == 0:
                        state_update(0, [PS[6], PS[7]])
                V(lambda e: e.tensor_tensor(out=yacc[:], in0=yacc[:], in1=zs[:], op=ALU.mult), [yacc, zs], [yacc])
                rms_stats(yacc[:], 1024, yacc, Wk["sq"], Wk["ss"], Wk["rstd"])
                V(lambda e: e.scalar_tensor_tensor(out=sn[:], in0=yacc[:], scalar=Wk["rstd"][:, 0:1], in1=gnormb[:], op0=ALU.mult, op1=ALU.mult),
                  [yacc, Wk["rstd"], gnormb], [sn])
                pt = PS[7]; ptv = pt[:].bitcast(BF16)
                for k in range(8):
                    T(lambda e, k=k: e.transpose(out=ptv[:, k * 128:(k + 1) * 128], in_=sn[:, k * 128:(k + 1) * 128], identity=idb[:]), [sn, idb], [pt])
                V(lambda e: e.tensor_copy(out=sT[:], in_=ptv.rearrange("p (k t) -> p k t", k=8)), [pt], [sT])
                ssd_ops = P.stop_capture()
                P.merge(att_ops, ssd_ops, lead_b=0.65)
                for hf in range(2):
                    po = PS[6 + hf]
                    for k in range(12):
                        lhs = aT[:, k, :] if k < 4 else sT[:, k - 4, :]
                        mm(po, po[:, 0:512], lhs, wO0[:, k, hf * 512:(hf + 1) * 512], k == 0, k == 11, [aT, sT, wO])
                    V(lambda e, hf=hf, po=po: e.tensor_tensor(out=tmp[:], in0=po[:, 0:512], in1=gateb[cond][:, hf * 512:(hf + 1) * 512], op=ALU.mult),
                      [po, gateb[cond]], [tmp])
                    G(lambda e, hf=hf: e.tensor_tensor(out=xnew[:, hf * 512:(hf + 1) * 512], in0=tmp[:], in1=xr[:, hf * 512:(hf + 1) * 512], op=ALU.add),
                      [tmp, xr], [xnew])
                if not lat:
                    P.dma(xc1_d[si, c * 128:(c + 1) * 128, :], xnew[:], reads=[xnew, xc1_tok], writes=[xc1_tok])
                else:
                    P.dma(xl1_d[ro(c) * 128:(ro(c) + 1) * 128, :], xnew[:], reads=[xnew, xl1_tok], writes=[xl1_tok])
                    norm_to_hT(xnew, 1, 1, h1T, Wk)
                    pkv = PS[0]
                    for k in range(8):
                        mm(pkv, pkv[:, 0:160], h1T[:, k, :], wKV1[:, k, :], k == 0, k == 7, [h1T, wKV1])
                    A(lambda e: e.activation(out=Wk["sq"][:, 0:128], in_=pkv[:, 0:128], func=AF.Square, accum_out=Wk["ss"][:]), [pkv], [Wk["sq"], Wk["ss"]])
                    A(lambda e: e.activation(out=Wk["rstd"][:], in_=Wk["ss"][:], func=AF.Ln, scale=1.0 / 128, bias=EPS), [Wk["ss"]], [Wk["rstd"]])
                    A(lambda e: e.activation(out=Wk["rstd"][:], in_=Wk["rstd"][:], func=AF.Exp, scale=-0.5), [Wk["rstd"]], [Wk["rstd"]])
                    V(lambda e: e.scalar_tensor_tensor(out=ckn[:], in0=pkv[:, 0:128], scalar=Wk["rstd"][:, 0:1], in1=kvnormb[:], op0=ALU.mult, op1=ALU.mult),
                      [pkv, Wk["rstd"], kvnormb], [ckn])
                    P.dma(csk[:, 0, :], c_cosK[ro(c) * 128:(ro(c) + 1) * 128, :], writes=[csk])
                    P.dma(csk[:, 1, :], c_sinK[ro(c) * 128:(ro(c) + 1) * 128, :], writes=[csk])
                    A(lambda e: e.copy(out=kp[:], in_=pkv[:, 128:160]), [pkv], [kp])
                    V(lambda e: e.tensor_tensor(out=kp1[:], in0=kp[:], in1=csk[:, 0, :], op=ALU.mult), [kp, csk], [kp1])
                    kv4 = kp[:].rearrange("p (a b i) -> p a b i", a=2, b=2)
                    k24 = kp2[:].rearrange("p (a b i) -> p a b i", a=2, b=2)
                    s4 = csk[:, 1, :].rearrange("p (a b i) -> p a b i", a=2, b=2)
                    V(lambda e: e.tensor_tensor(out=k24[:, :, 0, :], in0=kv4[:, :, 1, :], in1=s4[:, :, 0, :], op=ALU.mult), [kp, csk], [kp2])
                    V(lambda e: e.tensor_tensor(out=k24[:, :, 1, :], in0=kv4[:, :, 0, :], in1=s4[:, :, 1, :], op=ALU.mult), [kp, csk, kp2], [kp2])
                    V(lambda e: e.tensor_tensor(out=kpb[:], in0=kp1[:], in1=kp2[:], op=ALU.add), [kp1, kp2], [kpb])
                    pt3 = PS[1]; ptv3 = pt3[:].bitcast(BF16)
                    T(lambda e: e.transpose(out=ptv3[:, 0:128], in_=ckn[:], identity=idb[:]), [ckn, idb], [pt3])
                    T(lambda e: e.transpose(out=ptv3[0:32, 128:256], in_=kpb[:], identity=idb[:]), [kpb, idb], [pt3])
                    V(lambda e: e.tensor_copy(out=ckT[:], in_=ptv3[:, 0:128]), [pt3], [ckT])
                    V(lambda e: e.tensor_copy(out=kpT[:], in_=ptv3[0:32, 128:256]), [pt3], [kpT])
                    P.dma(ckvT_d[:, ro(c) * 128:(ro(c) + 1) * 128], ckT[:], reads=[ckT, kvT_tok], writes=[kvT_tok])
                    P.dma(kpeT_d[:, ro(c) * 128:(ro(c) + 1) * 128], kpT[:], reads=[kpT, kvT_tok], writes=[kvT_tok])
            if not lat:
                transpose_state_out(Hst[0], nsf[si], Wk)
            else:
                all_gather4(kv_d, kv_g, kvT_tok, kvg_tok)


        def l1_setup(ph):
            for i_ in range(NSTG):
                stg[i_] = P.sb([128, 1024], F32, f"stg{i_}", ph)
            load_w(wI, w1I, w_in1, 8, 1440)
            load_w(wI, wUQ, w_uq, 2, 1536)
            load_w(wI, wUKV3, w_ukv, 1, 2048)
            load_w(wO, wO1, w_out1, 8, 1024)
            V(lambda e: e.memset(wK[:], 0.0), [], [wO])
            ukv4 = wUKV.rearrange("p (h c) -> p h c", h=16)
            V(lambda e: e.tensor_copy(out=wK[:, :, 0:64], in_=ukv4[:, :, 0:64]), [wI], [wO])
            V(lambda e: e.tensor_copy(out=wVp[:, :, 0:64], in_=ukv4[:, 0:8, 64:128]), [wI], [wO])
            V(lambda e: e.tensor_copy(out=wVp[:, :, 64:128], in_=ukv4[:, 8:16, 64:128]), [wI], [wO])
            compute_gates(1, ph)
            P.dma(gnormb[:], fnormw.partition_broadcast(128), writes=[gnormb])

        def mla_sequence(kind, si, ph, boff=None):
            lat = kind == "lat"

            def bk(i):
                return PS[i] if boff is None else PS[boff + i % 4]
            cond = 1 if lat else 0
            ntok = 1024 if lat else 256
            nblk = ntok // 128
            nkb = 34 if lat else 2
            nkeys = nkb * 128
            TG = 512 if lat else 256
            ntg = ntok // TG
            Wk = {}

            def sbw(name, shape, dt=F32):
                Wk[name] = P.sb(shape, dt, name, ph)
                return Wk[name]
            R = sbw("R", [128, 2048])
            Wk["sq"] = R.view(R[:, 0:1024]); Wk["ht"] = R.view(R[:, 1024:2048].rearrange("p (k t) -> p k t", k=8))
            sbw("ss", [128, 1]); sbw("rstd", [128, 1]); sbw("xn", [128, 1024], BF16)
            xr = sbw("xr", [128, 1024]); xq = xr; hT1 = sbw("hT1", [128, 8, 128], BF16)
            cqn = sbw("cqn", [128, 256], BF16); cqT = sbw("cqT", [128, 2, ntok], BF16)
            sg1 = sbw("sg1", [128, 8, ntok], BF16); a1T = sg1
            ckvT = sbw("ckvT", [128, nkeys], BF16); kpeT = sbw("kpeT", [32, nkeys], BF16)
            if lat:
                KT = Vp = None
            else:
                KT = sbw("KT", [96, nkeys], BF16); Vp = sbw("Vp", [128, nkb, 128], BF16)
            qp = sbw("qp", [96, ntok], BF16) if not lat else None
            ET = [sbw(f"ET{i}", [128, TG], BF16) for i in range(6 if lat else 4)]
            rd = sbw("rd", [128, TG])
            fnormb = gnormb; qnormb = sbw("qnormb", [128, 256])
            xfin = sbw("xfin", [128, 1024]); tmp = R.view(R[:, 1024:1536]); yo = xq
            accs = xfin.view(xfin[:, 0:512])
            swp = sbw("swp", [128, 128])
            V(lambda e: e.tensor_copy(out=swp[:, 0:64], in_=ident[:, 64:128]), [ident], [swp])
            V(lambda e: e.tensor_copy(out=swp[:, 64:128], in_=ident[:, 0:64]), [ident], [swp])
            bselb = sbw("bselb", [32, 96], BF16); bself = sbw("bself", [32, 96])
            P.dma(qnormb[:], qnormw.partition_broadcast(128), writes=[qnormb])
            P.dma(bself[:], c_bsel, writes=[bself])
            V(lambda e: e.tensor_copy(out=bselb[:], in_=bself[:]), [bself], [bselb])
            if lat:
                pmm = sbw("pmm", [96, 96]); cq_ = sbw("cosq", [96, 1024]); sq_ = sbw("sinq", [96, 1024])
                uq = sbw("uq", [96, 512]); tq1 = sbw("tq1", [96, 512]); tq2 = sbw("tq2", [96, 512])
                cf = sbw("cf", [128, 2, 160]); cfb = sbw("cfb", [128, 2, 160], BF16)
                P.dma(pmm[:], c_pmm, writes=[pmm]); P.dma(cq_[:], c_cosQ, writes=[cq_]); P.dma(sq_[:], c_sinQ, writes=[sq_])
                for q in range(4):
                    P.dma(ckvT[:, q * 1024:(q + 1) * 1024], kv_g[q * 160:q * 160 + 128, :], reads=[kvg_tok], writes=[ckvT])
                    P.dma(kpeT[:, q * 1024:(q + 1) * 1024], kv_g[q * 160 + 128:(q + 1) * 160, :], reads=[kvg_tok], writes=[kpeT])
                P.dma(cf[:, :, 0:128], cckv.rearrange("(b p) c -> p b c", p=128), writes=[cf])
                P.dma(cf[:, :, 128:160], ckpe.rearrange("(b p) c -> p b c", p=128), writes=[cf])
                V(lambda e: e.tensor_copy(out=cfb[:], in_=cf[:]), [cf], [cfb])
                pt = bk(6); ptv = pt[:].bitcast(BF16)
                for b in range(2):
                    T(lambda e, b=b: e.transpose(out=ptv[:, b * 128:(b + 1) * 128], in_=cfb[:, b, 0:128], identity=idb[:]), [cfb, idb], [pt])
                    T(lambda e, b=b: e.transpose(out=ptv[0:32, 256 + b * 128:256 + (b + 1) * 128], in_=cfb[:, b, 128:160], identity=idb[:]), [cfb, idb], [pt])
                V(lambda e: e.tensor_copy(out=ckvT[:, 4096:4352], in_=ptv[:, 0:256]), [pt], [ckvT])
                V(lambda e: e.tensor_copy(out=kpeT[:, 4096:4352], in_=ptv[0:32, 256:512]), [pt], [kpeT])
            else:
                ckf = sbw("ckf", [128, 160]); ckb = sbw("ckb", [128, 160], BF16)

            def load_x(i):
                if lat:
                    P.dma(xr[:], xl1_d[i * 128:(i + 1) * 128, :], reads=[xl1_tok], writes=[xr])
                else:
                    P.dma(xr[:], xc1_d[si, i * 128:(i + 1) * 128, :], reads=[xc1_tok], writes=[xr])

            for i in range(nblk):
                tsl = slice(i * 128, (i + 1) * 128)
                load_x(i)
                norm_to_hT(xr, 1, cond, hT1, Wk, bk(7))
                ncols = 256 if lat else 416
                pp = bk(0)
                for k in range(8):
                    mm(pp, pp[:, 0:ncols], hT1[:, k, :], w1I[:, k, 0:ncols], k == 0, k == 7, [hT1, wI])
                A(lambda e: e.activation(out=Wk["sq"][:, 0:256], in_=pp[:, 0:256], func=AF.Square, accum_out=Wk["ss"][:]), [pp], [Wk["sq"], Wk["ss"]])
                A(lambda e: e.activation(out=Wk["rstd"][:], in_=Wk["ss"][:], func=AF.Ln, scale=1.0 / 256, bias=EPS), [Wk["ss"]], [Wk["rstd"]])
                A(lambda e: e.activation(out=Wk["rstd"][:], in_=Wk["rstd"][:], func=AF.Exp, scale=-0.5), [Wk["rstd"]], [Wk["rstd"]])
                V(lambda e: e.scalar_tensor_tensor(out=cqn[:], in0=pp[:, 0:256], scalar=Wk["rstd"][:, 0:1], in1=qnormb[:], op0=ALU.mult, op1=ALU.mult),
                  [pp, Wk["rstd"], qnormb], [cqn])
                pt = bk(1); ptv = pt[:].bitcast(BF16)
                for k2 in range(2):
                    T(lambda e, k2=k2: e.transpose(out=ptv[:, k2 * 128:(k2 + 1) * 128], in_=cqn[:, k2 * 128:(k2 + 1) * 128], identity=idb[:]), [cqn, idb], [pt])
                V(lambda e, tsl=tsl: e.tensor_copy(out=cqT[:, :, tsl], in_=ptv[:, 0:256].rearrange("p (k t) -> p k t", k=2)), [pt], [cqT])
                if not lat:
                    A(lambda e: e.activation(out=Wk["sq"][:, 0:128], in_=pp[:, 256:384], func=AF.Square, accum_out=Wk["ss"][:]), [pp], [Wk["sq"], Wk["ss"]])
                    A(lambda e: e.activation(out=Wk["rstd"][:], in_=Wk["ss"][:], func=AF.Ln, scale=1.0 / 128, bias=EPS), [Wk["ss"]], [Wk["rstd"]])
                    A(lambda e: e.activation(out=Wk["rstd"][:], in_=Wk["rstd"][:], func=AF.Exp, scale=-0.5), [Wk["rstd"]], [Wk["rstd"]])
                    V(lambda e: e.scalar_tensor_tensor(out=ckf[:, 0:128], in0=pp[:, 256:384], scalar=Wk["rstd"][:, 0:1], in1=kvnormb[:], op0=ALU.mult, op1=ALU.mult),
                      [pp, Wk["rstd"], kvnormb], [ckf])
                    A(lambda e: e.copy(out=ckf[:, 128:160], in_=pp[:, 384:416]), [pp], [ckf])
                    P.dma(nckv[si, tsl, :], ckf[:, 0:128], reads=[ckf])
                    P.dma(nkpe[si, tsl, :], ckf[:, 128:160], reads=[ckf])
                    V(lambda e: e.tensor_copy(out=ckb[:], in_=ckf[:]), [ckf], [ckb])
                    pt2 = bk(2); ptv2 = pt2[:].bitcast(BF16)
                    T(lambda e: e.transpose(out=ptv2[:, 0:128], in_=ckb[:, 0:128], identity=idb[:]), [ckb, idb], [pt2])
                    T(lambda e: e.transpose(out=ptv2[0:32, 128:256], in_=ckb[:, 128:160], identity=idb[:]), [ckb, idb], [pt2])
                    V(lambda e, tsl=tsl: e.tensor_copy(out=ckvT[:, tsl], in_=ptv2[:, 0:128]), [pt2], [ckvT])
                    V(lambda e, tsl=tsl: e.tensor_copy(out=kpeT[:, tsl], in_=ptv2[0:32, 128:256]), [pt2], [kpeT])
                for j in range(8):
                    pg = bk(3 + j % 2)
                    for k in range(8):
                        mm(pg, pg[:, 0:128], w1I[:, k, 416 + j * 128:416 + (j + 1) * 128], hT1[:, k, :], k == 0, k == 7, [wI, hT1])
                    A(lambda e, j=j, pg=pg, tsl=tsl: e.activation(out=sg1[:, j, tsl], in_=pg[:, 0:128], func=AF.Silu), [pg], [sg1])

            segs = [(s0, min(512, nkeys - s0)) for s0 in range(0, nkeys, 512)]
            ev = [0]

            def evac(out_ap, in_ap, rbufs, wbufs):
                ev[0] += 1
                V(lambda e: e.tensor_copy(out=out_ap, in_=in_ap), rbufs, wbufs)

            if lat:
                NHALF = 2
                hkb = nkb // NHALF
                KTh = [sbw("KTa", [128, hkb * 128], BF16), sbw("KTb", [128, hkb * 128], BF16)]
                Vph = [sbw("Vpa", [128, hkb, 128], BF16), sbw("Vpb", [128, hkb, 128], BF16)]
                qps = [sbw("qp1", [128, ntok], BF16), sbw("qp2", [128, ntok], BF16)]
                for hf_ in range(NHALF):
                    V(lambda e, hf_=hf_: e.memset(KTh[hf_][:], 0.0), [], [KTh[hf_]])
                    P.dma(KTh[hf_][64:96, :], kpeT[:, hf_ * hkb * 128:(hf_ + 1) * hkb * 128], reads=[kpeT], writes=[KTh[hf_]])
                    V(lambda e, hf_=hf_: e.memset(qps[hf_][:], 0.0), [], [qps[hf_]])
                units = [(j, hs, hf) for j in range(8) for hs in range(2) for hf in range(NHALF)]

                def expand(ui):
                    j, hs, hf = units[ui]
                    h = j + 8 * hs
                    KTu, Vpu, qpu = KTh[ui % 2], Vph[ui % 2], qps[(ui // NHALF) % 2]
                    k0 = hf * hkb * 128
                    G(lambda e, hs=hs, Vpu=Vpu: e.memset(Vpu[:, :, 64 * (1 - hs):64 * (1 - hs) + 64], 1.0), [], [Vpu])
                    for kb0 in range(0, hkb, 8):
                        nb = min(8, hkb - kb0)
                        pb = bk(6 + (kb0 // 8) % 2)
                        for b_ in range(nb):
                            kk = k0 + (kb0 + b_) * 128
                            mm(pb, pb[:, b_ * 64:(b_ + 1) * 64], ckvT[:, kk:kk + 128], wVp[:, j, 64 * hs:64 * hs + 64], True, True, [ckvT, wO])
                        evac(Vpu[:, kb0:kb0 + nb, 64 * hs:64 * hs + 64], pb[:, 0:nb * 64].rearrange("p (b c) -> p b c", c=64), [pb], [Vpu])
                    for si_, s0 in enumerate(range(0, hkb * 128, 512)):
                        sn_ = min(512, hkb * 128 - s0)
                        pb = bk(6 + si_ % 2)
                        mm(pb, pb[0:64, 0:sn_], wK[:, h, 0:64], ckvT[:, k0 + s0:k0 + s0 + sn_], True, True, [wO, ckvT])
                        evac(KTu[0:64, s0:s0 + sn_], pb[0:64, 0:sn_], [pb], [KTu])
                    if hf == 0:
                        for tg in range(ntg):
                            tt = slice(tg * TG, (tg + 1) * TG)
                            pb = bk(6 + tg % 2)
                            for k2 in range(2):
                                mm(pb, pb[0:96, 0:TG], wUQ[:, k2, 96 * h:96 * h + 96], cqT[:, k2, tt], k2 == 0, k2 == 1, [wI, cqT])
                            V(lambda e, pb=pb: e.tensor_copy(out=uq[:], in_=pb[0:96, 0:TG]), [pb], [uq])
                            pr = bk(7 - tg % 2)
                            mm(pr, pr[0:96, 0:TG], pmm[:], uq[:], True, True, [pmm, uq])
                            V(lambda e, tt=tt: e.tensor_tensor(out=tq1[:], in0=uq[:], in1=cq_[:, tt], op=ALU.mult), [uq, cq_], [tq1])
                            V(lambda e, tt=tt, pr=pr: e.tensor_tensor(out=tq2[:], in0=pr[0:96, 0:TG], in1=sq_[:, tt], op=ALU.mult), [pr, sq_], [tq2])
                            G(lambda e, tt=tt, qpu=qpu: e.tensor_tensor(out=qpu[0:96, tt], in0=tq1[:], in1=tq2[:], op=ALU.add), [tq1, tq2], [qpu])

                def attend(ui):
                    j, hs, hf = units[ui]
                    r = slice(64 * hs, 64 * hs + 64)
                    KTu, Vpu, qpu = KTh[ui % 2], Vph[ui % 2], qps[(ui // NHALF) % 2]
                    SK = 3
                    for tg in range(ntg):
                        tt = slice(tg * TG, (tg + 1) * TG)
                        pacc = bk(4 + tg)

                        def s_stage(kb):
                            bi = kb % len(ET)
                            pS = bk(kb % 4)
                            mm(pS, pS[:, 0:TG], KTu[:, kb * 128:(kb + 1) * 128], qpu[:, tt], True, True, [KTu, qpu])
                            A(lambda e, bi=bi, pS=pS: e.activation(out=ET[bi][:], in_=pS[:, 0:TG], func=AF.Exp, scale=MLA_SCALE), [pS], [ET[bi]])

                        def pv_stage(kb):
                            bi = kb % len(ET)
                            mm(pacc, pacc[:, 0:TG], Vpu[:, kb, :], ET[bi][:], hf == 0 and kb == 0, hf == NHALF - 1 and kb == hkb - 1, [Vpu, ET[bi]])
                        for kb in range(hkb + SK):
                            if kb < hkb:
                                s_stage(kb)
                            if kb >= SK:
                                pv_stage(kb - SK)
                        if hf == NHALF - 1:
                            A(lambda e, pacc=pacc: e.copy(out=accs[:, 0:TG], in_=pacc[:, 0:TG]), [pacc], [accs])
                            mm(bk(0), bk(0)[:, 0:TG], swp[:], accs[:, 0:TG], True, True, [swp, accs])
                            A(lambda e, r=r: e.activation(out=rd[r, :], in_=bk(0)[r, 0:TG], func=AF.Ln), [bk(0)], [rd])
                            A(lambda e, r=r: e.activation(out=rd[r, :], in_=rd[r, :], func=AF.Exp, scale=-1.0), [rd], [rd])
                            V(lambda e, r=r: e.tensor_tensor(out=rd[r, :], in0=accs[r, 0:TG], in1=rd[r, :], op=ALU.mult), [accs, rd], [rd])
                            G(lambda e, r=r, j=j, tt=tt: e.tensor_tensor(out=a1T[r, j, tt], in0=rd[r, :], in1=sg1[r, j, tt], op=ALU.mult), [rd, sg1], [a1T])

                expand(0)
                for ui in range(len(units)):
                    P.start_capture()
                    attend(ui)
                    sa_ = P.stop_capture()
                    sb2 = []
                    if ui + 1 < len(units):
                        P.start_capture()
                        expand(ui + 1)
                        sb2 = P.stop_capture()
                    P.merge(sa_, sb2)
            else:
                for j in range(8):
                    for hs in range(2):
                        h = j + 8 * hs
                        r = slice(64 * hs, 64 * hs + 64)
                        G(lambda e, hs=hs: e.memset(Vp[:, :, 64 * (1 - hs):64 * (1 - hs) + 64], 1.0), [], [Vp])
                        for kb0 in range(0, nkb, 8):
                            nb = min(8, nkb - kb0)
                            pb = bk(6 + (kb0 // 8) % 2)
                            for b in range(nb):
                                mm(pb, pb[:, b * 64:(b + 1) * 64], ckvT[:, (kb0 + b) * 128:(kb0 + b + 1) * 128], wVp[:, j, 64 * hs:64 * hs + 64], True, True, [ckvT, wO])
                            evac(Vp[:, kb0:kb0 + nb, 64 * hs:64 * hs + 64], pb[:, 0:nb * 64].rearrange("p (b c) -> p b c", c=64), [pb], [Vp])
                        for (s0, sn_) in segs:
                            pb = bk(6 + (s0 // 512) % 2)
                            mm(pb, pb[0:96, 0:sn_], wK[:, h, :], ckvT[:, s0:s0 + sn_], True, False, [wO, ckvT])
                            mm(pb, pb[0:96, 0:sn_], bselb[:], kpeT[:, s0:s0 + sn_], False, True, [bselb, kpeT])
                            evac(KT[:, s0:s0 + sn_], pb[0:96, 0:sn_], [pb], [KT])
                        for tg in range(ntg):
                            tt = slice(tg * TG, (tg + 1) * TG)
                            pb = bk(6 + tg % 2)
                            for k2 in range(2):
                                mm(pb, pb[0:96, 0:TG], wUQ[:, k2, 96 * h:96 * h + 96], cqT[:, k2, tt], k2 == 0, k2 == 1, [wI, cqT])
                            if lat:
                                V(lambda e, pb=pb: e.tensor_copy(out=uq[:], in_=pb[0:96, 0:TG]), [pb], [uq])
                                pr = bk(5)
                                mm(pr, pr[0:96, 0:TG], pmm[:], uq[:], True, True, [pmm, uq])
                                V(lambda e, tt=tt: e.tensor_tensor(out=tq1[:], in0=uq[:], in1=cq_[:, tt], op=ALU.mult), [uq, cq_], [tq1])
                                V(lambda e, tt=tt, pr=pr: e.tensor_tensor(out=tq2[:], in0=pr[0:96, 0:TG], in1=sq_[:, tt], op=ALU.mult), [pr, sq_], [tq2])
                                G(lambda e, tt=tt: e.tensor_tensor(out=qp[:, tt], in0=tq1[:], in1=tq2[:], op=ALU.add), [tq1, tq2], [qp])
                            else:
                                evac(qp[:, tt], pb[0:96, 0:TG], [pb], [qp])
                        for tg in range(ntg):
                            tt = slice(tg * TG, (tg + 1) * TG)
                            SK = 2

                            def s_stage(kb):
                                bi = kb % 4
                                pS = bk(bi)
                                mm(pS, pS[:, 0:TG], KT[:, kb * 128:(kb + 1) * 128], qp[:, tt], True, True, [KT, qp])
                                A(lambda e, bi=bi, pS=pS: e.activation(out=ET[bi][:], in_=pS[:, 0:TG], func=AF.Exp, scale=MLA_SCALE), [pS], [ET[bi]])

                            def pv_stage(kb):
                                bi = kb % 4
                                mm(bk(4), bk(4)[:, 0:TG], Vp[:, kb, :], ET[bi][:], kb == 0, kb == nkb - 1, [Vp, ET[bi]])
                            for kb in range(nkb + SK):
                                if kb < nkb:
                                    s_stage(kb)
                                if kb >= SK:
                                    pv_stage(kb - SK)
                            A(lambda e: e.copy(out=accs[:, 0:TG], in_=bk(4)[:, 0:TG]), [bk(4)], [accs])
                            mm(bk(5), bk(5)[:, 0:TG], swp[:], accs[:, 0:TG], True, True, [swp, accs])
                            V(lambda e, r=r: e.reciprocal(out=rd[r, :], in_=bk(5)[r, 0:TG]), [bk(5)], [rd])
                            V(lambda e, r=r: e.tensor_tensor(out=rd[r, :], in0=accs[r, 0:TG], in1=rd[r, :], op=ALU.mult), [accs, rd], [rd])
                            G(lambda e, r=r, j=j, tt=tt: e.tensor_tensor(out=a1T[r, j, tt], in0=rd[r, :], in1=sg1[r, j, tt], op=ALU.mult), [rd, sg1], [a1T])

            for i in range(nblk):
                tsl = slice(i * 128, (i + 1) * 128)
                load_x(i)
                for hf in range(2):
                    po = bk(hf)
                    for jj in range(8):
                        mm(po, po[:, 0:512], a1T[:, jj, tsl], wO1[:, jj, hf * 512:(hf + 1) * 512], jj == 0, jj == 7, [a1T, wO])
                    V(lambda e, hf=hf, po=po: e.tensor_tensor(out=tmp[:], in0=po[:, 0:512], in1=gateb[cond][:, hf * 512:(hf + 1) * 512], op=ALU.mult),
                      [po, gateb[cond]], [tmp])
                    G(lambda e, hf=hf: e.tensor_tensor(out=xfin[:, hf * 512:(hf + 1) * 512], in0=tmp[:], in1=xr[:, hf * 512:(hf + 1) * 512], op=ALU.add),
                      [tmp, xr], [xfin])
                rms_stats(xfin[:], 1024, xfin, Wk["sq"], Wk["ss"], Wk["rstd"])
                V(lambda e: e.scalar_tensor_tensor(out=yo[:], in0=xfin[:], scalar=Wk["rstd"][:, 0:1], in1=fnormb[:], op0=ALU.mult, op1=ALU.mult),
                  [xfin, Wk["rstd"], fnormb], [yo])
                dst = ys[tsl, :] if lat else yp[si, tsl, :]
                P.dma(dst, yo[:], reads=[yo])

        l0 = ExitStack()
        wdiag = P.sb([128, 10, 5, 128], BF16, "wdiag", l0)
        with ExitStack() as ph:
            compute_gates(0, ph)
            for j in range(10):
                for kk in range(5):
                    V(lambda e, j=j, kk=kk: e.tensor_scalar(out=wdiag[:, j, kk, :], in0=ident[:], scalar1=cw[:, j, kk:kk + 1], scalar2=None, op0=ALU.mult),
                      [ident, cw], [wdiag])
            P.end_phase()
        for si in range(4):
            with ExitStack() as ph:
                ab_sequence("ctx", si, ph)
                P.end_phase()
        if stop_after >= 1:
            with ExitStack() as ph:
                ab_sequence("lat", 0, ph)
                P.end_phase()
        l0.close()
        with ExitStack() as ph:
            l1_setup(ph)
            P.end_phase()
        for si in (0, 2):
            with ExitStack() as ph:
                P.start_capture()
                mla_sequence("ctx", si, ph, 0)
                sa = P.stop_capture()
                P.start_capture()
                mla_sequence("ctx", si + 1, ph, 4)
                sb_ = P.stop_capture()
                P.merge(sa, sb_)
                P.end_phase()
        if stop_after >= 1:
            with ExitStack() as ph:
                mla_sequence("lat", 0, ph)
                P.end_phase()
    return nc


def _consts():
    a = np.arange(128)
    c = {}
    c["c_ident"] = np.eye(128, dtype=np.float32)
    c["c_le"] = (a[:, None] <= a[None, :]).astype(np.float32)
    c["c_ge"] = (a[:, None] >= a[None, :]).astype(np.float32)
    c["c_gt"] = (a[:, None] > a[None, :]).astype(np.float32)
    c["c_lt"] = (a[:, None] < a[None, :]).astype(np.float32)

    def rope_tabs(length, dim):
        rows = length // 64
        row = np.repeat(np.arange(rows), 64).astype(np.float32)
        col = np.tile(np.arange(64), rows).astype(np.float32)
        nf = dim // 4
        inv = (1.0 / (10000.0 ** (np.arange(nf, dtype=np.float32) / nf))).astype(np.float32)
        ar = row[:, None] * inv[None, :]
        ac = col[:, None] * inv[None, :]
        ang = np.concatenate([ar, ar, ac, ac], axis=-1).astype(np.float32)
        sign = np.concatenate([-np.ones(nf), np.ones(nf), -np.ones(nf), np.ones(nf)]).astype(np.float32)
        perm = np.concatenate([np.arange(nf, 2 * nf), np.arange(0, nf), np.arange(3 * nf, 4 * nf), np.arange(2 * nf, 3 * nf)])
        return np.cos(ang).astype(np.float32), (np.sin(ang) * sign[None, :]).astype(np.float32), perm
    cosA, sinA, permA = rope_tabs(4096, 64)
    c["_cosA"] = np.ascontiguousarray(np.concatenate([cosA.T, cosA.T], axis=0))
    c["_sinA"] = np.ascontiguousarray(np.concatenate([sinA.T, sinA.T], axis=0))
    pm = np.zeros((128, 128), np.float32)
    for m in range(128):
        pm[(m // 64) * 64 + permA[m % 64], m] = 1.0
    c["c_pma"] = pm
    cosK, sinK, permK = rope_tabs(4096, 32)
    c["_cosK"] = cosK
    c["_sinK"] = sinK
    pmm = np.zeros((96, 96), np.float32)
    cosQ = np.ones((96, 4096), np.float32)
    sinQ = np.zeros((96, 4096), np.float32)
    cosQ[64:96] = cosK.T
    sinQ[64:96] = sinK.T
    for m in range(96):
        pmm[m if m < 64 else 64 + permK[m - 64], m] = 1.0
    c["c_pmm"] = pmm
    c["_cosQ"] = cosQ
    c["_sinQ"] = sinQ
    bs = np.zeros((32, 96), np.float32)
    bs[np.arange(32), 64 + np.arange(32)] = 1.0
    c["c_bsel"] = bs
    return c


_NC_CACHE = {}


def kernel(**inp):
    f = lambda a: np.ascontiguousarray(np.asarray(a, dtype=np.float32))
    inp = {k: f(v) for k, v in inp.items()}
    pq = np.concatenate([np.concatenate([np.arange(j * 64, (j + 1) * 64), np.arange(256 + j * 64, 256 + (j + 1) * 64)]) for j in range(4)])
    w_in0 = inp["ab_w_in"][0]
    perm0 = np.concatenate([pq, np.arange(512, 768), 768 + pq, np.arange(1280, 3616)])
    w_in0p = np.ascontiguousarray(w_in0[:, perm0])
    w_out0 = inp["ab_w_out"][0]
    w_out0p = np.ascontiguousarray(np.concatenate([w_out0[pq], w_out0[512:]], axis=0))
    pm = np.concatenate([np.concatenate([np.arange(h * 64, (h + 1) * 64), np.arange((h + 8) * 64, (h + 9) * 64)]) for h in range(8)])
    w_in1 = inp["mla_w_in"][0]
    w_in1p = np.ascontiguousarray(np.concatenate([w_in1[:, :416], w_in1[:, 416:][:, pm]], axis=1))
    w_out1p = np.ascontiguousarray(inp["mla_w_out"][0][pm])
    cst = _consts()
    shared = {
        "normwT": np.ascontiguousarray(inp["norm_w"].reshape(2, 8, 128).transpose(0, 2, 1)),
        "w_in0": w_in0p, "w_out0": w_out0p,
        "sink": inp["ab_sink"].reshape(1, 8),
        "cwT": np.ascontiguousarray(inp["ab_conv_w"][0].T.reshape(10, 128, 5).transpose(1, 0, 2)),
        "cbT": np.ascontiguousarray(inp["ab_conv_b"][0].reshape(10, 128).T),
        "dtbias": inp["ab_dt_bias"].reshape(1, 32), "alog": inp["ab_a_log"].reshape(1, 32),
        "dskip": inp["ab_d_skip"].reshape(1, 16), "gnormw": inp["ab_gnorm_w"].reshape(1, 1024),
        "w_in1": w_in1p, "qnormw": inp["mla_q_norm_w"].reshape(1, 256), "kvnormw": inp["mla_kv_norm_w"].reshape(1, 128),
        "w_uq": inp["mla_w_uq"][0], "w_ukv": inp["mla_w_ukv"][0], "w_out1": w_out1p,
        "fnormw": inp["final_norm_w"].reshape(1, 1024),
    }
    for k, v in cst.items():
        if not k.startswith("_"):
            shared[k] = v
    in_maps = []
    for i in range(8):
        b, r = i // 4, i % 4
        m = dict(shared)
        m["xp"] = inp["x_prompt"][4 * i:4 * i + 4]
        lo = 1024 * r - 128
        xs_l = np.zeros((1280, 1024), np.float32)
        a0, a1 = max(lo, 0), min(lo + 1280, 4096)
        xs_l[a0 - lo:a1 - lo] = inp["x_sample"][b, a0:a1]
        m["xs"] = xs_l
        ca = np.zeros((128, 1280), np.float32); sa = np.zeros((128, 1280), np.float32)
        ca[:, a0 - lo:a1 - lo] = cst["_cosA"][:, a0:a1]; sa[:, a0 - lo:a1 - lo] = cst["_sinA"][:, a0:a1]
        m["c_cosA"] = ca; m["c_sinA"] = sa
        m["c_cosK"] = np.ascontiguousarray(cst["_cosK"][r * 1024:(r + 1) * 1024])
        m["c_sinK"] = np.ascontiguousarray(cst["_sinK"][r * 1024:(r + 1) * 1024])
        m["fl"] = np.array([[1.0 if r > 0 else 0.0, 1.0 if r < 3 else 0.0]], np.float32)
        m["mlt"] = np.array([[1.0 if q < r else 0.0 for q in range(4)]], np.float32)
        m["mgt"] = np.array([[1.0 if q > r else 0.0 for q in range(4)]], np.float32)
        m["cak"] = inp["cache_a_k"][b, 0].reshape(256, 128)
        m["cav"] = inp["cache_a_v"][b, 0].reshape(256, 128)
        m["h0f"] = inp["state_ssd_fwd"][b, 0]
        m["h0b"] = inp["state_ssd_bwd"][b, 0]
        m["cckv"] = inp["cache_mla_ckv"][b, 0]
        m["ckpe"] = inp["cache_mla_kpe"][b, 0]
        cond = np.stack([inp["c_ctx"], inp["c"][b]], axis=1)
        c3 = np.stack([inp["c_ctx"], inp["c"][0], inp["c"][1]], axis=1)
        m["cond3"] = np.ascontiguousarray(c3.reshape(8, 128, 3).transpose(1, 0, 2))
        m["oh2"] = np.array([[1.0 if b == 0 else 0.0, 1.0 if b == 1 else 0.0]], np.float32)
        m["ada_w"] = np.ascontiguousarray(inp["ada_w"][:, :, r * 768:(r + 1) * 768])
        m["adabT"] = np.ascontiguousarray(inp["ada_b"].reshape(2, 24, 128).transpose(0, 2, 1)[:, :, 6 * r:6 * r + 6])
        m["condT"] = np.ascontiguousarray(cond.reshape(8, 128, 2).transpose(1, 0, 2))
        s = np.zeros((1, 4), np.float32)
        s[0, r] = 1.0
        m["sel"] = s
        m["c_cosQ"] = np.ascontiguousarray(cst["_cosQ"][:, r * 1024:(r + 1) * 1024])
        m["c_sinQ"] = np.ascontiguousarray(cst["_sinQ"][:, r * 1024:(r + 1) * 1024])
        in_maps.append(m)
    if "nc" not in _NC_CACHE:
        _NC_CACHE["nc"] = build_program()
    res = run_bass_kernel_spmd(_NC_CACHE["nc"], in_maps, core_ids=list(range(8)))
    R = res.results
    cat = lambda k: np.concatenate([np.asarray(R[i][k], dtype=np.float32) for i in range(8)], axis=0)
    y_prompt = cat("yp")
    y_sample = cat("ys").reshape(2, 4096, 1024)
    nk_ = cat("nk").reshape(32, 1, 256, 2, 64)
    nv_ = cat("nv").reshape(32, 1, 256, 2, 64)
    nsf_ = cat("nsf").reshape(32, 1, 16, 64, 64)
    nsb_ = cat("nsb").reshape(32, 1, 16, 64, 64)
    nckv_ = cat("nckv").reshape(32, 1, 256, 128)
    nkpe_ = cat("nkpe").reshape(32, 1, 256, 32)
    return (y_prompt, y_sample, nk_, nv_, nsf_, nsb_, nckv_, nkpe_)
```

```python
import math
import numpy as np
from contextlib import ExitStack
import concourse.bass as bass
import concourse.mybir as mybir
from concourse.bass_utils import run_bass_kernel_spmd

F32 = mybir.dt.float32
BF16 = mybir.dt.bfloat16
AF = mybir.ActivationFunctionType
ALU = mybir.AluOpType

N_DMA_SEMS = 24
import os
USE_CACHE = os.environ.get('KF_CACHE', '1') == '1'
EPS = 1e-6
Q0, K0, V0, G0, Z0, X0, DT0 = 0, 512, 640, 768, 1280, 2304, 3584
A_SCALE = 64 ** -0.5
MLA_SCALE = 96 ** -0.5


class Buf:
    __slots__ = ("name", "t", "lw", "rd", "rd_dma", "root", "excl")

    def __init__(self, name, t=None, root=None):
        self.excl = False
        self.name = name
        self.t = t
        self.lw = None
        self.rd = {}
        self.rd_dma = []
        self.root = root if root is not None else self

    def view(self, ap, name="view"):
        return Buf(name, ap, self.root)

    def __getitem__(self, k):
        return self.t[k]


class Op:
    __slots__ = ("eng", "fn", "deps", "signal", "is_dma", "dma_id", "cnt", "phase")

    def __init__(self, eng, fn, is_dma=False):
        self.eng = eng
        self.fn = fn
        self.deps = []
        self.signal = False
        self.is_dma = is_dma
        self.dma_id = None
        self.cnt = None
        self.phase = 0


class Prog:
    ENGS = ("tensor", "vector", "scalar", "gpsimd", "sync")

    def __init__(self, nc, stack):
        self.nc = nc
        self.stack = stack
        self.ops = {e: [] for e in self.ENGS}
        self.n_dma = 0
        self.nbuf = 0
        self.phase = 0
        self.cnt = {e: 0 for e in self.ENGS}
        self.seen = {e: {} for e in self.ENGS}
        self.seen_dma = {e: set() for e in self.ENGS}
        self.sems = {e: stack.enter_context(nc.semaphore(f"s_{e}")) for e in self.ENGS}
        self.dsems = [stack.enter_context(nc.semaphore(f"d_{i}")) for i in range(N_DMA_SEMS)]
        self.phase_dmas = []
        self.capture = None

    def sb(self, shape, dtype=F32, name="sb", stack=None):
        self.nbuf += 1
        t = (stack or self.stack).enter_context(self.nc.sbuf_tensor(f"{name}_{self.nbuf}", list(shape), dtype))
        return Buf(name, t)

    def ps(self, shape, dtype=F32, name="ps", stack=None):
        self.nbuf += 1
        t = (stack or self.stack).enter_context(self.nc.psum_tensor(f"{name}_{self.nbuf}", list(shape), dtype))
        b = Buf(name, t)
        b.excl = True
        return b

    def token(self, name="tok"):
        return Buf(name, None)

    def start_capture(self):
        self.capture = []

    def stop_capture(self):
        c = self.capture
        self.capture = None
        return c

    def merge(self, a, b):
        na, nb = len(a), len(b)
        i = j = 0
        while i < na or j < nb:
            if j >= nb or (i < na and i * nb <= j * na):
                o = a[i]; i += 1
            else:
                o = b[j]; j += 1
            self.ops[o.eng].append(o)

    def _add(self, op, reads, writes):
        op.phase = self.phase
        if self.capture is not None:
            self.capture.append(op)
        else:
            self.ops[op.eng].append(op)
        reads = list({id(r.root): r.root for r in reads}.values())
        writes = list({id(w.root): w.root for w in writes}.values())
        for r in reads:
            if r.excl and not any(r is w for w in writes):
                writes.append(r)
        deps = []
        for r in reads:
            if r.lw is not None:
                deps.append(r.lw)
        for w in writes:
            if w.lw is not None:
                deps.append(w.lw)
            deps.extend(w.rd.values())
            deps.extend(w.rd_dma)
        op.deps = [d for d in deps if d is not op and d.phase == self.phase
                   and not (op.eng == "tensor" and d.eng == "tensor" and not d.is_dma and not op.is_dma)]
        for r in reads:
            if any(r is w for w in writes):
                continue
            if op.is_dma:
                r.rd_dma.append(op)
            else:
                r.rd[op.eng] = op
        for w in writes:
            w.lw = op
            w.rd = {}
            w.rd_dma = []
        return op

    def op(self, eng, fn, reads=(), writes=()):
        return self._add(Op(eng, fn), list(reads), list(writes))

    def dma(self, out_ap, in_ap, reads=(), writes=(), eng="sync", **kw):
        def fn(e, out_ap=out_ap, in_ap=in_ap, kw=kw):
            return e.dma_start(out=out_ap, in_=in_ap, **kw)
        o = Op(eng, fn, is_dma=True)
        o.dma_id = self.n_dma
        self.n_dma += 1
        self.phase_dmas.append(o)
        return self._add(o, list(reads), list(writes))

    def _dma_wait(self, eng, did):
        eng.wait_ge(self.dsems[did % N_DMA_SEMS], 16 * (did // N_DMA_SEMS + 1))

    def end_phase(self):
        nc = self.nc
        ops = self.ops
        last = {}
        for e in self.ENGS:
            real = [o for o in ops[e] if not o.is_dma]
            if real:
                last[e] = real[-1]
                real[-1].signal = True
            for o in ops[e]:
                for d in o.deps:
                    if not d.is_dma:
                        d.signal = True
        for e in self.ENGS:
            for o in ops[e]:
                if o.signal and not o.is_dma:
                    self.cnt[e] += 1
                    o.cnt = self.cnt[e]
        n_dma = self.n_dma
        tail = list(range(max(0, n_dma - N_DMA_SEMS), n_dma))

        def run(ename, eng):
            seen = self.seen[ename]
            seen_dma = self.seen_dma[ename]
            for o in ops[ename]:
                if o.is_dma and o.dma_id >= N_DMA_SEMS:
                    prev = o.dma_id - N_DMA_SEMS
                    if prev not in seen_dma:
                        self._dma_wait(eng, prev)
                        seen_dma.add(prev)
                for d in o.deps:
                    if d.is_dma:
                        if d.dma_id not in seen_dma:
                            self._dma_wait(eng, d.dma_id)
                            seen_dma.add(d.dma_id)
                    elif seen.get(d.eng, 0) < d.cnt:
                        eng.wait_ge(self.sems[d.eng], d.cnt)
                        seen[d.eng] = d.cnt
                inst = o.fn(eng)
                if o.is_dma:
                    inst.then_inc(self.dsems[o.dma_id % N_DMA_SEMS], 16)
                elif o.signal:
                    inst.then_inc(self.sems[ename], 1)
            for e2, l in last.items():
                if e2 != ename and seen.get(e2, 0) < l.cnt:
                    eng.wait_ge(self.sems[e2], l.cnt)
                    seen[e2] = l.cnt
            for did in tail:
                if did not in seen_dma:
                    self._dma_wait(eng, did)
                    seen_dma.add(did)

        with nc.Block() as block:
            @block.sync
            def _(eng):
                run("sync", eng)

            @block.tensor
            def _(eng):
                run("tensor", eng)

            @block.vector
            def _(eng):
                run("vector", eng)

            @block.scalar
            def _(eng):
                run("scalar", eng)

            @block.gpsimd
            def _(eng):
                run("gpsimd", eng)
        self.ops = {e: [] for e in self.ENGS}
        self.phase_dmas = []
        self.phase += 1


def bc(ap, shape):
    return ap.to_broadcast(list(shape))


def build_program(stop_after=99):
    nc = bass.Bass("TRN2", target_bir_lowering=False)

    def din(name, shape, dt=F32):
        return nc.dram_tensor(name, list(shape), dt, kind="ExternalInput").ap()

    def dout(name, shape, dt=F32):
        return nc.dram_tensor(name, list(shape), dt, kind="ExternalOutput").ap()

    def dscr(name, shape, dt=F32):
        return nc.dram_tensor(name, list(shape), dt, kind="Internal").ap()

    xp = din("xp", [4, 256, 1024]); xs = din("xs", [1280, 1024])
    cak = din("cak", [256, 128]); cav = din("cav", [256, 128])
    h0f = din("h0f", [16, 64, 64]); h0b = din("h0b", [16, 64, 64])
    cckv = din("cckv", [256, 128]); ckpe = din("ckpe", [256, 32])
    condT = din("condT", [128, 8, 2]); sel = din("sel", [1, 4])
    fl = din("fl", [1, 2]); mlt = din("mlt", [1, 4]); mgt = din("mgt", [1, 4])
    ada_w = din("ada_w", [2, 1024, 768]); adabT = din("adabT", [2, 128, 6])
    cond3 = din("cond3", [128, 8, 3]); oh2 = din("oh2", [1, 2])
    mod_src = dscr("mod_src", [128, 64]); mod_dst = dscr("mod_dst", [512, 64]); mod_tok = Buf("mod_tok"); mod_tok2 = Buf("mod_tok2")
    normwT = din("normwT", [2, 128, 8])
    w_in0 = din("w_in0", [1024, 3616]); w_out0 = din("w_out0", [1536, 1024])
    sink = din("sink", [1, 8]); cwT = din("cwT", [128, 10, 5]); cbT = din("cbT", [128, 10])
    dtbias = din("dtbias", [1, 32]); alog = din("alog", [1, 32]); dskip = din("dskip", [1, 16])
    gnormw = din("gnormw", [1, 1024])
    w_in1 = din("w_in1", [1024, 1440]); qnormw = din("qnormw", [1, 256]); kvnormw = din("kvnormw", [1, 128])
    w_uq = din("w_uq", [256, 1536]); w_ukv = din("w_ukv", [128, 2048]); w_out1 = din("w_out1", [1024, 1024])
    fnormw = din("fnormw", [1, 1024])
    c_ident = din("c_ident", [128, 128]); c_le = din("c_le", [128, 128]); c_ge = din("c_ge", [128, 128])
    c_gt = din("c_gt", [128, 128]); c_lt = din("c_lt", [128, 128]); c_pma = din("c_pma", [128, 128])
    c_cosA = din("c_cosA", [128, 1280]); c_sinA = din("c_sinA", [128, 1280])
    c_cosK = din("c_cosK", [1024, 32]); c_sinK = din("c_sinK", [1024, 32])
    c_pmm = din("c_pmm", [96, 96]); c_cosQ = din("c_cosQ", [96, 1024]); c_sinQ = din("c_sinQ", [96, 1024])
    c_bsel = din("c_bsel", [32, 96])

    yp = dout("yp", [4, 256, 1024]); ys = dout("ys", [1024, 1024])
    nk = dout("nk", [4, 256, 128]); nv = dout("nv", [4, 256, 128])
    nsf = dout("nsf", [4, 16, 64, 64]); nsb = dout("nsb", [4, 16, 64, 64])
    nckv = dout("nckv", [4, 256, 128]); nkpe = dout("nkpe", [4, 256, 32])

    hT_d = dscr("hT_d", [128, 8, 4100], BF16)
    Hb_d = dscr("Hb_d", [32, 128, 512])
    xl1_d = dscr("xl1_d", [1024, 1024]); xc1_d = dscr("xc1_d", [4, 256, 1024])
    kv_d = dscr("kv_d", [160, 1024], BF16); ckvT_d = kv_d[0:128, :]; kpeT_d = kv_d[128:160, :]
    kv_g = dscr("kv_g", [640, 1024], BF16)
    ex_src = dscr("ex_src", [128, 1056]); ex_dst = dscr("ex_dst", [512, 1056])
    ex_tok = Buf("ex_tok"); ex_tok2 = Buf("ex_tok2"); kvg_tok = Buf("kvg_tok")
    hTd_tok = Buf("hTd_tok"); Hbd_tok = Buf("Hbd_tok"); xl1_tok = Buf("xl1_tok"); xc1_tok = Buf("xc1_tok")
    kvT_tok = Buf("kvT_tok")

    with ExitStack() as st:
        P = Prog(nc, st)
        rr = [0]

        def V(fn, r, w): return P.op("vector", fn, r, w)
        def A(fn, r, w): return P.op("scalar", fn, r, w)
        def G(fn, r, w): return P.op("gpsimd", fn, r, w)
        def T(fn, r, w): return P.op("tensor", fn, r, w)

        def VG(fn, r, w):
            rr[0] += 1
            return P.op("vector" if rr[0] % 2 else "gpsimd", fn, r, w)

        def mm(ps_buf, out_ap, lhsT, rhs, start, stop, reads):
            return T(lambda e: e.matmul(out_ap, lhsT=lhsT, rhs=rhs, start=start, stop=stop), reads, [ps_buf])

        wI = P.sb([128, 28928], BF16, "wI")
        wO = P.sb([128, 12288], BF16, "wO")
        wKV1 = P.sb([128, 8, 160], BF16, "wKV1")
        NSTG = 6
        stg = [None] * NSTG
        ident = P.sb([128, 128], F32, "ident"); idb = P.sb([128, 128], BF16, "idb")
        LE = P.sb([128, 128], F32, "LE"); GE = P.sb([128, 128], F32, "GE")
        GT = P.sb([128, 128], F32, "GT"); LT = P.sb([128, 128], F32, "LT")
        ones = P.sb([128, 128], F32, "ones"); onesb = P.sb([128, 128], BF16, "onesb")
        pma = P.sb([128, 128], F32, "pma")
        mvec = P.sb([128, 2, 24, 2], F32, "mvec")
        modA = P.sb([128, 2, 2, 8], F32, "modA")
        modB = P.sb([128, 2, 2, 8], F32, "modB")
        nwT = P.sb([128, 2, 8], F32, "nwT"); abT = P.sb([128, 2, 6], F32, "abT")
        gateb = [P.sb([128, 1024], F32, f"gateb{c}") for c in range(2)]
        gnormb = P.sb([128, 1024], F32, "gnormb")
        kvnormb = P.sb([128, 128], F32, "kvnormb")
        dskipb = P.sb([128, 16], F32, "dskipb"); dtbiasb = P.sb([128, 32], F32, "dtbiasb")
        arow = P.sb([128, 32], F32, "arow"); esink = P.sb([128, 8], F32, "esink")
        cw = P.sb([128, 10, 5], F32, "cw"); cb = P.sb([128, 10], F32, "cb")
        selb = P.sb([128, 4], F32, "selb")
        flb = P.sb([128, 2], F32, "flb"); mltb = P.sb([128, 4], F32, "mltb"); mgtb = P.sb([128, 4], F32, "mgtb")
        ccdummy = P.sb([128, 1], F32, "ccdummy")
        ccsem = st.enter_context(nc.semaphore("ccsem"))
        cc_count = [0]

        def all_gather4(src_ap, dst_ap, rtok, wtok, groups=((0, 1, 2, 3), (4, 5, 6, 7))):
            def fn(e):
                e.collective_compute("AllGather", op=ALU.bypass, replica_groups=[list(g) for g in groups],
                                     ins=[src_ap.opt()], outs=[dst_ap.opt()]).then_inc(ccsem)
                cc_count[0] += 1
                e.wait_ge(ccsem, cc_count[0])
                return e.memset(ccdummy[:], 0.0)
            P.op("gpsimd", fn, [rtok], [wtok, ccdummy])
        sc = P.sb([128, 8, 2], F32, "sc")
        PS = [P.ps([128, 512], F32, f"bank{i}") for i in range(8)]

        def load_const(buf, src, bcast=False):
            P.dma(buf[:], src.partition_broadcast(128) if bcast else src, writes=[buf])

        load_const(ident, c_ident); load_const(LE, c_le); load_const(GE, c_ge); load_const(GT, c_gt)
        load_const(LT, c_lt); load_const(pma, c_pma)
        V(lambda e: e.tensor_copy(out=idb[:], in_=ident[:]), [ident], [idb])
        V(lambda e: e.memset(ones[:], 1.0), [], [ones])
        V(lambda e: e.memset(onesb[:], 1.0), [], [onesb])
        load_const(nwT, normwT.rearrange("l p k -> p l k")); load_const(abT, adabT.rearrange("l p k -> p l k"))
        load_const(gnormb, gnormw, True)
        load_const(kvnormb, kvnormw, True)
        load_const(dskipb, dskip, True); load_const(dtbiasb, dtbias, True)
        load_const(arow, alog, True); load_const(esink, sink, True)
        load_const(cw, cwT); load_const(cb, cbT); load_const(selb, sel, True)
        load_const(flb, fl, True); load_const(mltb, mlt, True); load_const(mgtb, mgt, True)
        load_const(sc, condT)
        A(lambda e: e.activation(out=arow[:], in_=arow[:], func=AF.Exp), [arow], [arow])
        V(lambda e: e.tensor_scalar(out=arow[:], in0=arow[:], scalar1=-1.0, scalar2=None, op0=ALU.mult), [arow], [arow])
        A(lambda e: e.activation(out=esink[:], in_=esink[:], func=AF.Exp), [esink], [esink])
        A(lambda e: e.activation(out=sc[:], in_=sc[:], func=AF.Silu), [sc], [sc])

        def load_w(dst_buf, dst_view, src, kc, n):
            i = 0
            for k in range(kc):
                for c0 in range(0, n, 1024):
                    w = min(1024, n - c0)
                    s = stg[i % NSTG]
                    P.dma(s[:, 0:w], src[k * 128:(k + 1) * 128, c0:c0 + w], writes=[s])
                    eng = ("vector", "gpsimd", "scalar")[i % 3]
                    if eng == "scalar":
                        A(lambda e, s=s, k=k, c0=c0, w=w: e.copy(out=dst_view[:, k, c0:c0 + w], in_=s[:, 0:w]), [s], [dst_buf])
                    else:
                        P.op(eng, lambda e, s=s, k=k, c0=c0, w=w: e.tensor_copy(out=dst_view[:, k, c0:c0 + w], in_=s[:, 0:w]), [s], [dst_buf])
                    i += 1

        wI0 = wI[:, 0:28928].rearrange("p (k n) -> p k n", k=8)
        wO0 = wO[:, 0:12288].rearrange("p (k n) -> p k n", k=12)
        w1I = wI[:, 0:11520].rearrange("p (k n) -> p k n", k=8)
        wUQ = wI[:, 11520:14592].rearrange("p (k n) -> p k n", k=2)
        wUKV = wI[:, 14592:16640]
        wUKV3 = wI[:, 14592:16640].rearrange("p (k n) -> p k n", k=1)
        wK = wO[:, 8192:9728].rearrange("p (h c) -> p h c", h=16)
        wVp = wO[:, 9728:10752].rearrange("p (j c) -> p j c", j=8)
        wO1 = wO[:, 0:8192].rearrange("p (k n) -> p k n", k=8)

        with ExitStack() as ph:
            for i_ in range(NSTG):
                stg[i_] = P.sb([128, 1024], F32, f"stg{i_}", ph)
            aw = [P.sb([128, 8, 768], F32, f"aw{i}", ph) for i in range(2)]
            sc3 = P.sb([128, 8, 3], F32, "sc3", ph); mloc = P.sb([128, 2, 6, 3], F32, "mloc", ph)
            mall = P.sb([128, 4, 64], F32, "mall", ph); mpk = P.sb([128, 64], F32, "mpk", ph); ohb = P.sb([128, 2], F32, "ohb", ph); mt = P.sb([128, 4, 6], F32, "mt", ph)
            P.dma(sc3[:], cond3, writes=[sc3])
            P.dma(ohb[:], oh2.partition_broadcast(128), writes=[ohb])
            A(lambda e: e.activation(out=sc3[:], in_=sc3[:], func=AF.Silu), [sc3], [sc3])
            for l in range(2):
                a = aw[l]
                P.dma(a[:], ada_w[l].rearrange("(k p) n -> p k n", p=128), writes=[a])
                for j in range(6):
                    pb = PS[j % 2]
                    for k in range(8):
                        mm(pb, pb[:, 0:3], a[:, k, j * 128:(j + 1) * 128], sc3[:, k, :], k == 0, k == 7, [a, sc3])
                    A(lambda e, pb=pb, l=l, j=j: e.activation(out=mloc[:, l, j, :], in_=pb[:, 0:3], func=AF.Identity,
                                                              bias=abT[:, l, j:j + 1], scale=1.0), [pb, abT], [mloc])
            V(lambda e: e.memset(mpk[:], 0.0), [], [mpk])
            V(lambda e: e.tensor_copy(out=mpk[:, 0:36], in_=mloc[:].rearrange("p l j c -> p (l j c)")), [mloc], [mpk])
            P.dma(mod_src, mpk[:], reads=[mpk, mod_tok], writes=[mod_tok])
            all_gather4(mod_src, mod_dst, mod_tok, mod_tok2)
            P.dma(mall[:], mod_dst.rearrange("(r p) c -> p r c", p=128), reads=[mod_tok2], writes=[mall])
            m5 = mall[:, :, 0:36].rearrange("p r (l j c) -> p r l j c", l=2, j=6)
            for l in range(2):
                mv = mvec[:, l, :, :].rearrange("p (r j) c -> p r j c", j=6)
                V(lambda e, l=l, mv=mv: e.tensor_copy(out=mv[:, :, :, 0], in_=m5[:, :, l, :, 0]), [mall], [mvec])
                V(lambda e, l=l: e.tensor_scalar(out=mt[:], in0=m5[:, :, l, :, 1], scalar1=ohb[:, 0:1], scalar2=None, op0=ALU.mult), [mall, ohb], [mt])
                V(lambda e, l=l, mv=mv: e.scalar_tensor_tensor(out=mv[:, :, :, 1], in0=m5[:, :, l, :, 2], scalar=ohb[:, 1:2], in1=mt[:],
                                                               op0=ALU.mult, op1=ALU.add), [mall, ohb, mt], [mvec])
                for c in range(2):
                    V(lambda e, l=l, c=c: e.scalar_tensor_tensor(out=modA[:, l, c, :], in0=mvec[:, l, 8:16, c], scalar=1.0,
                                                                 in1=nwT[:, l, :], op0=ALU.add, op1=ALU.mult), [mvec, nwT], [modA])
                    V(lambda e, l=l, c=c: e.tensor_copy(out=modB[:, l, c, :], in_=mvec[:, l, 0:8, c]), [mvec], [modB])
            load_w(wI, wI0, w_in0, 8, 3616)
            load_w(wO, wO0, w_out0, 12, 1024)
            load_w(wKV1, wKV1[:], w_in1[:, 256:416], 8, 160)
            P.end_phase()

        def compute_gates(l, ph):
            dg = [P.sb([128, 128], F32, f"dg{i}", ph) for i in range(2)]
            for c in range(2):
                for hf in range(2):
                    pb = PS[2 * c + hf]
                    for k4 in range(4):
                        k = hf * 4 + k4
                        d = dg[k % 2]
                        V(lambda e, d=d, k=k, c=c: e.tensor_scalar(out=d[:], in0=ident[:], scalar1=mvec[:, l, 16 + k, c:c + 1], scalar2=None, op0=ALU.mult),
                          [ident, mvec], [d])
                        mm(pb, pb[:, k4 * 128:(k4 + 1) * 128], ones[:], d[:], True, True, [ones, d])
                    V(lambda e, pb=pb, c=c, hf=hf: e.tensor_copy(out=gateb[c][:, hf * 512:(hf + 1) * 512], in_=pb[:, 0:512]), [pb], [gateb[c]])

        def rms_stats(x_ap, n, xbuf, sq, ss, rstd):
            A(lambda e: e.activation(out=sq[:, 0:n], in_=x_ap, func=AF.Square, accum_out=ss[:]), [xbuf], [sq, ss])
            A(lambda e: e.activation(out=rstd[:], in_=ss[:], func=AF.Ln, scale=1.0 / n, bias=EPS), [ss], [rstd])
            A(lambda e: e.activation(out=rstd[:], in_=rstd[:], func=AF.Exp, scale=-0.5), [rstd], [rstd])

        def norm_to_hT(xr, l, c, hT, W, pt=None):
            rms_stats(xr[:], 1024, xr, W["sq"], W["ss"], W["rstd"])
            V(lambda e: e.tensor_scalar(out=W["xn"][:], in0=xr[:], scalar1=W["rstd"][:, 0:1], scalar2=None, op0=ALU.mult),
              [xr, W["rstd"]], [W["xn"]])
            pt = pt if pt is not None else PS[7]
            ptv = pt[:].bitcast(BF16)
            for k in range(8):
                T(lambda e, k=k: e.transpose(out=ptv[:, k * 128:(k + 1) * 128], in_=W["xn"][:, k * 128:(k + 1) * 128], identity=idb[:]),
                  [W["xn"], idb], [pt])
            V(lambda e: e.tensor_tensor(out=W["ht"][:], in0=ptv.rearrange("p (k t) -> p k t", k=8),
                                        in1=bc(modA[:, l, c, :].unsqueeze(2), [128, 8, 128]), op=ALU.mult), [pt, modA], [W["ht"]])
            V(lambda e: e.tensor_tensor(out=hT[:], in0=W["ht"][:], in1=bc(modB[:, l, c, :].unsqueeze(2), [128, 8, 128]), op=ALU.add),
              [W["ht"], modB], [hT])

        def transpose_state_out(Hst, dst, W):
            o = W["stout"]
            for g in range(2):
                for q4 in range(2):
                    pb = PS[(g * 2 + q4) % 4]
                    for i in range(4):
                        hl = q4 * 4 + i
                        T(lambda e, g=g, hl=hl, i=i, pb=pb: e.transpose(out=pb[0:64, i * 64:(i + 1) * 64], in_=Hst[64 * g:64 * g + 64, hl * 64:(hl + 1) * 64],
                                                                    identity=ident[64 * g:64 * g + 64, 64 * g:64 * g + 64]), [Hst, ident], [pb])
                    V(lambda e, g=g, q4=q4, pb=pb: e.tensor_copy(out=o[:, g * 8 + q4 * 4:g * 8 + q4 * 4 + 4, :],
                                                                in_=pb[0:64, 0:256].rearrange("p (h n) -> p h n", h=4)), [pb], [o])
            P.dma(dst.rearrange("h p n -> p h n"), o[:], reads=[o])

        def ab_sequence(kind, si, ph):
            lat = kind == "lat"
            cond = 1 if lat else 0
            L = 1280 if lat else 256
            nch = L // 128
            own = list(range(1, 9)) if lat else list(range(nch))

            def ro(c):
                return c - 1 if lat else c
            x_src = xs if lat else xp[si]
            Wk = {}
            G = V
            VG = V
            res_d = xl1_d if lat else xc1_d[si]
            fr_toks = {}

            def frt(c):
                if c not in fr_toks:
                    fr_toks[c] = Buf(f"frtok{c}")
                return fr_toks[c]

            def fr_x(c): return res_d[ro(c) * 128:(ro(c) + 1) * 128, :].bitcast(BF16)[:, 0:1024]
            def fr_b(c): return res_d[ro(c) * 128:(ro(c) + 1) * 128, :].bitcast(BF16)[:, 1024:1152]
            def fr_bc(c): return res_d[ro(c) * 128:(ro(c) + 1) * 128, :].bitcast(BF16)[:, 1152:1408].rearrange("p (a b) -> p a b", a=2)
            def fr_dt(c): return res_d[ro(c) * 128:(ro(c) + 1) * 128, 704:768].rearrange("p (a b) -> p a b", a=2)

            def sbw(name, shape, dt=F32):
                Wk[name] = P.sb(shape, dt, name, ph)
                return Wk[name]
            R = sbw("R", [128, 16, 128])
            Rf = R[:].rearrange("p h l -> p (h l)")
            Wk["sq"] = R.view(Rf[:, 0:1024]); Wk["ht"] = R.view(Rf[:, 1024:2048].rearrange("p (k t) -> p k t", k=8))
            sbw("ss", [128, 1]); sbw("rstd", [128, 1]); sbw("xn", [128, 1024], BF16)
            sbw("stout", [64, 16, 64])
            xr = sbw("xr", [128, 1024]); hT = sbw("hT", [128, 8, 128], BF16)
            nkb = nch + (2 if lat else 0)
            kT = sbw("kT", [128, nkb * 128], BF16); Vt = sbw("Vt", [128, nkb, 128], BF16)
            zt = sbw("zt", [128, 8, 4], BF16)
            uT = sbw("uT", [128, 128]); cs = sbw("cs", [128, 2, 128]); t1 = sbw("t1", [128, 128]); t2 = sbw("t2", [128, 128])
            kvo = sbw("kvo", [128, 2, 128])

            V(lambda e: e.memset(zt[:], 0.0), [], [zt])
            P.dma(hT_d[:, :, 0:2], zt[:, :, 0:2], reads=[zt, hTd_tok], writes=[hTd_tok])
            P.dma(hT_d[:, :, L + 2:L + 4], zt[:, :, 2:4], reads=[zt, hTd_tok], writes=[hTd_tok])
            if lat:
                cf = sbw("cf", [128, 2, 2, 128])
                P.dma(cf[:, 0, :, :], cak.rearrange("(b p) c -> p b c", p=128), writes=[cf])
                P.dma(cf[:, 1, :, :], cav.rearrange("(b p) c -> p b c", p=128), writes=[cf])
                cfb = sbw("cfb", [128, 2, 128], BF16)
                V(lambda e: e.tensor_copy(out=cfb[:], in_=cf[:, 0, :, :]), [cf], [cfb])
                V(lambda e: e.tensor_copy(out=Vt[:, nch:nch + 2, :], in_=cf[:, 1, :, :]), [cf], [Vt])
                pt = PS[6]; ptv = pt[:].bitcast(BF16)
                for b in range(2):
                    T(lambda e, b=b: e.transpose(out=ptv[:, b * 128:(b + 1) * 128], in_=cfb[:, b, :], identity=idb[:]), [cfb, idb], [pt])
                V(lambda e: e.tensor_copy(out=kT[:, L:L + 256], in_=ptv[:, 0:256]), [pt], [kT])
            for c in range(nch):
                P.dma(xr[:], x_src[c * 128:(c + 1) * 128, :], writes=[xr])
                norm_to_hT(xr, 0, cond, hT, Wk)
                if lat and c in (0, nch - 1):
                    fi = 0 if c == 0 else 1
                    V(lambda e, fi=fi: e.tensor_scalar(out=hT[:], in0=hT[:], scalar1=flb[:, fi:fi + 1], scalar2=None, op0=ALU.mult), [hT, flb], [hT])
                P.dma(hT_d[:, :, 2 + c * 128:2 + (c + 1) * 128], hT[:], reads=[hT, hTd_tok], writes=[hTd_tok])
                pk = PS[0]; pv = PS[1]
                for k in range(8):
                    mm(pk, pk[:, 0:128], wI0[:, k, K0:K0 + 128], hT[:, k, :], k == 0, k == 7, [wI, hT])
                for k in range(8):
                    mm(pv, pv[:, 0:128], hT[:, k, :], wI0[:, k, V0:V0 + 128], k == 0, k == 7, [wI, hT])
                if lat:
                    P.dma(cs[:, 0, :], c_cosA[:, c * 128:(c + 1) * 128], writes=[cs])
                    P.dma(cs[:, 1, :], c_sinA[:, c * 128:(c + 1) * 128], writes=[cs])
                    A(lambda e: e.copy(out=uT[:], in_=pk[:, 0:128]), [pk], [uT])
                    pr = PS[2]
                    mm(pr, pr[:, 0:128], pma[:], uT[:], True, True, [pma, uT])
                    V(lambda e: e.tensor_tensor(out=t1[:], in0=uT[:], in1=cs[:, 0, :], op=ALU.mult), [uT, cs], [t1])
                    V(lambda e: e.tensor_tensor(out=t2[:], in0=pr[:, 0:128], in1=cs[:, 1, :], op=ALU.mult), [pr, cs], [t2])
                    G(lambda e, c=c: e.tensor_tensor(out=kT[:, c * 128:(c + 1) * 128], in0=t1[:], in1=t2[:], op=ALU.add), [t1, t2], [kT])
                    V(lambda e, c=c: e.tensor_copy(out=Vt[:, c, :], in_=pv[:, 0:128]), [pv], [Vt])
                else:
                    A(lambda e, c=c: e.copy(out=kT[:, c * 128:(c + 1) * 128], in_=pk[:, 0:128]), [pk], [kT])
                    V(lambda e, c=c: e.tensor_copy(out=Vt[:, c, :], in_=pv[:, 0:128]), [pv], [Vt])
                    V(lambda e: e.tensor_copy(out=kvo[:, 1, :], in_=pv[:, 0:128]), [pv], [kvo])
                    pk2 = PS[2]
                    for k in range(8):
                        mm(pk2, pk2[:, 0:128], hT[:, k, :], wI0[:, k, K0:K0 + 128], k == 0, k == 7, [wI, hT])
                    A(lambda e: e.copy(out=kvo[:, 0, :], in_=pk2[:, 0:128]), [pk2], [kvo])
                    P.dma(nk[si, c * 128:(c + 1) * 128, :], kvo[:, 0, :], reads=[kvo])
                    P.dma(nv[si, c * 128:(c + 1) * 128, :], kvo[:, 1, :], reads=[kvo])

            Wn = sbw("Wn", [128, 8, 132], BF16)
            raw = sbw("rawb", [128, 10, 132], BF16)
            xbcT = sbw("xbcT", [128, 10, 128], BF16)
            xtok = sbw("xtok", [128, 1024], BF16); Btok = sbw("Btok", [128, 128], BF16)
            dtr = sbw("dtr", [128, 32]); dtl = sbw("dtl", [128, 2, 32])
            dtp = dtl.view(dtl[:, 0, :]); la = dtl.view(dtl[:, 1, :])
            tot = sbw("tot", [128, 16]); tmc = sbw("tmc", [128, 16]); wst = sbw("wst", [128, 16])
            dec = sbw("dec", [128, 16]); ecum = sbw("ecum", [128, 16]); dw = sbw("dw", [128, 16])
            xd = sbw("xd", [128, 1024], BF16); xdw = sbw("xdw", [128, 1024], BF16)
            Hst = [sbw(f"Hst{d}", [128, 512]) for d in range(2)]

            def front(c, nxc):
                P.dma(Wn[:], hT_d[:, :, c * 128:c * 128 + 132], reads=[hTd_tok], writes=[Wn])
                for j in range(nxc):
                    pb = PS[j % 4]
                    col = X0 + (j if nxc == 10 else j) * 128
                    for k in range(8):
                        mm(pb, pb[:, 0:132], wI0[:, k, col:col + 128], Wn[:, k, :], k == 0, k == 7, [wI, Wn])
                    if j % 2:
                        A(lambda e, j=j, pb=pb: e.copy(out=raw[:, j, :], in_=pb[:, 0:132]), [pb], [raw])
                    else:
                        V(lambda e, j=j, pb=pb: e.tensor_copy(out=raw[:, j, :], in_=pb[:, 0:132]), [pb], [raw])
                for j in range(nxc):
                    pb = PS[j % 4]
                    for kk in range(5):
                        mm(pb, pb[:, 0:128], wdiag[:, j, kk, :], raw[:, j, kk:kk + 128], kk == 0, kk == 4, [wdiag, raw])
                    A(lambda e, j=j, pb=pb: e.activation(out=xbcT[:, j, :], in_=pb[:, 0:128], func=AF.Silu, bias=cb[:, j:j + 1], scale=1.0), [pb, cb], [xbcT])
                pd = PS[4]
                for k in range(8):
                    mm(pd, pd[:, 0:32], Wn[:, k, 2:130], wI0[:, k, DT0:DT0 + 32], k == 0, k == 7, [wI, Wn])
                V(lambda e: e.tensor_tensor(out=dtr[:], in0=pd[:, 0:32], in1=dtbiasb[:], op=ALU.add), [pd, dtbiasb], [dtr])
                A(lambda e: e.activation(out=dtr[:], in_=dtr[:], func=AF.Exp), [dtr], [dtr])
                A(lambda e: e.activation(out=dtp[:], in_=dtr[:], func=AF.Ln, bias=1.0, scale=1.0), [dtr], [dtp])
                V(lambda e: e.tensor_tensor(out=la[:], in0=dtp[:], in1=arow[:], op=ALU.mult), [dtp, arow], [la])
                pt = PS[5]; ptv = pt[:].bitcast(BF16)
                for j in range(8):
                    T(lambda e, j=j: e.transpose(out=ptv[:, j * 128:(j + 1) * 128], in_=xbcT[:, j, :], identity=idb[:]), [xbcT, idb], [pt])
                V(lambda e: e.tensor_copy(out=xtok[:], in_=ptv), [pt], [xtok])
                pt2 = PS[6]; ptv2 = pt2[:].bitcast(BF16)
                T(lambda e: e.transpose(out=ptv2[:, 0:128], in_=xbcT[:, 8, :], identity=idb[:]), [xbcT, idb], [pt2])
                A(lambda e: e.copy(out=Btok[:], in_=ptv2[:, 0:128]), [pt2], [Btok])

            def dir_stats(d):
                pb = PS[6]
                ld = la[:, 16 * d:16 * d + 16]
                mm(pb, pb[:, 0:16], (LE if d == 0 else GE)[:], ld, True, True, [LE, GE, la])
                mm(pb, pb[:, 16:32], ones[:], ld, True, True, [ones, la])
                A(lambda e: e.activation(out=ecum[:], in_=pb[:, 0:16], func=AF.Exp), [pb], [ecum])
                V(lambda e: e.tensor_copy(out=tot[:], in_=pb[:, 16:32]), [pb], [tot])
                V(lambda e: e.tensor_tensor(out=tmc[:], in0=tot[:], in1=pb[:, 0:16], op=ALU.subtract), [tot, pb], [tmc])
                A(lambda e: e.activation(out=wst[:], in_=tmc[:], func=AF.Exp), [tmc], [wst])
                A(lambda e: e.activation(out=dec[:], in_=tot[:], func=AF.Exp), [tot], [dec])
                V(lambda e: e.tensor_tensor(out=dw[:], in0=dtp[:, 16 * d:16 * d + 16], in1=wst[:], op=ALU.mult), [dtp, wst], [dw])
                xv = xtok[:].rearrange("p (h q) -> p h q", h=16)
                G(lambda e: e.tensor_tensor(out=xdw[:].rearrange("p (h q) -> p h q", h=16), in0=xv, in1=bc(dw[:].unsqueeze(2), [128, 16, 64]), op=ALU.mult),
                  [xtok, dw], [xdw])

            def state_update(d, psS):
                H = Hst[d]
                for g in range(2):
                    pb = psS[g]
                    mm(pb, pb[:, 0:512], Btok[:], xdw[:, g * 512:(g + 1) * 512], True, True, [Btok, xdw])
                for g in range(2):
                    pb = psS[g]
                    r = slice(64 * g, 64 * g + 64)
                    G(lambda e, r=r, g=g: e.tensor_tensor(out=H[r, :].rearrange("p (h q) -> p h q", h=8), in0=H[r, :].rearrange("p (h q) -> p h q", h=8),
                                                          in1=bc(dec[r, 8 * g:8 * g + 8].unsqueeze(2), [64, 8, 64]), op=ALU.mult), [H, dec], [H])
                    V(lambda e, r=r, pb=pb: e.tensor_tensor(out=H[r, :], in0=H[r, :], in1=pb[r, 0:512], op=ALU.add), [H, pb], [H])

            def load_h0(src, H):
                o = Wk["stout"]
                P.dma(o[:], src.rearrange("h p n -> p h n"), writes=[o])
                V(lambda e: e.memset(R[:], 0.0), [], [R])
                for g in range(2):
                    V(lambda e, g=g: e.tensor_copy(out=R[0:64, 8 * g:8 * g + 8, 64 * g:64 * g + 64], in_=o[:, 8 * g:8 * g + 8, :]), [o], [R])
                for g in range(2):
                    for q4 in range(2):
                        pb = PS[(g * 2 + q4) % 4]
                        for i in range(4):
                            h = g * 8 + q4 * 4 + i
                            mm(pb, pb[:, i * 64:(i + 1) * 64], R[0:64, h, :], ident[0:64, 0:64], True, True, [R, ident])
                        V(lambda e, g=g, q4=q4, pb=pb: e.tensor_copy(out=H[64 * g:64 * g + 64, q4 * 256:(q4 + 1) * 256],
                                                                    in_=pb[64 * g:64 * g + 64, 0:256]), [pb], [H])

            V(lambda e: e.memset(Hst[1][:], 0.0), [], [Hst[1]])
            if lat:
                cdbT = sbw("cdbT", [128, 8, 16]); runb = sbw("runb", [128, 16]); runf = sbw("runf", [128, 16])
                dq = sbw("dq", [128, 16]); coef = sbw("coef", [128, 16]); HinB = sbw("HinB", [128, 512])
                V(lambda e: e.memset(runb[:], 1.0), [], [runb])
                V(lambda e: e.memset(runf[:], 1.0), [], [runf])
            for c in reversed(own):
                P.dma(Hb_d[ro(c)], Hst[1][:], reads=[Hst[1], Hbd_tok], writes=[Hbd_tok])
                if lat:
                    V(lambda e, c=c: e.tensor_copy(out=cdbT[:, ro(c), :], in_=runb[:]), [runb], [cdbT])
                front(c, 10 if USE_CACHE else 9)
                if USE_CACHE:
                  P.dma(fr_x(c), xtok[:], reads=[xtok, frt(c)], writes=[frt(c)])
                  P.dma(fr_b(c), Btok[:], reads=[Btok, frt(c)], writes=[frt(c)])
                  P.dma(fr_bc(c), xbcT[:, 8:10, :], reads=[xbcT, frt(c)], writes=[frt(c)])
                  P.dma(fr_dt(c), dtl[:], reads=[dtl, frt(c)], writes=[frt(c)])
                dir_stats(1)
                state_update(1, [PS[0], PS[1]])
                if lat:
                    V(lambda e: e.tensor_tensor(out=runb[:], in0=runb[:], in1=dec[:], op=ALU.mult), [runb, dec], [runb])
            if not lat:
                transpose_state_out(Hst[1], nsb[si], Wk)
            else:
                V(lambda e: e.memset(Hst[0][:], 0.0), [], [Hst[0]])
                for c in own:
                    P.dma(xtok[:], fr_x(c), reads=[frt(c)], writes=[xtok])
                    P.dma(Btok[:], fr_b(c), reads=[frt(c)], writes=[Btok])
                    P.dma(dtl[:], fr_dt(c), reads=[frt(c)], writes=[dtl])
                    dir_stats(0)
                    state_update(0, [PS[0], PS[1]])
                    V(lambda e: e.tensor_tensor(out=runf[:], in0=runf[:], in1=dec[:], op=ALU.mult), [runf, dec], [runf])
                P.dma(ex_src[:, 0:512], Hst[0][:], reads=[Hst[0], ex_tok], writes=[ex_tok])
                P.dma(ex_src[:, 512:1024], Hst[1][:], reads=[Hst[1], ex_tok], writes=[ex_tok])
                P.dma(ex_src[:, 1024:1040], runf[:], reads=[runf, ex_tok], writes=[ex_tok])
                P.dma(ex_src[:, 1040:1056], runb[:], reads=[runb, ex_tok], writes=[ex_tok])
                all_gather4(ex_src, ex_dst, ex_tok, ex_tok2)

            qT = sbw("qT", [128, 4, 128], BF16); sg = sbw("sg", [128, 4, 128], BF16)
            zs = sbw("zs", [128, 1024], BF16)
            STt = sbw("ST", [128, 16, 128], BF16)
            CBm = sbw("CBm", [128, 2, 2, 128], BF16)
            Hin = sbw("Hin", [128, 512], BF16); Hbin = sbw("Hbin", [128, 512])
            yacc = sbw("yacc", [128, 1024]); tmp = sbw("tmp", [128, 512])
            sn = sbw("sn", [128, 1024], BF16); sT = sbw("sT", [128, 8, 128], BF16)
            ET = [sbw(f"ET{i}", [128, 512], BF16) for i in range(2)]
            rd = [sbw(f"rd{i}", [128, 512]) for i in range(2)]
            Es = [sbw(f"Es{i}", [128, 512]) for i in range(2)]
            aT = sbw("aT", [128, 4, 128], BF16)
            xnew = sbw("xnew", [128, 1024])
            if lat:
                h1T = sbw("h1T", [128, 8, 128], BF16)
                ckn = sbw("ckn", [128, 128], BF16); kp = sbw("kp", [128, 32]); kp1 = sbw("kp1", [128, 32]); kp2 = sbw("kp2", [128, 32])
                kpb = sbw("kpb", [128, 32], BF16); csk = sbw("csk", [128, 2, 32])
                ckT = sbw("ckT", [128, 128], BF16); kpT = sbw("kpT", [32, 128], BF16)
            if lat:
                def compose(H, h0src, col0, dcol0, mk, order):
                    load_h0(h0src, H)
                    for q in order:
                        P.dma(Hbin[:], ex_dst[q * 128:(q + 1) * 128, col0:col0 + 512], reads=[ex_tok2], writes=[Hbin])
                        P.dma(dq[:], ex_dst[q * 128:(q + 1) * 128, dcol0:dcol0 + 16], reads=[ex_tok2], writes=[dq])
                        V(lambda e, q=q: e.tensor_scalar(out=coef[:], in0=dq[:], scalar1=-1.0, scalar2=mk[:, q:q + 1], op0=ALU.add, op1=ALU.mult), [dq, mk], [coef])
                        V(lambda e: e.tensor_scalar(out=coef[:], in0=coef[:], scalar1=1.0, scalar2=None, op0=ALU.add), [coef], [coef])
                        for g in range(2):
                            r = slice(64 * g, 64 * g + 64)
                            V(lambda e, r=r, g=g: e.tensor_tensor(out=H[r, :].rearrange("p (h q) -> p h q", h=8), in0=H[r, :].rearrange("p (h q) -> p h q", h=8),
                                                                  in1=bc(coef[r, 8 * g:8 * g + 8].unsqueeze(2), [64, 8, 64]), op=ALU.mult), [H, coef], [H])
                        V(lambda e, q=q: e.scalar_tensor_tensor(out=H[:], in0=Hbin[:], scalar=mk[:, q:q + 1], in1=H[:], op0=ALU.mult, op1=ALU.add), [Hbin, mk, H], [H])
                compose(Hst[0], h0f, 0, 1024, mltb, [0, 1, 2, 3])
                compose(HinB, h0b, 512, 1040, mgtb, [3, 2, 1, 0])
            else:
                V(lambda e: e.memset(Hst[0][:], 0.0), [], [Hst[0]])
            for c in own:
                if USE_CACHE:
                    P.dma(Wn[:], hT_d[:, :, c * 128:c * 128 + 132], reads=[hTd_tok], writes=[Wn])
                    P.dma(xtok[:], fr_x(c), reads=[frt(c)], writes=[xtok])
                    P.dma(Btok[:], fr_b(c), reads=[frt(c)], writes=[Btok])
                    P.dma(xbcT[:, 8:10, :], fr_bc(c), reads=[frt(c)], writes=[xbcT])
                    P.dma(dtl[:], fr_dt(c), reads=[frt(c)], writes=[dtl])
                else:
                    front(c, 10)
                P.dma(xr[:], x_src[c * 128:(c + 1) * 128, :], writes=[xr])
                P.dma(Hbin[:], Hb_d[ro(c)], reads=[Hbd_tok], writes=[Hbin])
                if lat:
                    for g in range(2):
                        r = slice(64 * g, 64 * g + 64)
                        V(lambda e, r=r, g=g, c=c: e.tensor_tensor(out=tmp[r, :].rearrange("p (h q) -> p h q", h=8), in0=HinB[r, :].rearrange("p (h q) -> p h q", h=8),
                                                                   in1=bc(cdbT[r, ro(c), 8 * g:8 * g + 8].unsqueeze(2), [64, 8, 64]), op=ALU.mult), [HinB, cdbT], [tmp])
                    V(lambda e: e.tensor_tensor(out=Hbin[:], in0=Hbin[:], in1=tmp[:], op=ALU.add), [Hbin, tmp], [Hbin])
                Wm = Wn
                P.start_capture()
                if lat:
                    P.dma(cs[:, 0, :], c_cosA[:, c * 128:(c + 1) * 128], writes=[cs])
                    P.dma(cs[:, 1, :], c_sinA[:, c * 128:(c + 1) * 128], writes=[cs])
                for j in range(4):
                    pg = PS[2 + j % 2]
                    for k in range(8):
                        mm(pg, pg[:, 0:128], wI0[:, k, G0 + j * 128:G0 + (j + 1) * 128], Wm[:, k, 2:130], k == 0, k == 7, [wI, Wn])
                    A(lambda e, j=j, pg=pg: e.activation(out=sg[:, j, :], in_=pg[:, 0:128], func=AF.Silu), [pg], [sg])
                for j in range(4):
                    pb = PS[j % 2]
                    for k in range(8):
                        mm(pb, pb[:, 0:128], wI0[:, k, Q0 + j * 128:Q0 + (j + 1) * 128], Wm[:, k, 2:130], k == 0, k == 7, [wI, Wn])
                    if lat:
                        A(lambda e, pb=pb: e.copy(out=uT[:], in_=pb[:, 0:128]), [pb], [uT])
                        pr = PS[2]
                        mm(pr, pr[:, 0:128], pma[:], uT[:], True, True, [pma, uT])
                        V(lambda e: e.tensor_tensor(out=t1[:], in0=uT[:], in1=cs[:, 0, :], op=ALU.mult), [uT, cs], [t1])
                        V(lambda e, pr=pr: e.tensor_tensor(out=t2[:], in0=pr[:, 0:128], in1=cs[:, 1, :], op=ALU.mult), [pr, cs], [t2])
                        G(lambda e, j=j: e.tensor_tensor(out=qT[:, j, :], in0=t1[:], in1=t2[:], op=ALU.add), [t1, t2], [qT])
                    else:
                        V(lambda e, j=j, pb=pb: e.tensor_copy(out=qT[:, j, :], in_=pb[:, 0:128]), [pb], [qT])
                if lat:
                    kbs = [(c - 1, GE, 0 if c == 1 else None), (c, None, None), (c + 1, LE, 1 if c == 8 else None), (nch, None, None), (nch + 1, None, None)]
                else:
                    kbs = [(0, None, None), (1, None, None)]
                for ki, (kb, msk, fidx) in enumerate(kbs):
                    for hh in range(2):
                        pS = PS[hh]
                        r = slice(64 * hh, 64 * hh + 64)
                        for j in range(4):
                            mm(pS, pS[:, j * 128:(j + 1) * 128], kT[r, kb * 128:(kb + 1) * 128], qT[r, j, :], True, True, [kT, qT])
                        A(lambda e, hh=hh, pS=pS: e.activation(out=ET[hh][:], in_=pS[:, 0:512], func=AF.Exp, scale=A_SCALE), [pS], [ET[hh]])
                        if msk is not None:
                            VG(lambda e, hh=hh, msk=msk: e.tensor_tensor(out=ET[hh][:].rearrange("p (j q) -> p j q", j=4), in0=ET[hh][:].rearrange("p (j q) -> p j q", j=4),
                                                                         in1=bc(msk[:].unsqueeze(1), [128, 4, 128]), op=ALU.mult), [ET[hh], msk], [ET[hh]])
                        if fidx is not None:
                            V(lambda e, hh=hh, fidx=fidx: e.tensor_scalar(out=ET[hh][:], in0=ET[hh][:], scalar1=flb[:, fidx:fidx + 1], scalar2=None, op0=ALU.mult),
                              [ET[hh], flb], [ET[hh]])
                        mm(PS[2 + hh], PS[2 + hh][:, 0:512], Vt[:, kb, :], ET[hh][:], ki == 0, ki == len(kbs) - 1, [Vt, ET[hh]])
                        if ki == 0:
                            V(lambda e, hh=hh: e.tensor_copy(out=Es[hh][:], in_=ET[hh][:]), [ET[hh]], [Es[hh]])
                        else:
                            V(lambda e, hh=hh: e.tensor_tensor(out=Es[hh][:], in0=Es[hh][:], in1=ET[hh][:], op=ALU.add), [ET[hh], Es[hh]], [Es[hh]])
                for hh in range(2):
                    mm(PS[hh], PS[hh][:, 0:512], ones[:], Es[hh][:], True, True, [ones, Es[hh]])
                for hh in range(2):
                    r = slice(64 * hh, 64 * hh + 64)
                    V(lambda e, hh=hh, r=r: e.tensor_tensor(out=rd[hh][r, :].rearrange("p (j q) -> p j q", j=4), in0=PS[hh][r, 0:512].rearrange("p (j q) -> p j q", j=4),
                                                            in1=bc(esink[r, 4 * hh:4 * hh + 4].unsqueeze(2), [64, 4, 128]), op=ALU.add), [PS[hh], esink], [rd[hh]])
                    A(lambda e, hh=hh, r=r: e.activation(out=rd[hh][r, :], in_=rd[hh][r, :], func=AF.Ln), [rd[hh]], [rd[hh]])
                    A(lambda e, hh=hh, r=r: e.activation(out=rd[hh][r, :], in_=rd[hh][r, :], func=AF.Exp, scale=-1.0), [rd[hh]], [rd[hh]])
                    V(lambda e, hh=hh, r=r: e.tensor_tensor(out=rd[hh][r, :], in0=PS[2 + hh][r, 0:512], in1=rd[hh][r, :], op=ALU.mult), [PS[2 + hh], rd[hh]], [rd[hh]])
                    G(lambda e, hh=hh, r=r: e.tensor_tensor(out=aT[r, :, :].rearrange("p j q -> p (j q)"), in0=rd[hh][r, :],
                                                            in1=sg[r, :, :].rearrange("p j q -> p (j q)"), op=ALU.mult), [rd[hh], sg], [aT])
                att_ops = P.stop_capture()
                P.start_capture()
                for hf in range(2):
                    pz = PS[6 + hf]
                    for k in range(8):
                        mm(pz, pz[:, 0:512], Wm[:, k, 2:130], wI0[:, k, Z0 + hf * 512:Z0 + (hf + 1) * 512], k == 0, k == 7, [wI, Wn])
                    A(lambda e, hf=hf, pz=pz: e.activation(out=zs[:, hf * 512:(hf + 1) * 512], in_=pz[:, 0:512], func=AF.Silu), [pz], [zs])
                pcbs = [PS[4], PS[5]]
                for g in range(2):
                    mm(pcbs[g], pcbs[g][:, 0:128], xbcT[64 * g:64 * g + 64, 8, :], xbcT[64 * g:64 * g + 64, 9, :], True, True, [xbcT])
                for g in range(2):
                    for d in range(2):
                        V(lambda e, g=g, d=d: e.tensor_tensor(out=CBm[:, g, d, :], in0=pcbs[g][:, 0:128],
                                                              in1=(LE if d == 0 else GE)[:], op=ALU.mult), [pcbs[g], LE, GE], [CBm])
                xv = xtok[:].rearrange("p (h q) -> p h q", h=16)
                G(lambda e: e.tensor_tensor(out=yacc[:].rearrange("p (h q) -> p h q", h=16), in0=xv, in1=bc(dskipb[:].unsqueeze(2), [128, 16, 64]), op=ALU.mult),
                  [xtok, dskipb], [yacc])
                for d in range(2):
                    dir_stats(d)
                    H = Hst[0] if d == 0 else Hbin
                    A(lambda e, H=H: e.copy(out=Hin[:], in_=H[:]), [H], [Hin])
                    G(lambda e, d=d: e.tensor_tensor(out=xd[:].rearrange("p (h q) -> p h q", h=16), in0=xv,
                                                     in1=bc(dtp[:, 16 * d:16 * d + 16].unsqueeze(2), [128, 16, 64]), op=ALU.mult), [xtok, dtp], [xd])
                    tri = LE if d == 0 else GE
                    G(lambda e, d=d, tri=tri: e.tensor_tensor(out=R[:], in0=bc(tri[:].unsqueeze(1), [128, 16, 128]),
                                                              in1=bc(la[:, 16 * d:16 * d + 16].unsqueeze(2), [128, 16, 128]), op=ALU.mult), [tri, la], [R])
                    st = GT if d == 0 else LT
                    for i in range(4):
                        pb = PS[4 + i % 2]
                        mm(pb, pb[:, 0:512], st[:], R[:, 4 * i:4 * i + 4, :].rearrange("p h l -> p (h l)"), True, True, [st, R])
                        A(lambda e, i=i, pb=pb: e.activation(out=STt[:, 4 * i:4 * i + 4, :].rearrange("p h l -> p (h l)"), in_=pb[:, 0:512], func=AF.Exp), [pb], [STt])
                    for i in range(4):
                        g = i // 2
                        VG(lambda e, i=i, g=g, d=d: e.tensor_tensor(out=STt[:, 4 * i:4 * i + 4, :], in0=STt[:, 4 * i:4 * i + 4, :],
                                                                    in1=bc(CBm[:, g, d, :].unsqueeze(1), [128, 4, 128]), op=ALU.mult), [STt, CBm], [STt])
                    for g in range(2):
                        po = PS[6 + g]
                        mm(po, po[:, 0:512], xbcT[64 * g:64 * g + 64, 9, :], Hin[64 * g:64 * g + 64, :], True, True, [xbcT, Hin])
                        pdg = PS[4 + g]
                        for hl in range(8):
                            h = g * 8 + hl
                            mm(pdg, pdg[:, hl * 64:(hl + 1) * 64], STt[:, h, :], xd[:, h * 64:(h + 1) * 64], True, True, [STt, xd])
                        ya = yacc[:, g * 512:(g + 1) * 512]
                        V(lambda e, g=g, po=po: e.tensor_tensor(out=tmp[:].rearrange("p (h q) -> p h q", h=8), in0=po[:, 0:512].rearrange("p (h q) -> p h q", h=8),
                                                                in1=bc(ecum[:, 8 * g:8 * g + 8].unsqueeze(2), [128, 8, 64]), op=ALU.mult), [po, ecum], [tmp])
                        G(lambda e, ya=ya: e.tensor_tensor(out=ya, in0=ya, in1=tmp[:], op=ALU.add), [yacc, tmp], [yacc])
                        V(lambda e, ya=ya, pdg=pdg: e.tensor_tensor(out=ya, in0=ya, in1=pdg[:, 0:512], op=ALU.add), [yacc, pdg], [yacc])
                    if d == 0:
                        state_update(0, [PS[6], PS[7]])
                V(lambda e: e.tensor_tensor(out=yacc[:], in0=yacc[:], in1=zs[:], op=ALU.mult), [yacc, zs], [yacc])
                rms_stats(yacc[:], 1024, yacc, Wk["sq"], Wk["ss"], Wk["rstd"])
                V(lambda e: e.scalar_tensor_tensor(out=sn[:], in0=yacc[:], scalar=Wk["rstd"][:, 0:1], in1=gnormb[:], op0=ALU.mult, op1=ALU.mult),
                  [yacc, Wk["rstd"], gnormb], [sn])
                pt = PS[7]; ptv = pt[:].bitcast(BF16)
                for k in range(8):
                    T(lambda e, k=k: e.transpose(out=ptv[:, k * 128:(k + 1) * 128], in_=sn[:, k * 128:(k + 1) * 128], identity=idb[:]), [sn, idb], [pt])
                V(lambda e: e.tensor_copy(out=sT[:], in_=ptv.rearrange("p (k t) -> p k t", k=8)), [pt], [sT])
                ssd_ops = P.stop_capture()
                P.merge(att_ops, ssd_ops)
                for hf in range(2):
                    po = PS[6 + hf]
                    for k in range(12):
                        lhs = aT[:, k, :] if k < 4 else sT[:, k - 4, :]
                        mm(po, po[:, 0:512], lhs, wO0[:, k, hf * 512:(hf + 1) * 512], k == 0, k == 11, [aT, sT, wO])
                    V(lambda e, hf=hf, po=po: e.tensor_tensor(out=tmp[:], in0=po[:, 0:512], in1=gateb[cond][:, hf * 512:(hf + 1) * 512], op=ALU.mult),
                      [po, gateb[cond]], [tmp])
                    G(lambda e, hf=hf: e.tensor_tensor(out=xnew[:, hf * 512:(hf + 1) * 512], in0=tmp[:], in1=xr[:, hf * 512:(hf + 1) * 512], op=ALU.add),
                      [tmp, xr], [xnew])
                if not lat:
                    P.dma(xc1_d[si, c * 128:(c + 1) * 128, :], xnew[:], reads=[xnew, xc1_tok, frt(c)], writes=[xc1_tok, frt(c)])
                else:
                    P.dma(xl1_d[ro(c) * 128:(ro(c) + 1) * 128, :], xnew[:], reads=[xnew, xl1_tok, frt(c)], writes=[xl1_tok, frt(c)])
                    norm_to_hT(xnew, 1, 1, h1T, Wk)
                    pkv = PS[0]
                    for k in range(8):
                        mm(pkv, pkv[:, 0:160], h1T[:, k, :], wKV1[:, k, :], k == 0, k == 7, [h1T, wKV1])
                    A(lambda e: e.activation(out=Wk["sq"][:, 0:128], in_=pkv[:, 0:128], func=AF.Square, accum_out=Wk["ss"][:]), [pkv], [Wk["sq"], Wk["ss"]])
                    A(lambda e: e.activation(out=Wk["rstd"][:], in_=Wk["ss"][:], func=AF.Ln, scale=1.0 / 128, bias=EPS), [Wk["ss"]], [Wk["rstd"]])
                    A(lambda e: e.activation(out=Wk["rstd"][:], in_=Wk["rstd"][:], func=AF.Exp, scale=-0.5), [Wk["rstd"]], [Wk["rstd"]])
                    V(lambda e: e.scalar_tensor_tensor(out=ckn[:], in0=pkv[:, 0:128], scalar=Wk["rstd"][:, 0:1], in1=kvnormb[:], op0=ALU.mult, op1=ALU.mult),
                      [pkv, Wk["rstd"], kvnormb], [ckn])
                    P.dma(csk[:, 0, :], c_cosK[ro(c) * 128:(ro(c) + 1) * 128, :], writes=[csk])
                    P.dma(csk[:, 1, :], c_sinK[ro(c) * 128:(ro(c) + 1) * 128, :], writes=[csk])
                    A(lambda e: e.copy(out=kp[:], in_=pkv[:, 128:160]), [pkv], [kp])
                    V(lambda e: e.tensor_tensor(out=kp1[:], in0=kp[:], in1=csk[:, 0, :], op=ALU.mult), [kp, csk], [kp1])
                    kv4 = kp[:].rearrange("p (a b i) -> p a b i", a=2, b=2)
                    k24 = kp2[:].rearrange("p (a b i) -> p a b i", a=2, b=2)
                    s4 = csk[:, 1, :].rearrange("p (a b i) -> p a b i", a=2, b=2)
                    V(lambda e: e.tensor_tensor(out=k24[:, :, 0, :], in0=kv4[:, :, 1, :], in1=s4[:, :, 0, :], op=ALU.mult), [kp, csk], [kp2])
                    V(lambda e: e.tensor_tensor(out=k24[:, :, 1, :], in0=kv4[:, :, 0, :], in1=s4[:, :, 1, :], op=ALU.mult), [kp, csk, kp2], [kp2])
                    V(lambda e: e.tensor_tensor(out=kpb[:], in0=kp1[:], in1=kp2[:], op=ALU.add), [kp1, kp2], [kpb])
                    pt3 = PS[1]; ptv3 = pt3[:].bitcast(BF16)
                    T(lambda e: e.transpose(out=ptv3[:, 0:128], in_=ckn[:], identity=idb[:]), [ckn, idb], [pt3])
                    T(lambda e: e.transpose(out=ptv3[0:32, 128:256], in_=kpb[:], identity=idb[:]), [kpb, idb], [pt3])
                    V(lambda e: e.tensor_copy(out=ckT[:], in_=ptv3[:, 0:128]), [pt3], [ckT])
                    V(lambda e: e.tensor_copy(out=kpT[:], in_=ptv3[0:32, 128:256]), [pt3], [kpT])
                    P.dma(ckvT_d[:, ro(c) * 128:(ro(c) + 1) * 128], ckT[:], reads=[ckT, kvT_tok], writes=[kvT_tok])
                    P.dma(kpeT_d[:, ro(c) * 128:(ro(c) + 1) * 128], kpT[:], reads=[kpT, kvT_tok], writes=[kvT_tok])
            if not lat:
                transpose_state_out(Hst[0], nsf[si], Wk)
            else:
                all_gather4(kv_d, kv_g, kvT_tok, kvg_tok)


        def l1_setup(ph):
            for i_ in range(NSTG):
                stg[i_] = P.sb([128, 1024], F32, f"stg{i_}", ph)
            load_w(wI, w1I, w_in1, 8, 1440)
            load_w(wI, wUQ, w_uq, 2, 1536)
            load_w(wI, wUKV3, w_ukv, 1, 2048)
            load_w(wO, wO1, w_out1, 8, 1024)
            V(lambda e: e.memset(wK[:], 0.0), [], [wO])
            ukv4 = wUKV.rearrange("p (h c) -> p h c", h=16)
            V(lambda e: e.tensor_copy(out=wK[:, :, 0:64], in_=ukv4[:, :, 0:64]), [wI], [wO])
            V(lambda e: e.tensor_copy(out=wVp[:, :, 0:64], in_=ukv4[:, 0:8, 64:128]), [wI], [wO])
            V(lambda e: e.tensor_copy(out=wVp[:, :, 64:128], in_=ukv4[:, 8:16, 64:128]), [wI], [wO])
            compute_gates(1, ph)
            P.dma(gnormb[:], fnormw.partition_broadcast(128), writes=[gnormb])

        def mla_sequence(kind, si, ph, boff=None):
            lat = kind == "lat"

            def bk(i):
                return PS[i] if boff is None else PS[boff + i % 4]
            cond = 1 if lat else 0
            ntok = 1024 if lat else 256
            nblk = ntok // 128
            nkb = 34 if lat else 2
            nkeys = nkb * 128
            TG = 512 if lat else 256
            ntg = ntok // TG
            Wk = {}

            def sbw(name, shape, dt=F32):
                Wk[name] = P.sb(shape, dt, name, ph)
                return Wk[name]
            R = sbw("R", [128, 2048])
            Wk["sq"] = R.view(R[:, 0:1024]); Wk["ht"] = R.view(R[:, 1024:2048].rearrange("p (k t) -> p k t", k=8))
            sbw("ss", [128, 1]); sbw("rstd", [128, 1]); sbw("xn", [128, 1024], BF16)
            xr = sbw("xr", [128, 1024]); xq = xr; hT1 = sbw("hT1", [128, 8, 128], BF16)
            cqn = sbw("cqn", [128, 256], BF16); cqT = sbw("cqT", [128, 2, ntok], BF16)
            sg1 = sbw("sg1", [128, 8, ntok], BF16); a1T = sg1
            ckvT = sbw("ckvT", [128, nkeys], BF16); kpeT = sbw("kpeT", [32, nkeys], BF16)
            if lat:
                KT = Vp = None
            else:
                KT = sbw("KT", [96, nkeys], BF16); Vp = sbw("Vp", [128, nkb, 128], BF16)
            qp = sbw("qp", [96, ntok], BF16) if not lat else None
            ET = [sbw(f"ET{i}", [128, TG], BF16) for i in range(6 if lat else 4)]
            rd = sbw("rd", [128, TG])
            fnormb = gnormb; qnormb = sbw("qnormb", [128, 256])
            xfin = sbw("xfin", [128, 1024]); tmp = R.view(R[:, 1024:1536]); yo = xq
            accs = xfin.view(xfin[:, 0:512])
            swp = sbw("swp", [128, 128])
            V(lambda e: e.tensor_copy(out=swp[:, 0:64], in_=ident[:, 64:128]), [ident], [swp])
            V(lambda e: e.tensor_copy(out=swp[:, 64:128], in_=ident[:, 0:64]), [ident], [swp])
            bselb = sbw("bselb", [32, 96], BF16); bself = sbw("bself", [32, 96])
            P.dma(qnormb[:], qnormw.partition_broadcast(128), writes=[qnormb])
            P.dma(bself[:], c_bsel, writes=[bself])
            V(lambda e: e.tensor_copy(out=bselb[:], in_=bself[:]), [bself], [bselb])
            if lat:
                pmm = sbw("pmm", [96, 96]); cq_ = sbw("cosq", [96, 1024]); sq_ = sbw("sinq", [96, 1024])
                uq = sbw("uq", [96, 512]); tq1 = sbw("tq1", [96, 512]); tq2 = sbw("tq2", [96, 512])
                cf = sbw("cf", [128, 2, 160]); cfb = sbw("cfb", [128, 2, 160], BF16)
                P.dma(pmm[:], c_pmm, writes=[pmm]); P.dma(cq_[:], c_cosQ, writes=[cq_]); P.dma(sq_[:], c_sinQ, writes=[sq_])
                for q in range(4):
                    P.dma(ckvT[:, q * 1024:(q + 1) * 1024], kv_g[q * 160:q * 160 + 128, :], reads=[kvg_tok], writes=[ckvT])
                    P.dma(kpeT[:, q * 1024:(q + 1) * 1024], kv_g[q * 160 + 128:(q + 1) * 160, :], reads=[kvg_tok], writes=[kpeT])
                P.dma(cf[:, :, 0:128], cckv.rearrange("(b p) c -> p b c", p=128), writes=[cf])
                P.dma(cf[:, :, 128:160], ckpe.rearrange("(b p) c -> p b c", p=128), writes=[cf])
                V(lambda e: e.tensor_copy(out=cfb[:], in_=cf[:]), [cf], [cfb])
                pt = bk(6); ptv = pt[:].bitcast(BF16)
                for b in range(2):
                    T(lambda e, b=b: e.transpose(out=ptv[:, b * 128:(b + 1) * 128], in_=cfb[:, b, 0:128], identity=idb[:]), [cfb, idb], [pt])
                    T(lambda e, b=b: e.transpose(out=ptv[0:32, 256 + b * 128:256 + (b + 1) * 128], in_=cfb[:, b, 128:160], identity=idb[:]), [cfb, idb], [pt])
                V(lambda e: e.tensor_copy(out=ckvT[:, 4096:4352], in_=ptv[:, 0:256]), [pt], [ckvT])
                V(lambda e: e.tensor_copy(out=kpeT[:, 4096:4352], in_=ptv[0:32, 256:512]), [pt], [kpeT])
            else:
                ckf = sbw("ckf", [128, 160]); ckb = sbw("ckb", [128, 160], BF16)

            def load_x(i):
                if lat:
                    P.dma(xr[:], xl1_d[i * 128:(i + 1) * 128, :], reads=[xl1_tok], writes=[xr])
                else:
                    P.dma(xr[:], xc1_d[si, i * 128:(i + 1) * 128, :], reads=[xc1_tok], writes=[xr])

            for i in range(nblk):
                tsl = slice(i * 128, (i + 1) * 128)
                load_x(i)
                norm_to_hT(xr, 1, cond, hT1, Wk, bk(7))
                ncols = 256 if lat else 416
                pp = bk(0)
                for k in range(8):
                    mm(pp, pp[:, 0:ncols], hT1[:, k, :], w1I[:, k, 0:ncols], k == 0, k == 7, [hT1, wI])
                A(lambda e: e.activation(out=Wk["sq"][:, 0:256], in_=pp[:, 0:256], func=AF.Square, accum_out=Wk["ss"][:]), [pp], [Wk["sq"], Wk["ss"]])
                A(lambda e: e.activation(out=Wk["rstd"][:], in_=Wk["ss"][:], func=AF.Ln, scale=1.0 / 256, bias=EPS), [Wk["ss"]], [Wk["rstd"]])
                A(lambda e: e.activation(out=Wk["rstd"][:], in_=Wk["rstd"][:], func=AF.Exp, scale=-0.5), [Wk["rstd"]], [Wk["rstd"]])
                V(lambda e: e.scalar_tensor_tensor(out=cqn[:], in0=pp[:, 0:256], scalar=Wk["rstd"][:, 0:1], in1=qnormb[:], op0=ALU.mult, op1=ALU.mult),
                  [pp, Wk["rstd"], qnormb], [cqn])
                pt = bk(1); ptv = pt[:].bitcast(BF16)
                for k2 in range(2):
                    T(lambda e, k2=k2: e.transpose(out=ptv[:, k2 * 128:(k2 + 1) * 128], in_=cqn[:, k2 * 128:(k2 + 1) * 128], identity=idb[:]), [cqn, idb], [pt])
                V(lambda e, tsl=tsl: e.tensor_copy(out=cqT[:, :, tsl], in_=ptv[:, 0:256].rearrange("p (k t) -> p k t", k=2)), [pt], [cqT])
                if not lat:
                    A(lambda e: e.activation(out=Wk["sq"][:, 0:128], in_=pp[:, 256:384], func=AF.Square, accum_out=Wk["ss"][:]), [pp], [Wk["sq"], Wk["ss"]])
                    A(lambda e: e.activation(out=Wk["rstd"][:], in_=Wk["ss"][:], func=AF.Ln, scale=1.0 / 128, bias=EPS), [Wk["ss"]], [Wk["rstd"]])
                    A(lambda e: e.activation(out=Wk["rstd"][:], in_=Wk["rstd"][:], func=AF.Exp, scale=-0.5), [Wk["rstd"]], [Wk["rstd"]])
                    V(lambda e: e.scalar_tensor_tensor(out=ckf[:, 0:128], in0=pp[:, 256:384], scalar=Wk["rstd"][:, 0:1], in1=kvnormb[:], op0=ALU.mult, op1=ALU.mult),
                      [pp, Wk["rstd"], kvnormb], [ckf])
                    A(lambda e: e.copy(out=ckf[:, 128:160], in_=pp[:, 384:416]), [pp], [ckf])
                    P.dma(nckv[si, tsl, :], ckf[:, 0:128], reads=[ckf])
                    P.dma(nkpe[si, tsl, :], ckf[:, 128:160], reads=[ckf])
                    V(lambda e: e.tensor_copy(out=ckb[:], in_=ckf[:]), [ckf], [ckb])
                    pt2 = bk(2); ptv2 = pt2[:].bitcast(BF16)
                    T(lambda e: e.transpose(out=ptv2[:, 0:128], in_=ckb[:, 0:128], identity=idb[:]), [ckb, idb], [pt2])
                    T(lambda e: e.transpose(out=ptv2[0:32, 128:256], in_=ckb[:, 128:160], identity=idb[:]), [ckb, idb], [pt2])
                    V(lambda e, tsl=tsl: e.tensor_copy(out=ckvT[:, tsl], in_=ptv2[:, 0:128]), [pt2], [ckvT])
                    V(lambda e, tsl=tsl: e.tensor_copy(out=kpeT[:, tsl], in_=ptv2[0:32, 128:256]), [pt2], [kpeT])
                for j in range(8):
                    pg = bk(3 + j % 2)
                    for k in range(8):
                        mm(pg, pg[:, 0:128], w1I[:, k, 416 + j * 128:416 + (j + 1) * 128], hT1[:, k, :], k == 0, k == 7, [wI, hT1])
                    A(lambda e, j=j, pg=pg, tsl=tsl: e.activation(out=sg1[:, j, tsl], in_=pg[:, 0:128], func=AF.Silu), [pg], [sg1])

            segs = [(s0, min(512, nkeys - s0)) for s0 in range(0, nkeys, 512)]
            ev = [0]

            def evac(out_ap, in_ap, rbufs, wbufs):
                ev[0] += 1
                V(lambda e: e.tensor_copy(out=out_ap, in_=in_ap), rbufs, wbufs)

            if lat:
                NHALF = 2
                hkb = nkb // NHALF
                KTh = [sbw("KTa", [128, hkb * 128], BF16), sbw("KTb", [128, hkb * 128], BF16)]
                Vph = [sbw("Vpa", [128, hkb, 128], BF16), sbw("Vpb", [128, hkb, 128], BF16)]
                qps = [sbw("qp1", [128, ntok], BF16), sbw("qp2", [128, ntok], BF16)]
                for hf_ in range(NHALF):
                    V(lambda e, hf_=hf_: e.memset(KTh[hf_][:], 0.0), [], [KTh[hf_]])
                    P.dma(KTh[hf_][64:96, :], kpeT[:, hf_ * hkb * 128:(hf_ + 1) * hkb * 128], reads=[kpeT], writes=[KTh[hf_]])
                    V(lambda e, hf_=hf_: e.memset(qps[hf_][:], 0.0), [], [qps[hf_]])
                units = [(j, hs, hf) for j in range(8) for hs in range(2) for hf in range(NHALF)]

                def expand(ui):
                    j, hs, hf = units[ui]
                    h = j + 8 * hs
                    KTu, Vpu, qpu = KTh[ui % 2], Vph[ui % 2], qps[(ui // NHALF) % 2]
                    k0 = hf * hkb * 128
                    G(lambda e, hs=hs, Vpu=Vpu: e.memset(Vpu[:, :, 64 * (1 - hs):64 * (1 - hs) + 64], 1.0), [], [Vpu])
                    for kb0 in range(0, hkb, 8):
                        nb = min(8, hkb - kb0)
                        pb = bk(6 + (kb0 // 8) % 2)
                        for b_ in range(nb):
                            kk = k0 + (kb0 + b_) * 128
                            mm(pb, pb[:, b_ * 64:(b_ + 1) * 64], ckvT[:, kk:kk + 128], wVp[:, j, 64 * hs:64 * hs + 64], True, True, [ckvT, wO])
                        evac(Vpu[:, kb0:kb0 + nb, 64 * hs:64 * hs + 64], pb[:, 0:nb * 64].rearrange("p (b c) -> p b c", c=64), [pb], [Vpu])
                    for si_, s0 in enumerate(range(0, hkb * 128, 512)):
                        sn_ = min(512, hkb * 128 - s0)
                        pb = bk(6 + si_ % 2)
                        mm(pb, pb[0:64, 0:sn_], wK[:, h, 0:64], ckvT[:, k0 + s0:k0 + s0 + sn_], True, True, [wO, ckvT])
                        evac(KTu[0:64, s0:s0 + sn_], pb[0:64, 0:sn_], [pb], [KTu])
                    if hf == 0:
                        for tg in range(ntg):
                            tt = slice(tg * TG, (tg + 1) * TG)
                            pb = bk(6 + tg % 2)
                            for k2 in range(2):
                                mm(pb, pb[0:96, 0:TG], wUQ[:, k2, 96 * h:96 * h + 96], cqT[:, k2, tt], k2 == 0, k2 == 1, [wI, cqT])
                            V(lambda e, pb=pb: e.tensor_copy(out=uq[:], in_=pb[0:96, 0:TG]), [pb], [uq])
                            pr = bk(7 - tg % 2)
                            mm(pr, pr[0:96, 0:TG], pmm[:], uq[:], True, True, [pmm, uq])
                            V(lambda e, tt=tt: e.tensor_tensor(out=tq1[:], in0=uq[:], in1=cq_[:, tt], op=ALU.mult), [uq, cq_], [tq1])
                            V(lambda e, tt=tt, pr=pr: e.tensor_tensor(out=tq2[:], in0=pr[0:96, 0:TG], in1=sq_[:, tt], op=ALU.mult), [pr, sq_], [tq2])
                            G(lambda e, tt=tt, qpu=qpu: e.tensor_tensor(out=qpu[0:96, tt], in0=tq1[:], in1=tq2[:], op=ALU.add), [tq1, tq2], [qpu])

                def attend(ui):
                    j, hs, hf = units[ui]
                    r = slice(64 * hs, 64 * hs + 64)
                    KTu, Vpu, qpu = KTh[ui % 2], Vph[ui % 2], qps[(ui // NHALF) % 2]
                    SK = 3
                    for tg in range(ntg):
                        tt = slice(tg * TG, (tg + 1) * TG)
                        pacc = bk(4 + tg)

                        def s_stage(kb):
                            bi = kb % len(ET)
                            pS = bk(kb % 4)
                            mm(pS, pS[:, 0:TG], KTu[:, kb * 128:(kb + 1) * 128], qpu[:, tt], True, True, [KTu, qpu])
                            A(lambda e, bi=bi, pS=pS: e.activation(out=ET[bi][:], in_=pS[:, 0:TG], func=AF.Exp, scale=MLA_SCALE), [pS], [ET[bi]])

                        def pv_stage(kb):
                            bi = kb % len(ET)
                            mm(pacc, pacc[:, 0:TG], Vpu[:, kb, :], ET[bi][:], hf == 0 and kb == 0, hf == NHALF - 1 and kb == hkb - 1, [Vpu, ET[bi]])
                        for kb in range(hkb + SK):
                            if kb < hkb:
                                s_stage(kb)
                            if kb >= SK:
                                pv_stage(kb - SK)
                        if hf == NHALF - 1:
                            A(lambda e, pacc=pacc: e.copy(out=accs[:, 0:TG], in_=pacc[:, 0:TG]), [pacc], [accs])
                            mm(bk(0), bk(0)[:, 0:TG], swp[:], accs[:, 0:TG], True, True, [swp, accs])
                            A(lambda e, r=r: e.activation(out=rd[r, :], in_=bk(0)[r, 0:TG], func=AF.Ln), [bk(0)], [rd])
                            A(lambda e, r=r: e.activation(out=rd[r, :], in_=rd[r, :], func=AF.Exp, scale=-1.0), [rd], [rd])
                            V(lambda e, r=r: e.tensor_tensor(out=rd[r, :], in0=accs[r, 0:TG], in1=rd[r, :], op=ALU.mult), [accs, rd], [rd])
                            G(lambda e, r=r, j=j, tt=tt: e.tensor_tensor(out=a1T[r, j, tt], in0=rd[r, :], in1=sg1[r, j, tt], op=ALU.mult), [rd, sg1], [a1T])

                expand(0)
                for ui in range(len(units)):
                    P.start_capture()
                    attend(ui)
                    sa_ = P.stop_capture()
                    sb2 = []
                    if ui + 1 < len(units):
                        P.start_capture()
                        expand(ui + 1)
                        sb2 = P.stop_capture()
                    P.merge(sa_, sb2)
            else:
                for j in range(8):
                    for hs in range(2):
                        h = j + 8 * hs
                        r = slice(64 * hs, 64 * hs + 64)
                        G(lambda e, hs=hs: e.memset(Vp[:, :, 64 * (1 - hs):64 * (1 - hs) + 64], 1.0), [], [Vp])
                        for kb0 in range(0, nkb, 8):
                            nb = min(8, nkb - kb0)
                            pb = bk(6 + (kb0 // 8) % 2)
                            for b in range(nb):
                                mm(pb, pb[:, b * 64:(b + 1) * 64], ckvT[:, (kb0 + b) * 128:(kb0 + b + 1) * 128], wVp[:, j, 64 * hs:64 * hs + 64], True, True, [ckvT, wO])
                            evac(Vp[:, kb0:kb0 + nb, 64 * hs:64 * hs + 64], pb[:, 0:nb * 64].rearrange("p (b c) -> p b c", c=64), [pb], [Vp])
                        for (s0, sn_) in segs:
                            pb = bk(6 + (s0 // 512) % 2)
                            mm(pb, pb[0:96, 0:sn_], wK[:, h, :], ckvT[:, s0:s0 + sn_], True, False, [wO, ckvT])
                            mm(pb, pb[0:96, 0:sn_], bselb[:], kpeT[:, s0:s0 + sn_], False, True, [bselb, kpeT])
                            evac(KT[:, s0:s0 + sn_], pb[0:96, 0:sn_], [pb], [KT])
                        for tg in range(ntg):
                            tt = slice(tg * TG, (tg + 1) * TG)
                            pb = bk(6 + tg % 2)
                            for k2 in range(2):
                                mm(pb, pb[0:96, 0:TG], wUQ[:, k2, 96 * h:96 * h + 96], cqT[:, k2, tt], k2 == 0, k2 == 1, [wI, cqT])
                            if lat:
                                V(lambda e, pb=pb: e.tensor_copy(out=uq[:], in_=pb[0:96, 0:TG]), [pb], [uq])
                                pr = bk(5)
                                mm(pr, pr[0:96, 0:TG], pmm[:], uq[:], True, True, [pmm, uq])
                                V(lambda e, tt=tt: e.tensor_tensor(out=tq1[:], in0=uq[:], in1=cq_[:, tt], op=ALU.mult), [uq, cq_], [tq1])
                                V(lambda e, tt=tt, pr=pr: e.tensor_tensor(out=tq2[:], in0=pr[0:96, 0:TG], in1=sq_[:, tt], op=ALU.mult), [pr, sq_], [tq2])
                                G(lambda e, tt=tt: e.tensor_tensor(out=qp[:, tt], in0=tq1[:], in1=tq2[:], op=ALU.add), [tq1, tq2], [qp])
                            else:
                                evac(qp[:, tt], pb[0:96, 0:TG], [pb], [qp])
                        for tg in range(ntg):
                            tt = slice(tg * TG, (tg + 1) * TG)
                            SK = 2

                            def s_stage(kb):
                                bi = kb % 4
                                pS = bk(bi)
                                mm(pS, pS[:, 0:TG], KT[:, kb * 128:(kb + 1) * 128], qp[:, tt], True, True, [KT, qp])
                                A(lambda e, bi=bi, pS=pS: e.activation(out=ET[bi][:], in_=pS[:, 0:TG], func=AF.Exp, scale=MLA_SCALE), [pS], [ET[bi]])

                            def pv_stage(kb):
                                bi = kb % 4
                                mm(bk(4), bk(4)[:, 0:TG], Vp[:, kb, :], ET[bi][:], kb == 0, kb == nkb - 1, [Vp, ET[bi]])
                            for kb in range(nkb + SK):
                                if kb < nkb:
                                    s_stage(kb)
                                if kb >= SK:
                                    pv_stage(kb - SK)
                            A(lambda e: e.copy(out=accs[:, 0:TG], in_=bk(4)[:, 0:TG]), [bk(4)], [accs])
                            mm(bk(5), bk(5)[:, 0:TG], swp[:], accs[:, 0:TG], True, True, [swp, accs])
                            V(lambda e, r=r: e.reciprocal(out=rd[r, :], in_=bk(5)[r, 0:TG]), [bk(5)], [rd])
                            V(lambda e, r=r: e.tensor_tensor(out=rd[r, :], in0=accs[r, 0:TG], in1=rd[r, :], op=ALU.mult), [accs, rd], [rd])
                            G(lambda e, r=r, j=j, tt=tt: e.tensor_tensor(out=a1T[r, j, tt], in0=rd[r, :], in1=sg1[r, j, tt], op=ALU.mult), [rd, sg1], [a1T])

            for i in range(nblk):
                tsl = slice(i * 128, (i + 1) * 128)
                load_x(i)
                for hf in range(2):
                    po = bk(hf)
                    for jj in range(8):
                        mm(po, po[:, 0:512], a1T[:, jj, tsl], wO1[:, jj, hf * 512:(hf + 1) * 512], jj == 0, jj == 7, [a1T, wO])
                    V(lambda e, hf=hf, po=po: e.tensor_tensor(out=tmp[:], in0=po[:, 0:512], in1=gateb[cond][:, hf * 512:(hf + 1) * 512], op=ALU.mult),
                      [po, gateb[cond]], [tmp])
                    G(lambda e, hf=hf: e.tensor_tensor(out=xfin[:, hf * 512:(hf + 1) * 512], in0=tmp[:], in1=xr[:, hf * 512:(hf + 1) * 512], op=ALU.add),
                      [tmp, xr], [xfin])
                rms_stats(xfin[:], 1024, xfin, Wk["sq"], Wk["ss"], Wk["rstd"])
                V(lambda e: e.scalar_tensor_tensor(out=yo[:], in0=xfin[:], scalar=Wk["rstd"][:, 0:1], in1=fnormb[:], op0=ALU.mult, op1=ALU.mult),
                  [xfin, Wk["rstd"], fnormb], [yo])
                dst = ys[tsl, :] if lat else yp[si, tsl, :]
                P.dma(dst, yo[:], reads=[yo])

        l0 = ExitStack()
        wdiag = P.sb([128, 10, 5, 128], BF16, "wdiag", l0)
        with ExitStack() as ph:
            compute_gates(0, ph)
            for j in range(10):
                for kk in range(5):
                    V(lambda e, j=j, kk=kk: e.tensor_scalar(out=wdiag[:, j, kk, :], in0=ident[:], scalar1=cw[:, j, kk:kk + 1], scalar2=None, op0=ALU.mult),
                      [ident, cw], [wdiag])
            P.end_phase()
        for si in range(4):
            with ExitStack() as ph:
                ab_sequence("ctx", si, ph)
                P.end_phase()
        if stop_after >= 1:
            with ExitStack() as ph:
                ab_sequence("lat", 0, ph)
                P.end_phase()
        l0.close()
        with ExitStack() as ph:
            l1_setup(ph)
            P.end_phase()
        for si in (0, 2):
            with ExitStack() as ph:
                P.start_capture()
                mla_sequence("ctx", si, ph, 0)
                sa = P.stop_capture()
                P.start_capture()
                mla_sequence("ctx", si + 1, ph, 4)
                sb_ = P.stop_capture()
                P.merge(sa, sb_)
                P.end_phase()
        if stop_after >= 1:
            with ExitStack() as ph:
                mla_sequence("lat", 0, ph)
                P.end_phase()
    return nc


def _consts():
    a = np.arange(128)
    c = {}
    c["c_ident"] = np.eye(128, dtype=np.float32)
    c["c_le"] = (a[:, None] <= a[None, :]).astype(np.float32)
    c["c_ge"] = (a[:, None] >= a[None, :]).astype(np.float32)
    c["c_gt"] = (a[:, None] > a[None, :]).astype(np.float32)
    c["c_lt"] = (a[:, None] < a[None, :]).astype(np.float32)

    def rope_tabs(length, dim):
        rows = length // 64
        row = np.repeat(np.arange(rows), 64).astype(np.float32)
        col = np.tile(np.arange(64), rows).astype(np.float32)
        nf = dim // 4
        inv = (1.0 / (10000.0 ** (np.arange(nf, dtype=np.float32) / nf))).astype(np.float32)
        ar = row[:, None] * inv[None, :]
        ac = col[:, None] * inv[None, :]
        ang = np.concatenate([ar, ar, ac, ac], axis=-1).astype(np.float32)
        sign = np.concatenate([-np.ones(nf), np.ones(nf), -np.ones(nf), np.ones(nf)]).astype(np.float32)
        perm = np.concatenate([np.arange(nf, 2 * nf), np.arange(0, nf), np.arange(3 * nf, 4 * nf), np.arange(2 * nf, 3 * nf)])
        return np.cos(ang).astype(np.float32), (np.sin(ang) * sign[None, :]).astype(np.float32), perm
    cosA, sinA, permA = rope_tabs(4096, 64)
    c["_cosA"] = np.ascontiguousarray(np.concatenate([cosA.T, cosA.T], axis=0))
    c["_sinA"] = np.ascontiguousarray(np.concatenate([sinA.T, sinA.T], axis=0))
    pm = np.zeros((128, 128), np.float32)
    for m in range(128):
        pm[(m // 64) * 64 + permA[m % 64], m] = 1.0
    c["c_pma"] = pm
    cosK, sinK, permK = rope_tabs(4096, 32)
    c["_cosK"] = cosK
    c["_sinK"] = sinK
    pmm = np.zeros((96, 96), np.float32)
    cosQ = np.ones((96, 4096), np.float32)
    sinQ = np.zeros((96, 4096), np.float32)
    cosQ[64:96] = cosK.T
    sinQ[64:96] = sinK.T
    for m in range(96):
        pmm[m if m < 64 else 64 + permK[m - 64], m] = 1.0
    c["c_pmm"] = pmm
    c["_cosQ"] = cosQ
    c["_sinQ"] = sinQ
    bs = np.zeros((32, 96), np.float32)
    bs[np.arange(32), 64 + np.arange(32)] = 1.0
    c["c_bsel"] = bs
    return c


_NC_CACHE = {}


def kernel(**inp):
    f = lambda a: np.ascontiguousarray(np.asarray(a, dtype=np.float32))
    inp = {k: f(v) for k, v in inp.items()}
    pq = np.concatenate([np.concatenate([np.arange(j * 64, (j + 1) * 64), np.arange(256 + j * 64, 256 + (j + 1) * 64)]) for j in range(4)])
    w_in0 = inp["ab_w_in"][0]
    perm0 = np.concatenate([pq, np.arange(512, 768), 768 + pq, np.arange(1280, 3616)])
    w_in0p = np.ascontiguousarray(w_in0[:, perm0])
    w_out0 = inp["ab_w_out"][0]
    w_out0p = np.ascontiguousarray(np.concatenate([w_out0[pq], w_out0[512:]], axis=0))
    pm = np.concatenate([np.concatenate([np.arange(h * 64, (h + 1) * 64), np.arange((h + 8) * 64, (h + 9) * 64)]) for h in range(8)])
    w_in1 = inp["mla_w_in"][0]
    w_in1p = np.ascontiguousarray(np.concatenate([w_in1[:, :416], w_in1[:, 416:][:, pm]], axis=1))
    w_out1p = np.ascontiguousarray(inp["mla_w_out"][0][pm])
    cst = _consts()
    shared = {
        "normwT": np.ascontiguousarray(inp["norm_w"].reshape(2, 8, 128).transpose(0, 2, 1)),
        "w_in0": w_in0p, "w_out0": w_out0p,
        "sink": inp["ab_sink"].reshape(1, 8),
        "cwT": np.ascontiguousarray(inp["ab_conv_w"][0].T.reshape(10, 128, 5).transpose(1, 0, 2)),
        "cbT": np.ascontiguousarray(inp["ab_conv_b"][0].reshape(10, 128).T),
        "dtbias": inp["ab_dt_bias"].reshape(1, 32), "alog": inp["ab_a_log"].reshape(1, 32),
        "dskip": inp["ab_d_skip"].reshape(1, 16), "gnormw": inp["ab_gnorm_w"].reshape(1, 1024),
        "w_in1": w_in1p, "qnormw": inp["mla_q_norm_w"].reshape(1, 256), "kvnormw": inp["mla_kv_norm_w"].reshape(1, 128),
        "w_uq": inp["mla_w_uq"][0], "w_ukv": inp["mla_w_ukv"][0], "w_out1": w_out1p,
        "fnormw": inp["final_norm_w"].reshape(1, 1024),
    }
    for k, v in cst.items():
        if not k.startswith("_"):
            shared[k] = v
    in_maps = []
    for i in range(8):
        b, r = i // 4, i % 4
        m = dict(shared)
        m["xp"] = inp["x_prompt"][4 * i:4 * i + 4]
        lo = 1024 * r - 128
        xs_l = np.zeros((1280, 1024), np.float32)
        a0, a1 = max(lo, 0), min(lo + 1280, 4096)
        xs_l[a0 - lo:a1 - lo] = inp["x_sample"][b, a0:a1]
        m["xs"] = xs_l
        ca = np.zeros((128, 1280), np.float32); sa = np.zeros((128, 1280), np.float32)
        ca[:, a0 - lo:a1 - lo] = cst["_cosA"][:, a0:a1]; sa[:, a0 - lo:a1 - lo] = cst["_sinA"][:, a0:a1]
        m["c_cosA"] = ca; m["c_sinA"] = sa
        m["c_cosK"] = np.ascontiguousarray(cst["_cosK"][r * 1024:(r + 1) * 1024])
        m["c_sinK"] = np.ascontiguousarray(cst["_sinK"][r * 1024:(r + 1) * 1024])
        m["fl"] = np.array([[1.0 if r > 0 else 0.0, 1.0 if r < 3 else 0.0]], np.float32)
        m["mlt"] = np.array([[1.0 if q < r else 0.0 for q in range(4)]], np.float32)
        m["mgt"] = np.array([[1.0 if q > r else 0.0 for q in range(4)]], np.float32)
        m["cak"] = inp["cache_a_k"][b, 0].reshape(256, 128)
        m["cav"] = inp["cache_a_v"][b, 0].reshape(256, 128)
        m["h0f"] = inp["state_ssd_fwd"][b, 0]
        m["h0b"] = inp["state_ssd_bwd"][b, 0]
        m["cckv"] = inp["cache_mla_ckv"][b, 0]
        m["ckpe"] = inp["cache_mla_kpe"][b, 0]
        cond = np.stack([inp["c_ctx"], inp["c"][b]], axis=1)
        c3 = np.stack([inp["c_ctx"], inp["c"][0], inp["c"][1]], axis=1)
        m["cond3"] = np.ascontiguousarray(c3.reshape(8, 128, 3).transpose(1, 0, 2))
        m["oh2"] = np.array([[1.0 if b == 0 else 0.0, 1.0 if b == 1 else 0.0]], np.float32)
        m["ada_w"] = np.ascontiguousarray(inp["ada_w"][:, :, r * 768:(r + 1) * 768])
        m["adabT"] = np.ascontiguousarray(inp["ada_b"].reshape(2, 24, 128).transpose(0, 2, 1)[:, :, 6 * r:6 * r + 6])
        m["condT"] = np.ascontiguousarray(cond.reshape(8, 128, 2).transpose(1, 0, 2))
        s = np.zeros((1, 4), np.float32)
        s[0, r] = 1.0
        m["sel"] = s
        m["c_cosQ"] = np.ascontiguousarray(cst["_cosQ"][:, r * 1024:(r + 1) * 1024])
        m["c_sinQ"] = np.ascontiguousarray(cst["_sinQ"][:, r * 1024:(r + 1) * 1024])
        in_maps.append(m)
    if "nc" not in _NC_CACHE:
        _NC_CACHE["nc"] = build_program()
    res = run_bass_kernel_spmd(_NC_CACHE["nc"], in_maps, core_ids=list(range(8)))
    R = res.results
    cat = lambda k: np.concatenate([np.asarray(R[i][k], dtype=np.float32) for i in range(8)], axis=0)
    y_prompt = cat("yp")
    y_sample = cat("ys").reshape(2, 4096, 1024)
    nk_ = cat("nk").reshape(32, 1, 256, 2, 64)
    nv_ = cat("nv").reshape(32, 1, 256, 2, 64)
    nsf_ = cat("nsf").reshape(32, 1, 16, 64, 64)
    nsb_ = cat("nsb").reshape(32, 1, 16, 64, 64)
    nckv_ = cat("nckv").reshape(32, 1, 256, 128)
    nkpe_ = cat("nkpe").reshape(32, 1, 256, 32)
    return (y_prompt, y_sample, nk_, nv_, nsf_, nsb_, nckv_, nkpe_)
```
